# Optimizing a Trainium2 kernel written in Bass

```python
import math
import jax, jax.numpy as jnp
from jax import lax
import numpy as np

D_MODEL = 1024
BATCH = 8
SEQ = 2048
DEPTH = 2

PLE_DIM = 256
EPS = 1e-6
MIX_W = D_MODEL
GLA_W = MIX_W // 2
DIL_W = MIX_W - GLA_W
GLA_HEADS = 4
GLA_DV = GLA_W // GLA_HEADS
GLA_DK = GLA_DV // 2
GLA_QK = GLA_HEADS * GLA_DK
GLA_LOWRANK = 16
GLA_TAU = 16.0
GLA_CHUNK = 16
DIL_HD = 64
DIL_HEADS = DIL_W // DIL_HD
DIL_PATTERNS = ((128, 1), (512, 4), (2048, 16))
BAND = 128
BLK = 128
REL_BUCKETS = 32
REL_MAX_DIST = 2048
D_FF = 4 * D_MODEL
NEG = -1e30
IN_SIZES = (GLA_QK, GLA_QK, GLA_W, GLA_W, GLA_LOWRANK, DIL_W, DIL_W, DIL_W)
IN_W = sum(IN_SIZES)
IN_SPLITS = tuple(int(c) for c in np.cumsum(IN_SIZES)[:-1])

kernel_name = "hybrid_gla_dilated_sandwich_ple"


def rmsnorm(x, g):
    x32 = x.astype(jnp.float32)
    y = x32 * lax.rsqrt(jnp.mean(x32 * x32, axis=-1, keepdims=True) + EPS)
    return (y * g.astype(jnp.float32)).astype(x.dtype)


def t5_bucket(dist):
    max_exact = REL_BUCKETS // 2
    d = jnp.maximum(dist, 1).astype(jnp.float32)
    large = max_exact + (jnp.log(d / max_exact) / math.log(REL_MAX_DIST / max_exact)
                         * (REL_BUCKETS - max_exact)).astype(jnp.int32)
    large = jnp.minimum(large, REL_BUCKETS - 1)
    return jnp.where(dist < max_exact, dist, large)


def gla_branch(q, k, v, log_a):
    B, S, H, K = q.shape
    V = v.shape[-1]
    C = GLA_CHUNK
    N = S // C
    q = q * (K ** -0.5)
    q, k, log_a = (t.reshape(B, N, C, H, K) for t in (q, k, log_a))
    v = v.reshape(B, N, C, H, V)
    b = jnp.cumsum(log_a, axis=2)
    b_last = b[:, :, -1]
    causal = jnp.tril(jnp.ones((C, C), dtype=bool))
    diff = b[:, :, :, None] - b[:, :, None, :]
    decay = jnp.exp(jnp.where(causal[None, None, :, :, None, None], diff, NEG))
    att = jnp.einsum('bnihk,bnjhk,bnijhk->bnhij', q, k, decay)
    o_intra = jnp.einsum('bnhij,bnjhv->bnihv', att, v)
    q_g = q * jnp.exp(b)
    k_g = k * jnp.exp(b_last[:, :, None] - b)
    kv = jnp.einsum('bnjhk,bnjhv->nbhkv', k_g, v)
    a_last = jnp.exp(b_last).transpose(1, 0, 2, 3)

    def step(state, inp):
        dec, kv_n = inp
        return dec[..., None] * state + kv_n, state

    _, states = lax.scan(step, jnp.zeros((B, H, K, V), jnp.float32), (a_last, kv))
    o_inter = jnp.einsum('bnihk,nbhkv->bnihv', q_g, states)
    return (o_intra + o_inter).reshape(B, S, H, V)


def dilated_branch(q, k, v, rel_bias, dil):
    B, S, H, E = q.shape
    L = S // dil
    nb = -(-L // BLK)
    Lp = nb * BLK

    def sub(t):
        t = t.reshape(B, L, dil, H, E)
        return jnp.pad(t, ((0, 0), (0, Lp - L), (0, 0), (0, 0), (0, 0)))

    def band(t):
        tp = jnp.pad(t, ((0, 0), (BLK, 0), (0, 0), (0, 0), (0, 0)))
        prev = tp[:, :Lp].reshape(B, nb, BLK, dil, H, E)
        cur = tp[:, BLK:].reshape(B, nb, BLK, dil, H, E)
        return jnp.concatenate([prev, cur], axis=2)

    qs = sub(q).reshape(B, nb, BLK, dil, H, E)
    kb, vb = band(sub(k)), band(sub(v))
    logits = jnp.einsum('bnqrhe,bnkrhe->bnrhqk', qs, kb) * (E ** -0.5)
    qi = jnp.arange(BLK)[:, None]
    ki = jnp.arange(2 * BLK)[None, :]
    j = qi + BLK - ki
    bias = rel_bias[t5_bucket(jnp.maximum(j, 0) * dil)]
    bias = bias.transpose(2, 0, 1).astype(jnp.float32)
    key_pos = jnp.arange(nb)[:, None, None] * BLK + ki[None] - BLK
    valid = (j >= 0)[None] & (j <= BAND)[None] & (key_pos >= 0)
    logits = jnp.where(valid[None, :, None, None], logits + bias, NEG)
    m = jnp.max(logits, axis=-1, keepdims=True)
    e = jnp.exp(logits - m)
    s = jnp.sum(e, axis=-1)
    o = jnp.einsum('bnrhqk,bnkrhe->bnqrhe', e, vb)
    o = o / s.transpose(0, 1, 4, 2, 3)[..., None]
    lse = (m[..., 0] + jnp.log(s)).transpose(0, 1, 4, 2, 3)
    o = o.reshape(B, Lp, dil, H, E)[:, :L].reshape(B, S, H, E)
    lse = lse.reshape(B, Lp, dil, H)[:, :L].reshape(B, S, H)
    return o, lse


def setup_inputs(seed: int = 0) -> dict:
    key = jax.random.key(seed)
    ks = jax.random.split(key, 20)
    nrm = lambda k, shape, s: jax.random.normal(k, shape, jnp.float32) * s
    gain = lambda k, shape: 1.0 + 0.05 * jax.random.normal(k, shape, jnp.float32)
    return {
        "x": nrm(ks[0], (BATCH, SEQ, D_MODEL), 1.0),
        "p": nrm(ks[1], (DEPTH, BATCH, SEQ, PLE_DIM), 1.0),
        "w_in": nrm(ks[2], (DEPTH, D_MODEL, IN_W), D_MODEL ** -0.5),
        "w_gla_a2": nrm(ks[3], (DEPTH, GLA_LOWRANK, GLA_QK), GLA_LOWRANK ** -0.5),
        "b_gla_a": nrm(ks[4], (DEPTH, GLA_QK), 0.1),
        "gla_norm_g": gain(ks[5], (DEPTH, GLA_DV)),
        "w_out": nrm(ks[6], (DEPTH, MIX_W, D_MODEL), MIX_W ** -0.5),
        "rel_bias": nrm(ks[7], (REL_BUCKETS, DIL_HEADS), 0.1),
        "pre_mix_g": gain(ks[8], (DEPTH, D_MODEL)),
        "post_mix_g": gain(ks[9], (DEPTH, D_MODEL)),
        "pre_mlp_g": gain(ks[10], (DEPTH, D_MODEL)),
        "post_mlp_g": gain(ks[11], (DEPTH, D_MODEL)),
        "w_mlp_in": nrm(ks[12], (DEPTH, D_MODEL, D_FF), D_MODEL ** -0.5),
        "w_mlp_out": nrm(ks[13], (DEPTH, D_FF, D_MODEL), D_FF ** -0.5),
        "w_ple_gate": nrm(ks[14], (DEPTH, D_MODEL, D_MODEL), D_MODEL ** -0.5),
        "w_ple_proj": nrm(ks[15], (DEPTH, PLE_DIM, D_MODEL), PLE_DIM ** -0.5),
    }


def reference(x, p, w_in, w_gla_a2, b_gla_a, gla_norm_g, w_out, rel_bias,
              pre_mix_g, post_mix_g, pre_mlp_g, post_mlp_g,
              w_mlp_in, w_mlp_out, w_ple_gate, w_ple_proj):
    B, S, _ = x.shape
    f32 = jnp.float32
    h = x
    for i in range(DEPTH):
        xn = rmsnorm(h, pre_mix_g[i])
        proj = xn @ w_in[i]
        gq, gk, gv, gg, glr, dq, dk, dv = jnp.split(proj, IN_SPLITS, axis=-1)
        log_a = jax.nn.log_sigmoid((glr @ w_gla_a2[i] + b_gla_a[i]).astype(f32)) / GLA_TAU
        o_gla = gla_branch(gq.astype(f32).reshape(B, S, GLA_HEADS, GLA_DK),
                           gk.astype(f32).reshape(B, S, GLA_HEADS, GLA_DK),
                           gv.astype(f32).reshape(B, S, GLA_HEADS, GLA_DV),
                           log_a.reshape(B, S, GLA_HEADS, GLA_DK))
        o_gla = rmsnorm(o_gla, gla_norm_g[i]).reshape(B, S, GLA_W)
        o_gla = o_gla * jax.nn.silu(gg.astype(f32))
        qd = dq.astype(f32).reshape(B, S, DIL_HEADS, DIL_HD)
        kd = dk.astype(f32).reshape(B, S, DIL_HEADS, DIL_HD)
        vd = dv.astype(f32).reshape(B, S, DIL_HEADS, DIL_HD)
        outs, lses = [], []
        for _, dil in DIL_PATTERNS:
            o_g, lse_g = dilated_branch(qd, kd, vd, rel_bias, dil)
            outs.append(o_g)
            lses.append(lse_g)
        wts = jax.nn.softmax(jnp.stack(lses, axis=0), axis=0)
        o_dil = jnp.sum(wts[..., None] * jnp.stack(outs, axis=0), axis=0).reshape(B, S, DIL_W)
        mix = jnp.concatenate([o_gla, o_dil], axis=-1).astype(h.dtype) @ w_out[i]
        h = h + rmsnorm(mix, post_mix_g[i])
        xn = rmsnorm(h, pre_mlp_g[i])
        f = jnp.square(jax.nn.relu(xn @ w_mlp_in[i])) @ w_mlp_out[i]
        h = h + rmsnorm(f, post_mlp_g[i])
        h = h + jax.nn.sigmoid(h @ w_ple_gate[i]) * (p[i] @ w_ple_proj[i])
    return h
```

```python
import numpy as np
from contextlib import ExitStack
import concourse.bass as bass
import concourse.mybir as mybir
from concourse.bass_utils import run_bass_kernel_spmd

F32 = mybir.dt.float32
BF16 = mybir.dt.bfloat16
ALU = mybir.AluOpType
AF = mybir.ActivationFunctionType

SEQ = 2048
D = 1024
DEPTH = 2
IN_W = 3088
EPS = 1e-6
NDS = 12


class Sched:
    def __init__(self, nc):
        self.nc = nc
        self.E = {'pe': nc.tensor, 'act': nc.scalar, 'dve': nc.vector, 'pool': nc.gpsimd, 'sp': nc.sync}
        self.semh = {}
        for k in ['pe', 'act', 'dve', 'pool']:
            self.semh[k] = nc.alloc_semaphore("sem_" + k)
        for i in range(NDS):
            self.semh['dma%d' % i] = nc.alloc_semaphore("semdma%d" % i)
        self.cnt = {k: 0 for k in self.semh}
        self.known = {k: {} for k in self.E}
        self.lastw = {}
        self.readers = {}
        self.dnext = {'pool': 0, 'sp': 0}

    def _deps(self, r, w):
        evs = []
        for k in r:
            if k in self.lastw:
                evs.append(self.lastw[k])
        for k in w:
            if k in self.lastw:
                evs.append(self.lastw[k])
            rd = self.readers.get(k)
            if rd:
                evs.extend(rd.items())
        return evs

    def _wait(self, eng, evs):
        need = {}
        for (sn, v) in evs:
            if sn == 'pe' and eng == 'pe':
                continue
            if v > need.get(sn, 0):
                need[sn] = v
        kn = self.known[eng]
        for sn, v in need.items():
            if kn.get(sn, 0) >= v:
                continue
            self.E[eng].wait_ge(self.semh[sn], v)
            kn[sn] = v

    def _record(self, ev, r, w):
        for k in r:
            d = self.readers.setdefault(k, {})
            if ev[1] > d.get(ev[0], 0):
                d[ev[0]] = ev[1]
        for k in w:
            self.lastw[k] = ev
            self.readers[k] = {}

    def _war(self, keys):
        evs = []
        for k in keys:
            if k in self.lastw:
                evs.append(self.lastw[k])
            rd = self.readers.get(k)
            if rd:
                evs.extend(rd.items())
        return evs

    def op(self, eng, fn, r=(), w=(), inc=True, war=()):
        self._wait(eng, self._deps(r, w) + self._war(war))
        ins = fn()
        if inc:
            self.cnt[eng] += 1
            ins.then_inc(self.semh[eng], 1)
            ev = (eng, self.cnt[eng])
        else:
            ev = (eng, self.cnt[eng] + 1)
        self._record(ev, r, w)
        return ins

    def dma(self, q, out, in_, r=(), w=(), war=()):
        half = NDS // 2
        j = self.dnext[q]
        self.dnext[q] = (j + 1) % half
        i = j + (half if q == 'pool' else 0)
        sn = 'dma%d' % i
        evs = self._deps(r, w) + self._war(war)
        if self.cnt[sn] > 0:
            evs.append((sn, self.cnt[sn]))
        self._wait(q, evs)
        self.cnt[sn] += 16
        self.E[q].dma_start(out=out, in_=in_).then_inc(self.semh[sn], 16)
        ev = (sn, self.cnt[sn])
        self._record(ev, r, w)
        return ev

    def barrier(self, exclude=(), engines=None):
        ex = set(exclude)
        evs = [(k, v) for k, v in self.cnt.items() if v > 0 and (k, v) not in ex]
        for e in (engines if engines is not None else self.E):
            self._wait(e, evs)

    def fence(self, keys, exclude=()):
        ex = set(exclude)
        snap = {k: v for k, v in self.cnt.items() if v > 0 and (k, v) not in ex}
        for k in keys:
            d = self.readers.setdefault(k, {})
            for sn, v in snap.items():
                if v > d.get(sn, 0):
                    d[sn] = v


def _t5_bucket_np(dist):
    dist = np.asarray(dist, np.int64)
    d = np.maximum(dist, 1).astype(np.float32)
    large = 16 + (np.log(d / np.float32(16)) / np.float32(np.log(128.0)) * np.float32(16)).astype(np.int32)
    large = np.minimum(large, 31)
    return np.where(dist < 16, dist, large).astype(np.int64)


def _bias_tables():
    k = np.arange(128)[:, None]
    q = np.arange(128)[None, :]
    kinds = []
    for (dil, prev) in [(1, True), (1, False), (4, True), (4, False), (16, False)]:
        if prev:
            j = q + 128 - k
            valid = (k >= q)
        else:
            j = q - k
            valid = (k <= q)
        idx = _t5_bucket_np(np.maximum(j, 0) * dil)
        kinds.append((idx, valid.astype(np.float32)))
    return kinds


def build(n_layers=DEPTH):
    nc = bass.Bass("TRN2", target_bir_lowering=False)
    S = Sched(nc)

    def din(name, shape):
        return nc.dram_tensor(name, shape, F32, kind="ExternalInput").ap()

    x_d = din("x", [SEQ, D])
    p_d = din("p", [DEPTH, SEQ, 256])
    w_in_d = din("w_in", [DEPTH, D, IN_W])
    w_a2_d = din("w_gla_a2", [DEPTH, 16, 256])
    b_a_d = din("b_gla_a", [DEPTH, 256])
    gng_d = din("gla_norm_g", [DEPTH, 128])
    w_out_d = din("w_out", [DEPTH, D, D])
    pre_mix_d = din("pre_mix_g", [DEPTH, D])
    post_mix_d = din("post_mix_g", [DEPTH, D])
    pre_mlp_d = din("pre_mlp_g", [DEPTH, D])
    post_mlp_d = din("post_mlp_g", [DEPTH, D])
    w1_d = din("w_mlp_in", [DEPTH, D, 4 * D])
    w2_d = din("w_mlp_out", [DEPTH, 4 * D, D])
    wg_d = din("w_ple_gate", [DEPTH, D, D])
    wp_d = din("w_ple_proj", [DEPTH, 256, D])
    bias_d = din("bias_tab", [128, 5 * 8 * 128])
    mask_d = din("mask_tab", [128, 5 * 128])
    ident_d = din("ident", [128, 128])
    caus_d = din("caus", [128, 128])
    out_d = nc.dram_tensor("out", [SEQ, D], F32, kind="ExternalOutput").ap()

    sb = nc.alloc_sbuf_tensor
    h = sb("h", [128, 16, D], F32)
    ident = sb("ident_sb", [128, 128], BF16)
    caus = sb("caus_sb", [128, 128], F32)
    onesf = sb("onesf", [128, 128], F32)
    gcols = sb("gcols", [128, DEPTH, 2, 8], F32)
    negb = sb("negb", [128, DEPTH, 2], F32)
    gng = sb("gng", [128, DEPTH], F32)
    ss = sb("ss", [128, 16], F32)
    sq = sb("sq", [128, 16], F32)
    rstd = sb("rstd", [128, 16], F32)
    ssn = sb("ssn", [128, 16], F32)
    sqn = sb("sqn", [128, 16], F32)
    rstdn = sb("rstdn", [128, 16], F32)
    junk = sb("junk", [128, D], BF16)
    xsb = sb("xsb", [128, 2, D], BF16)
    xs = [xsb[:, j, :] for j in range(2)]
    tmp4 = sb("tmp4", [128, D], F32)
    gbc = sb("gbc", [128, 1, D], F32)
    wbA = sb("wbA", [128, 8, 512], BF16)
    wbB = sb("wbB", [128, 8, 512], BF16)
    KEYS = {"wbA": ["wbA"], "wbB": ["wbB", "wbB0", "wbB1"]}

    def WK(k):
        return KEYS[k]

    PP = [nc.alloc_psum_tensor("pp%d" % i, [128, 1024], F32) for i in range(4)]

    def bank(i):
        return PP[i // 2][:, (i % 2) * 512:(i % 2) * 512 + 512]

    def bkey(i):
        return ("bank", i)

    with nc.allow_non_contiguous_dma(reason="small constant loads"):
        S.dma('pool', ident[:, :], ident_d, w=["ident"])
        S.dma('sp', caus[:, :], caus_d, w=["caus"])
        for l in range(n_layers):
            S.dma('sp', gcols[:, l, 0, :], pre_mix_d[l].rearrange("(dc p) -> p dc", p=128), w=["gcols"])
            S.dma('sp', gcols[:, l, 1, :], pre_mlp_d[l].rearrange("(dc p) -> p dc", p=128), w=["gcols"])
            S.dma('sp', negb[:, l, :], b_a_d[l].rearrange("(c p) -> p c", p=128), w=["negb"])
            S.dma('sp', gng[:, l:l + 1], gng_d[l].rearrange("(p o) -> p o", o=1), w=["gng"])
    S.op('dve', lambda: nc.vector.memset(onesf[:, :], 1.0), w=["onesf"])
    S.op('dve', lambda: nc.vector.tensor_scalar(out=negb[:, :, :], in0=negb[:, :, :], scalar1=-1.0, scalar2=None,
                                                op0=ALU.mult), r=["negb"], w=["negb"])

    xv = x_d.rearrange("(t p) d -> p t d", p=128)
    for g in range(8):
        S.dma('sp', h[:, 2 * g:2 * g + 2, :], xv[:, 2 * g:2 * g + 2, :], w=[("h", t) for t in range(2 * g, 2 * g + 2)])

    def norm_T(dstT, dkey, t0, nt, gcol, do_norm, war=(), dbase=None):
        if dbase is None:
            dbase = t0
        if do_norm:
            for t in range(t0, t0 + nt):
                S.op('act', lambda: nc.scalar.activation(out=junk[:, :], in_=h[:, t, :], func=AF.Square,
                                                         accum_out=ssn[:, t:t + 1]), r=[("h", t)], w=[("ssn", t), "junk"])
            S.op('act', lambda: nc.scalar.activation(out=sqn[:, t0:t0 + nt], in_=ssn[:, t0:t0 + nt], func=AF.Sqrt,
                                                     scale=1.0 / D, bias=epsc[:, 0:1]),
                 r=[("ssn", t) for t in range(t0, t0 + nt)] + ["epsc"], w=[("sq", t0)])
            S.op('dve', lambda: nc.vector.reciprocal(out=rstdn[:, t0:t0 + nt], in_=sqn[:, t0:t0 + nt]), r=[("sq", t0)], w=[("rstdg", t0)])
        for t in range(t0, t0 + nt):
            j = t % 2
            if do_norm:
                S.op('act', lambda: nc.scalar.mul(out=xs[j], in_=h[:, t, :], mul=rstdn[:, t:t + 1]),
                     r=[("h", t), ("rstdg", t0)], w=[("xs", j)])
            else:
                S.op('act', lambda: nc.scalar.copy(out=xs[j], in_=h[:, t, :]), r=[("h", t)], w=[("xs", j)])
            pt = bank(6 + j).bitcast(BF16).rearrange("p (a b) -> p a b", b=128)
            for dc in range(8):
                S.op('pe', lambda: nc.tensor.transpose(out=pt[:, dc, :], in_=xs[j][:, dc * 128:(dc + 1) * 128],
                                                       identity=ident[:, :]),
                     r=[("xs", j), "ident"], w=[bkey(6 + j)], inc=(dc == 7))
            dst = dstT[:, :, (t - dbase) * 128:(t - dbase + 1) * 128]
            if gcol is not None:
                S.op('dve', lambda: nc.vector.tensor_tensor(out=dst, in0=pt, in1=gcol.unsqueeze(2).to_broadcast([128, 8, 128]),
                                                            op=ALU.mult), r=[bkey(6 + j), "gcols"], w=[(dkey, t - dbase)])
            else:
                S.op('dve', lambda: nc.vector.tensor_copy(out=dst, in_=pt), r=[bkey(6 + j)], w=[(dkey, t - dbase)], war=war)

    epsc = sb("epsc", [128, 2], F32)
    S.op('dve', lambda: nc.vector.memset(epsc[:, 0:1], EPS), w=["epsc"])
    S.op('dve', lambda: nc.vector.memset(epsc[:, 1:2], 1.0), w=["epsc"])

    def load_w(wb, wkey, src_ap, ncols, c0=0):
        kc = src_ap.shape[0] // 128
        return S.dma('pool', wb[:, 0:kc, c0:c0 + ncols], src_ap.rearrange("(kc p) c -> p kc c", p=128),
                     w=(WK(wkey) if wkey in KEYS else [wkey]))

    SPLIT = [False]

    def mm_acc(ps, pskey, lhs_list, rhs_list, rkeys):
        n = len(lhs_list)
        wide = SPLIT[0] and len(rhs_list[0].shape) == 2 and rhs_list[0].shape[-1] == 512
        for i in range(n):
            if wide:
                for ch in range(4):
                    cs = slice(ch * 128, (ch + 1) * 128)
                    last = (i == n - 1 and ch == 3)
                    S.op('pe', lambda: nc.tensor.matmul(ps[:, cs], lhsT=lhs_list[i], rhs=rhs_list[i][:, cs],
                                                        start=(i == 0 and ch == 0), stop=last, skip_group_check=True),
                         r=rkeys, w=[pskey], inc=last)
            else:
                S.op('pe', lambda: nc.tensor.matmul(ps, lhsT=lhs_list[i], rhs=rhs_list[i], start=(i == 0), stop=(i == n - 1)),
                     r=rkeys, w=[pskey], inc=(i == n - 1))

    def resid_epilogue(src_ap, srckeys, t, gkey, src_is_psum, add_eng='pool'):
        S.op('act', lambda: nc.scalar.activation(out=junk[:, :], in_=src_ap, func=AF.Square, accum_out=ss[:, t:t + 1]),
             r=srckeys, w=[("ss", t), "junk"])
        S.op('act', lambda: nc.scalar.activation(out=sq[:, t:t + 1], in_=ss[:, t:t + 1], func=AF.Sqrt, scale=1.0 / D,
                                                 bias=epsc[:, 0:1]), r=[("ss", t), "epsc"], w=[("sq", t)])
        S.op('dve', lambda: nc.vector.reciprocal(out=rstd[:, t:t + 1], in_=sq[:, t:t + 1]), r=[("sq", t)], w=[("rstd", t)])
        S.op('dve', lambda: nc.vector.scalar_tensor_tensor(out=tmp4[:, :], in0=src_ap, scalar=rstd[:, t:t + 1],
                                                           in1=gbc[:, 0, :], op0=ALU.mult, op1=ALU.mult),
             r=srckeys + [("rstd", t), gkey], w=["tmp4", ("sqr", 0), ("sqr", 1)])
        S.op(add_eng, lambda: (nc.gpsimd if add_eng == 'pool' else nc.vector).tensor_tensor(out=h[:, t, :], in0=h[:, t, :], in1=tmp4[:, :], op=ALU.add),
             r=["tmp4", ("sqr", 0), ("sqr", 1), ("h", t)], w=[("h", t)])

    def gla_prefetch(l):
        evs = [load_w(wbB, "wbB1", w_in_d[l][:, 1536:1664], 128, c0=256),
               load_w(wbB, "wbB0", w_in_d[l][:, 0:128], 128, c0=0),
               load_w(wbB, "wbB0", w_in_d[l][:, 256:384], 128, c0=128),
               load_w(wbA, "wbA", w_in_d[l][:, 512:1024], 512)]
        return evs

    pre_evs = gla_prefetch(0)
    for l in range(n_layers):
        w_in_l = w_in_d[l]
        with ExitStack() as es1:
            xnT = es1.enter_context(nc.sbuf_tensor("xnT_%d" % l, [128, 8, SEQ], BF16))
            mixT = es1.enter_context(nc.sbuf_tensor("mixT_%d" % l, [128, 8, SEQ], BF16))
            for g_ in range(4):
                norm_T(xnT, "xnT", 4 * g_, 4, gcols[:, l, 0, :], True, dbase=0)
            xkeys = [("xnT", t) for t in range(16)]

            def xk(tok0, ntok):
                return [("xnT", t) for t in range(tok0 // 128, (tok0 + ntok + 127) // 128)]

            with ExitStack() as es2:
                def A2(name, shape, dt):
                    return es2.enter_context(nc.sbuf_tensor("%s_%d" % (name, l), shape, dt))
                glrT = A2("glrT", [128, SEQ], BF16)
                wa2 = A2("wa2", [128, 256], BF16)
                qgz = A2("qgz", [128, 2, SEQ], BF16)
                kdT = A2("kdT", [128, SEQ], BF16)
                kg = A2("kg", [128, 16, 128], BF16)
                gv = A2("gv", [128, 16, 256], BF16)
                alast = A2("alast", [128, 16], F32)
                scr = A2("scr", [128, 2, 3, 512], F32)
                kgT = A2("kgT", [128, 2, 512], BF16)
                Sf = A2("Sf", [128, 256], F32)
                Sb = A2("Sb", [128, 3, 256], BF16)
                attm = A2("attm", [128, 2, 2, 128], BF16)

                def drive(gens, width):
                    gens = list(gens)
                    active = []
                    while gens or active:
                        while len(active) < width and gens:
                            active.append(gens.pop(0))
                        for g_ in list(active):
                            try:
                                next(g_)
                            except StopIteration:
                                active.remove(g_)

                S.op('dve', lambda: nc.vector.memset(wa2[:, :], 0.0), w=["wa2"])
                S.op('dve', lambda: nc.vector.memset(qgz[:, :, :], 0.0), w=["qgz"])
                S.dma('pool', wa2[0:16, :], w_a2_d[l], r=[], w=["wa2"])
                for tg in range(4):
                    b = tg % 2
                    mm_acc(bank(b), bkey(b), [wbB[:, dc, 256:384] for dc in range(8)],
                           [xnT[:, dc, tg * 512:(tg + 1) * 512] for dc in range(8)], ["wbB1"] + xk(tg * 512, 512))
                    S.op('act', lambda: nc.scalar.copy(out=glrT[:, tg * 512:(tg + 1) * 512], in_=bank(b)),
                         r=[bkey(b)], w=[("glrT", tg)])

                def prep_gen(pr, tg):
                    s_ = tg % 2
                    t1, cc, Eq = scr[:, s_, 0, :], scr[:, s_, 1, :], scr[:, s_, 2, :]
                    cc3 = cc.rearrange("p (a b) -> p a b", b=128)
                    k1, kc, kq, kk = ("t1", s_), ("cc", s_), ("Eq", s_), ("kgT", s_)
                    tsl = slice(tg * 512, (tg + 1) * 512)
                    bx, bq, bk_ = s_, 2 + s_, 4 + s_
                    S.op('pe', lambda: nc.tensor.matmul(bank(bx), lhsT=wa2[:, pr * 128:(pr + 1) * 128], rhs=glrT[:, tsl],
                                                        start=True, stop=True), r=["wa2", ("glrT", tg)], w=[bkey(bx)])
                    yield
                    mm_acc(bank(bq), bkey(bq), [wbB[:, dc, 0:128] for dc in range(8)],
                           [xnT[:, dc, tsl] for dc in range(8)], ["wbB0"] + xk(tg * 512, 512))
                    yield
                    mm_acc(bank(bk_), bkey(bk_), [wbB[:, dc, 128:256] for dc in range(8)],
                           [xnT[:, dc, tsl] for dc in range(8)], ["wbB0"] + xk(tg * 512, 512))
                    yield
                    S.op('act', lambda: nc.scalar.activation(out=t1, in_=bank(bx), func=AF.Exp, scale=-1.0,
                                                             bias=negb[:, l, pr:pr + 1]), r=[bkey(bx), "negb"], w=[k1])
                    yield
                    S.op('act', lambda: nc.scalar.activation(out=t1, in_=t1, func=AF.Ln, scale=1.0, bias=epsc[:, 1:2]),
                         r=[k1, "epsc"], w=[k1])
                    yield
                    for ch in range(4):
                        S.op('dve', lambda: nc.vector.tensor_tensor_scan(
                            out=cc3[:, ch, :], data0=onesf[:, :], data1=t1[:, ch * 128:(ch + 1) * 128], initial=0.0,
                            op0=ALU.mult, op1=ALU.add), r=[k1, "onesf"], w=[kc])
                    yield
                    S.op('act', lambda: nc.scalar.activation(out=Eq, in_=cc, func=AF.Exp, scale=-1.0 / 16.0), r=[kc], w=[kq])
                    yield
                    S.op('dve', lambda: nc.vector.tensor_tensor(
                        out=t1.rearrange("p (a b) -> p a b", b=128), in0=cc3, in1=cc3[:, :, 127:128].to_broadcast([128, 4, 128]),
                        op=ALU.subtract), r=[kc], w=[k1])
                    yield
                    S.op('act', lambda: nc.scalar.activation(out=t1, in_=t1, func=AF.Exp, scale=1.0 / 16.0), r=[k1], w=[k1])
                    yield
                    S.op('act', lambda: nc.scalar.activation(out=cc, in_=cc, func=AF.Exp, scale=1.0 / 16.0), r=[kc], w=[kc])
                    yield
                    S.op('act', lambda: nc.scalar.copy(out=alast[:, tg * 4:(tg + 1) * 4],
                                                       in_=Eq.rearrange("p (a b) -> p a b", b=128)[:, :, 127]), r=[kq], w=[("alast", tg)])
                    yield
                    for hh in range(2):
                        ps_ = slice(hh * 64, hh * 64 + 64)
                        S.op('dve', lambda: nc.vector.scalar_tensor_tensor(
                            out=qgz[ps_, hh, tsl], in0=bank(bq)[ps_, :], scalar=0.125, in1=Eq[ps_, :],
                            op0=ALU.mult, op1=ALU.mult), r=[bkey(bq), kq], w=[("qgz", tg)])
                    yield
                    S.op('dve', lambda: nc.vector.tensor_tensor(out=kdT[:, tsl], in0=bank(bk_), in1=cc, op=ALU.mult),
                         r=[bkey(bk_), kc], w=[("kdT", tg)])
                    yield
                    S.op('dve', lambda: nc.vector.tensor_tensor(out=kgT[:, s_, :], in0=bank(bk_), in1=t1, op=ALU.mult),
                         r=[bkey(bk_), k1], w=[kk])
                    yield
                    pt = bank(6).bitcast(BF16).rearrange("p (a b) -> p a b", b=128)
                    for ch in range(4):
                        S.op('pe', lambda: nc.tensor.transpose(out=pt[:, ch, :], in_=kgT[:, s_, ch * 128:(ch + 1) * 128],
                                                               identity=ident[:, :]), r=[kk, "ident"], w=[bkey(6)], inc=(ch == 3))
                    S.op('act', lambda: nc.scalar.copy(out=kg[:, tg * 4:(tg + 1) * 4, :], in_=pt[:, 0:4, :]),
                         r=[bkey(6)], w=[("kg", tg)])
                    yield

                def gv_gen(pr):
                    for t in range(16):
                        mm_acc(bank(7)[:, 0:256], bkey(7), [xnT[:, dc, t * 128:(t + 1) * 128] for dc in range(8)],
                               [wbA[:, dc, pr * 256:(pr + 1) * 256] for dc in range(8)], WK("wbA") + [("xnT", t)])
                        yield
                        S.op('act', lambda: nc.scalar.copy(out=gv[:, t, :], in_=bank(7)[:, 0:256]), r=[bkey(7)], w=[("gv", t)])
                        yield
                        yield

                def epi_ops(pr, tg, hh):
                    hd = 2 * pr + hh
                    osb, osq, sgt = scr[:, hh, 0, :], scr[:, hh, 1, :], scr[:, hh, 2, :]
                    ko, kq2, kg2 = ("t1", hh), ("cc", hh), ("Eq", hh)
                    tsl = slice(tg * 512, (tg + 1) * 512)
                    ops = []
                    ops.append(lambda: S.op('act', lambda: nc.scalar.copy(out=osb, in_=bank(4 + hh)), r=[bkey(4 + hh)], w=[ko]))
                    ops.append(lambda: S.op('act', lambda: nc.scalar.activation(out=osq, in_=osb, func=AF.Square), r=[ko], w=[kq2]))
                    ops.append(lambda: S.op('pe', lambda: nc.tensor.matmul(bank(hh), lhsT=onesf[:, :], rhs=osq, start=True, stop=True),
                                            r=["onesf", kq2], w=[bkey(hh)]))
                    ops.append(lambda: S.op('act', lambda: nc.scalar.activation(out=osq, in_=bank(hh), func=AF.Ln, scale=1.0 / 128.0,
                                                                             bias=epsc[:, 0:1]), r=[bkey(hh), "epsc"], w=[kq2]))
                    ops.append(lambda: S.op('act', lambda: nc.scalar.activation(out=osq, in_=osq, func=AF.Exp, scale=-0.5), r=[kq2], w=[kq2]))
                    ops.append(lambda: S.op('dve', lambda: nc.vector.tensor_tensor(out=osb, in0=osb, in1=osq, op=ALU.mult),
                                            r=[ko, kq2], w=[ko]))
                    ops.append(lambda: mm_acc(bank(hh), bkey(hh), [wbB[:, dc, (2 + hh) * 128:(3 + hh) * 128] for dc in range(8)],
                                              [xnT[:, dc, tsl] for dc in range(8)], ["wbB1"] + xk(tg * 512, 512)))
                    ops.append(lambda: S.op('act', lambda: nc.scalar.activation(out=sgt, in_=bank(hh), func=AF.Silu), r=[bkey(hh)], w=[kg2]))
                    ops.append(lambda: S.op('dve', lambda: nc.vector.scalar_tensor_tensor(
                        out=mixT[:, hd, tsl], in0=osb, scalar=gng[:, l:l + 1], in1=sgt, op0=ALU.mult, op1=ALU.mult),
                        r=[ko, kg2, "gng"], w=[("mixT", hd, tg)]))
                    return ops

                for pr in range(2):
                    for hh_ in range(2):
                        load_w(wbB, "wbB1", w_in_l[:, 1024 + (2 * pr + hh_) * 128:1024 + (2 * pr + hh_ + 1) * 128], 128, c0=256 + hh_ * 128)
                    drive([prep_gen(pr, 0), prep_gen(pr, 1), gv_gen(pr), prep_gen(pr, 2), prep_gen(pr, 3)], 3)
                    if pr == 0:
                        load_w(wbB, "wbB0", w_in_l[:, 128:256], 128, c0=0)
                        load_w(wbB, "wbB0", w_in_l[:, 384:512], 128, c0=128)

                    S.op('dve', lambda: nc.vector.memset(Sf[:, :], 0.0), w=["Sf"])
                    S.op('dve', lambda: nc.vector.memset(Sb[:, 0, :], 0.0), w=[("Sb", 0)])
                    pend = []

                    def drip(k=1):
                        for _ in range(k):
                            if pend:
                                pend.pop(0)()

                    def stage_A(n):
                        tg = n // 4
                        csl = slice(n * 128, (n + 1) * 128)
                        a2 = n % 2
                        ab = 2 + a2
                        kb = 6 + a2
                        for hh in range(2):
                            S.op('pe', lambda: nc.tensor.matmul(bank(ab)[:, hh * 128:(hh + 1) * 128], lhsT=kdT[:, csl],
                                                                rhs=qgz[:, hh, csl], start=True, stop=True, skip_group_check=True),
                                 r=[("kdT", tg), ("qgz", tg)], w=[bkey(ab)], inc=(hh == 1))
                        S.op('pe', lambda: nc.tensor.matmul(bank(kb)[:, 0:256], lhsT=kg[:, n, :], rhs=gv[:, n, :], start=True, stop=True),
                             r=[("kg", tg), ("gv", n)], w=[bkey(kb)])
                        S.op('dve', lambda: nc.vector.tensor_tensor(
                            out=attm[:, a2, :, :], in0=bank(ab)[:, 0:256].rearrange("p (a b) -> p a b", b=128),
                            in1=caus[:, :].unsqueeze(1).to_broadcast([128, 2, 128]), op=ALU.mult),
                            r=[bkey(ab), "caus"], w=[("attm", a2)])
                        S.op('dve', lambda: nc.vector.scalar_tensor_tensor(
                            out=Sf[:, :], in0=Sf[:, :], scalar=alast[:, n:n + 1], in1=bank(kb)[:, 0:256],
                            op0=ALU.mult, op1=ALU.add), r=["Sf", bkey(kb), ("alast", tg)], w=["Sf"])
                        S.op('act', lambda: nc.scalar.copy(out=Sb[:, (n + 1) % 3, :], in_=Sf[:, :]), r=["Sf"], w=[("Sb", (n + 1) % 3)])

                    def stage_B(n):
                        tg = n // 4
                        csl = slice(n * 128, (n + 1) * 128)
                        a2 = n % 2
                        for hh in range(2):
                            ob = bank(4 + hh)[:, (n % 4) * 128:(n % 4 + 1) * 128]
                            S.op('pe', lambda: nc.tensor.matmul(ob, lhsT=gv[:, n, hh * 128:(hh + 1) * 128], rhs=attm[:, a2, hh, :],
                                                                start=True, stop=False, skip_group_check=True),
                                 r=[("gv", n), ("attm", a2)], w=[bkey(4 + hh)], inc=False)
                            S.op('pe', lambda: nc.tensor.matmul(ob, lhsT=Sb[:, n % 3, hh * 128:(hh + 1) * 128], rhs=qgz[:, hh, csl],
                                                                start=False, stop=True, skip_group_check=True),
                                 r=[("Sb", n % 3), ("qgz", tg)], w=[bkey(4 + hh)])

                    stage_A(0)
                    for n in range(16):
                        if n + 1 < 16:
                            stage_A(n + 1)
                        drip(2)
                        stage_B(n)
                        drip(2)
                        if n % 4 == 3:
                            drip(len(pend))
                            e0, e1 = epi_ops(pr, n // 4, 0), epi_ops(pr, n // 4, 1)
                            e0[0]()
                            e1[0]()
                            for x0, x1 in zip(e0[1:], e1[1:]):
                                pend.append(x0)
                                pend.append(x1)
                    drip(len(pend))
                S.barrier(engines=('act', 'dve', 'sp'))

            with ExitStack() as es3:
                qz = es3.enter_context(nc.sbuf_tensor("qz_%d" % l, [128, 2, SEQ], BF16))
                kT = es3.enter_context(nc.sbuf_tensor("kT_%d" % l, [128, SEQ], BF16))
                Vn = es3.enter_context(nc.sbuf_tensor("Vn_%d" % l, [128, 3, 16, 192], BF16))
                ebf = es3.enter_context(nc.sbuf_tensor("ebf_%d" % l, [128, 2, 512], BF16))
                acc = es3.enter_context(nc.sbuf_tensor("acc_%d" % l, [128, 2, 512], F32))
                rs = es3.enter_context(nc.sbuf_tensor("rs_%d" % l, [128, 512], F32))
                biasM = es3.enter_context(nc.sbuf_tensor("biasM_%d" % l, [128, 5, 2, 128], BF16))
                madd = es3.enter_context(nc.sbuf_tensor("madd_%d" % l, [128, 5, 128], F32))
                et = es3.enter_context(nc.sbuf_tensor("et_%d" % l, [128, 2, 128], F32))
                S.dma('sp', madd[:, :, :], mask_d.rearrange("p (k q) -> p k q", q=128), w=["madd"])
                S.op('dve', lambda: nc.vector.tensor_scalar(out=madd[:, :, :], in0=madd[:, :, :], scalar1=-1.0, scalar2=30000.0,
                                                            op0=ALU.add, op1=ALU.mult), r=["madd"], w=["madd"])
                S.op('dve', lambda: nc.vector.memset(qz[:, :, :], 0.0), w=["qz"])
                S.op('dve', lambda: nc.vector.memset(Vn[:, :, :, 64:128], 1.0), w=["Vn"])
                for pr in range(4):
                    wb = wbA if pr % 2 == 0 else wbB
                    wk = "wbA" if pr % 2 == 0 else "wbB"
                    bv_ = bias_d.rearrange("p (k h q) -> p k h q", k=5, h=8)
                    for kd in range(5):
                        S.dma('sp', et[:, :, :], bv_[:, kd, 2 * pr:2 * pr + 2, :], w=["et"])
                        S.op('dve', lambda: nc.vector.tensor_tensor(
                            out=biasM[:, kd, :, :], in0=et[:, :, :], in1=madd[:, kd, :].unsqueeze(1).to_broadcast([128, 2, 128]),
                            op=ALU.add), r=["et", "madd"], w=["biasM"])
                    for i, c0 in enumerate([1552, 2064, 2576]):
                        S.dma('pool', wb[:, :, i * 128:(i + 1) * 128],
                              w_in_l[:, c0 + pr * 128:c0 + (pr + 1) * 128].rearrange("(kc p) c -> p kc c", p=128), w=WK(wk))
                    for tg in range(4):
                        tsl = slice(tg * 512, (tg + 1) * 512)
                        mm_acc(bank(5), bkey(5), [wb[:, dc, 0:128] for dc in range(8)], [xnT[:, dc, tsl] for dc in range(8)],
                               WK(wk) + xk(tg * 512, 512))
                        for hh in range(2):
                            ps_ = slice(hh * 64, hh * 64 + 64)
                            S.op('act', lambda: nc.scalar.mul(out=qz[ps_, hh, tsl], in_=bank(5)[ps_, :], mul=0.125),
                                 r=[bkey(5)], w=["qz"])
                        mm_acc(bank(6), bkey(6), [wb[:, dc, 128:256] for dc in range(8)], [xnT[:, dc, tsl] for dc in range(8)],
                               WK(wk) + xk(tg * 512, 512))
                        S.op('act', lambda: nc.scalar.copy(out=kT[:, tsl], in_=bank(6)), r=[bkey(6)], w=["kT"])
                    vT = xsb[:, :, :].rearrange("p a b -> p (a b)")
                    for tg in range(4):
                        tsl = slice(tg * 512, (tg + 1) * 512)
                        b = 5 + (tg % 2)
                        mm_acc(bank(b), bkey(b), [wb[:, dc, 256:384] for dc in range(8)], [xnT[:, dc, tsl] for dc in range(8)],
                               WK(wk) + xk(tg * 512, 512))
                        S.op('act', lambda: nc.scalar.copy(out=vT[:, tsl], in_=bank(b)), r=[bkey(b)], w=[("xs", tg // 2)])
                    for lay in range(3):
                        for g8 in range(2):
                            b = 5 + ((lay * 2 + g8) % 2)
                            ptb = bank(b).bitcast(BF16).rearrange("p (a b) -> p a b", b=128)
                            for tt in range(8):
                                ti = g8 * 8 + tt
                                if lay == 0:
                                    tok = slice(ti * 128, (ti + 1) * 128)
                                elif lay == 1:
                                    c_, r_ = ti // 4, ti % 4
                                    tok = slice(512 * c_ + r_, 512 * c_ + 512, 4)
                                else:
                                    tok = slice(ti, SEQ, 16)
                                S.op('pe', lambda: nc.tensor.transpose(out=ptb[:, tt, :], in_=vT[:, tok], identity=ident[:, :]),
                                     r=[("xs", 0), ("xs", 1), "ident"], w=[bkey(b)], inc=(tt == 7))
                            evac_eng = 'act' if (lay * 2 + g8) % 2 == 0 else 'dve'
                            dstv = Vn[:, lay, g8 * 8:(g8 + 1) * 8, :].rearrange("p t (h e) -> p t h e", e=64)[:, :, 0:3:2, :]
                            srcv = ptb.rearrange("p t (h e) -> p t h e", e=64)
                            if evac_eng == 'act':
                                S.op('act', lambda: nc.scalar.copy(out=dstv, in_=srcv), r=[bkey(b)], w=["Vn"])
                            else:
                                S.op('dve', lambda: nc.vector.tensor_copy(out=dstv, in_=srcv), r=[bkey(b)], w=["Vn"])
                    if pr == 3:
                        load_w(wbA, "wbA", w_out_d[l][:, 0:512], 512)
                        load_w(wbB, "wbB", w_out_d[l][:, 512:1024], 512)
                    stages = []
                    for hh in range(2):
                        for c in range(4):
                            g = (hh * 4 + c)
                            ub0 = 2 if g % 2 == 0 else 5
                            cur1, prev1, cur4, prev4, d16 = [], [], [], [], []
                            for n2 in range(4):
                                nb = 4 * c + n2
                                qs = slice(nb * 128, (nb + 1) * 128)
                                cur1.append((qs, qs, (0, nb), n2 * 128, 128))
                                if nb > 0:
                                    prev1.append((slice((nb - 1) * 128, nb * 128), qs, (0, nb - 1), n2 * 128, 128))
                            for r_ in range(4):
                                qs = slice(512 * c + r_, 512 * c + 512, 4)
                                cur4.append((qs, qs, (1, 4 * c + r_), r_ * 128, 128))
                                if c > 0:
                                    prev4.append((slice(512 * (c - 1) + r_, 512 * c, 4), qs, (1, 4 * (c - 1) + r_), r_ * 128, 128))
                            for r16 in range(16):
                                d16.append((slice(r16, SEQ, 16), slice(512 * c + r16, 512 * c + 512, 16), (2, r16), r16 * 32, 32))
                            grp = [(1, ub0, cur1), (0, ub0, prev1), (3, ub0 + 1, cur4), (2, ub0 + 1, prev4), (4, ub0 + 2, d16)]
                            grp = [x for x in grp if x[2]]
                            started = set()
                            for gi, (kind, ub, items) in enumerate(grp):
                                first = ub not in started
                                started.add(ub)
                                stages.append(dict(hh=hh, c=c, kind=kind, ub=ub, items=items, first=first,
                                                   last=(gi == len(grp) - 1), ub0=ub0))

                    def emit_L(i, st):
                        lb = i % 2
                        hh, c, kind, items = st['hh'], st['c'], st['kind'], st['items']
                        lo = items[0][3]
                        if kind == 4:
                            bv = biasM[:, 4, hh, 32 * c:32 * c + 32].unsqueeze(1).to_broadcast([128, 16, 32])
                        else:
                            bv = biasM[:, kind, hh, :].unsqueeze(1).to_broadcast([128, (512 - lo) // 128, 128])
                        S.op('pe', lambda: nc.tensor.matmul(bank(lb)[:, lo:512], lhsT=ident[:, :], rhs=bv, start=True, stop=False,
                                                            skip_group_check=True), r=["ident", "biasM"], w=[bkey(lb)], inc=False)
                        for (ks, qs, vt, c0, ncol) in items:
                            S.op('pe', lambda: nc.tensor.matmul(bank(lb)[:, c0:c0 + ncol], lhsT=kT[:, ks], rhs=qz[:, hh, qs],
                                                                start=False, stop=(c0 + ncol == 512), skip_group_check=True),
                                 r=["kT", "qz"], w=[bkey(lb)], inc=(c0 + ncol == 512))
                        S.op('act', lambda: nc.scalar.activation(out=ebf[:, lb, lo:512], in_=bank(lb)[:, lo:512], func=AF.Exp),
                             r=[bkey(lb)], w=[("ebf", lb)])

                    def emit_PV(i, st):
                        lb = i % 2
                        hh, c, ub, items = st['hh'], st['c'], st['ub'], st['items']
                        for ii, (ks, qs, vt, c0, ncol) in enumerate(items):
                            S.op('pe', lambda: nc.tensor.matmul(bank(ub)[:, c0:c0 + ncol], lhsT=Vn[:, vt[0], vt[1], 64 * hh:64 * hh + 128],
                                                                rhs=ebf[:, lb, c0:c0 + ncol], start=(st['first'] and ii == 0), stop=False,
                                                                skip_group_check=True),
                                 r=["Vn", ("ebf", lb)], w=[bkey(ub)], inc=(ii == len(items) - 1))
                        if st['last']:
                            u0 = st['ub0']
                            a_ = acc[:, (hh * 4 + c) % 2, :]
                            ak = ("acc", (hh * 4 + c) % 2)
                            S.op('dve', lambda: nc.vector.tensor_copy(out=a_, in_=bank(u0)), r=[bkey(u0)], w=[ak])
                            S.op('dve', lambda: nc.vector.tensor_tensor(
                                out=a_.rearrange("p (i r) -> p i r", r=4), in0=bank(u0 + 1).rearrange("p (r i) -> p i r", r=4),
                                in1=a_.rearrange("p (i r) -> p i r", r=4), op=ALU.add), r=[bkey(u0 + 1), ak], w=[ak])
                            S.op('dve', lambda: nc.vector.tensor_tensor(
                                out=a_.rearrange("p (i r) -> p i r", r=16), in0=bank(u0 + 2).rearrange("p (r i) -> p i r", r=16),
                                in1=a_.rearrange("p (i r) -> p i r", r=16), op=ALU.add), r=[bkey(u0 + 2), ak], w=[ak])
                            up = slice(64 * hh, 64 * hh + 64)
                            sp_ = slice(64 * (1 - hh), 64 * (1 - hh) + 64)

                            def fin():
                                S.op('act', lambda: nc.scalar.activation(out=rs[up, :], in_=a_[sp_, :], func=AF.Ln), r=[ak], w=["rs"])
                                S.op('act', lambda: nc.scalar.activation(out=rs[up, :], in_=rs[up, :], func=AF.Exp, scale=-1.0),
                                     r=["rs"], w=["rs"])
                                S.op('dve', lambda: nc.vector.tensor_tensor(
                                    out=mixT[up, 4 + pr, c * 512:(c + 1) * 512], in0=a_[up, :], in1=rs[up, :],
                                    op=ALU.mult), r=[ak, "rs"], w=[("mixT", 4 + pr, c, hh)])
                            pend_n.append([i + 2, fin])

                    pend_n = []
                    for i, st in enumerate(stages):
                        emit_L(i, st)
                        while pend_n and pend_n[0][0] <= i:
                            pend_n.pop(0)[1]()
                        if i >= 1:
                            emit_PV(i - 1, stages[i - 1])
                    emit_PV(len(stages) - 1, stages[-1])
                    while pend_n:
                        pend_n.pop(0)[1]()

            with nc.allow_non_contiguous_dma(reason="gain broadcast"):
                S.dma('sp', gbc[:, :, :], post_mix_d[l:l + 1, :].partition_broadcast(128), w=["gbc"])
            for t in range(16):
                pp = PP[t % 2]
                pks = [bkey(2 * (t % 2)), bkey(2 * (t % 2) + 1)]
                mkeys = [("mixT", hd_, t // 4) for hd_ in range(4)] + [("mixT", 4 + p_, t // 4, h_) for p_ in range(4) for h_ in range(2)]
                for nb_, (wb, wk) in enumerate([(wbA, "wbA"), (wbB, "wbB")]):
                    for kc in range(8):
                        S.op('pe', lambda: nc.tensor.matmul(pp[:, nb_ * 512:(nb_ + 1) * 512], lhsT=mixT[:, kc, t * 128:(t + 1) * 128],
                                                            rhs=wb[:, kc, :], start=(kc == 0), stop=(kc == 7)),
                             r=WK(wk) + mkeys, w=pks, inc=(kc == 7))
                resid_epilogue(pp[:, :], pks, t, "gbc", True)
            pre_evs = [load_w(wbA, "wbA", w1_d[l][:, 0:512], 512), load_w(wbB, "wbB", w1_d[l][:, 512:1024], 512)]
            mlp_fence = True

        with ExitStack() as es4:
            hidT = es4.enter_context(nc.sbuf_tensor("hidT_%d" % l, [128, 32, 1024], BF16))
            fbuf = es4.enter_context(nc.sbuf_tensor("fbuf_%d" % l, [128, 8, D], F32))
            x2T = es4.enter_context(nc.sbuf_tensor("x2T_%d" % l, [128, 8, 1024], BF16))
            sqr = tmp4[:, :].rearrange("p (a b) -> p a b", b=512)
            wbs = [(wbA, "wbA"), (wbB, "wbB")]
            with nc.allow_non_contiguous_dma(reason="gain broadcast"):
                S.dma('sp', gbc[:, :, :], post_mlp_d[l:l + 1, :].partition_broadcast(128), w=["gbc"])
            x2keys = [("x2T", t) for t in range(8)]
            S.fence(x2keys + ["hidT", ("fbuf", 0), ("fbuf", 1)], exclude=pre_evs)
            wi = [0]

            def mm1(hf, after_fb=None):
                for fb in range(8):
                    wb, wk = wbs[wi[0] % 2]
                    wi[0] += 1
                    if not (hf == 0 and fb < 2):
                        load_w(wb, wk, w1_d[l][:, fb * 512:(fb + 1) * 512], 512)
                    for fc in range(4):
                        for tg in range(2):
                            b = (fc * 2 + tg) % 4
                            mm_acc(bank(b), bkey(b), [wb[:, dc, fc * 128:(fc + 1) * 128] for dc in range(8)],
                                   [x2T[:, dc, tg * 512:(tg + 1) * 512] for dc in range(8)], WK(wk) + x2keys[4 * tg:4 * tg + 4])
                            s2 = b % 2
                            S.op('act', lambda: nc.scalar.activation(out=sqr[:, s2, :], in_=bank(b), func=AF.Square),
                                 r=[bkey(b)], w=[("sqr", s2)])
                            S.op('dve', lambda: nc.vector.scalar_tensor_tensor(
                                out=hidT[:, fb * 4 + fc, tg * 512:(tg + 1) * 512], in0=bank(b), scalar=0.0, in1=sqr[:, s2, :],
                                op0=ALU.is_gt, op1=ALU.mult), r=[bkey(b), ("sqr", s2)], w=["hidT"])
                    if after_fb is not None:
                        after_fb(fb)

            def mm2(hf, after_cb=None):
                for cb in range(8):
                    if after_cb is not None:
                        after_cb(cb)
                    wb, wk = wbs[wi[0] % 2]
                    wi[0] += 1
                    wv = wb[:, :, :].rearrange("p a (b c) -> p (a b) c", c=128)
                    S.dma('pool', wv, w2_d[l][:, cb * 128:(cb + 1) * 128].rearrange("(kc p) c -> p kc c", p=128), w=WK(wk))
                    for g4 in range(2):
                        b = 4 + (cb * 2 + g4) % 2
                        for tt in range(4):
                            t = g4 * 4 + tt
                            mm_acc(bank(b)[:, tt * 128:(tt + 1) * 128], bkey(b),
                                   [hidT[:, fc, t * 128:(t + 1) * 128] for fc in range(32)], [wv[:, fc, :] for fc in range(32)],
                                   WK(wk) + ["hidT"])
                        S.op('act', lambda: nc.scalar.copy(out=fbuf[:, g4 * 4:(g4 + 1) * 4, cb * 128:(cb + 1) * 128],
                                                           in_=bank(b).rearrange("p (t c) -> p t c", c=128)),
                             r=[bkey(b)], w=[("fbuf", g4)])

            def epi(hf, t8):
                resid_epilogue(fbuf[:, t8, :], [("fbuf", t8 // 4)], 8 * hf + t8, "gbc", False, add_eng=('dve' if hf == 0 else 'pool'))

            norm_T(x2T, "x2T", 0, 4, gcols[:, l, 1, :], True, dbase=0)
            norm_T(x2T, "x2T", 4, 4, gcols[:, l, 1, :], True, dbase=0)
            mm1(0)
            norm_T(x2T, "x2T", 8, 4, gcols[:, l, 1, :], True, dbase=8)
            norm_T(x2T, "x2T", 12, 4, gcols[:, l, 1, :], True, dbase=8)
            mm2(0)
            mm1(1, after_fb=lambda fb: epi(0, fb))
            pbf = x2T[:, 0:4, :].rearrange("p a (b c) -> p (a b) c", c=256)
            pT = x2T[:, 4:8, :].rearrange("p (a b) c -> p a (b c)", a=2)
            def p_prep(cb):
                if cb == 4:
                    S.dma('pool', pbf, p_d[l].rearrange("(t p) k -> p t k", p=128), w=["pbf"], war=x2keys)
                if cb != 7:
                    return
                for t in range(16):
                    pt = bank(6 + t % 2).bitcast(BF16).rearrange("p (a b) -> p a b", b=128)
                    for kc in range(2):
                        S.op('pe', lambda: nc.tensor.transpose(out=pt[:, kc, :], in_=pbf[:, t, kc * 128:(kc + 1) * 128],
                                                               identity=ident[:, :]), r=["pbf", "ident"], w=[bkey(6 + t % 2)],
                             inc=(kc == 1))
                    S.op('act', lambda: nc.scalar.copy(out=pT[:, :, t * 128:(t + 1) * 128], in_=pt[:, 0:2, :]),
                         r=[bkey(6 + t % 2)], w=[("pT", t)], war=x2keys)

            mm2(1, after_cb=p_prep)
            load_w(wbA, "wbA", wg_d[l][:, 0:512], 512)
            load_w(wbB, "wbB", wg_d[l][:, 512:1024], 512)

            hT = hidT[:, 0:8, :]
            wpp = hidT[:, 16:18, :]
            sgm = hidT[:, 18:20, :].rearrange("p a c -> p (a c)").bitcast(F32).rearrange("p (a c) -> p a c", a=2)
            S.dma('pool', wpp, wp_d[l].rearrange("(kc p) c -> p kc c", p=128), w=["wpp"], war=["hidT"])

            def ple_half(hf, between=None):
                norm_T(hT, "hT", 8 * hf, 8, None, False, war=["hidT"])
                for t8 in range(8):
                    t = 8 * hf + t8
                    for nb_, (wb, wk) in enumerate([(wbA, "wbA"), (wbB, "wbB")]):
                        gb = (t % 2) * 2 + nb_
                        pb = 4 + (t % 2)
                        mm_acc(bank(gb), bkey(gb), [hT[:, dc, t8 * 128:(t8 + 1) * 128] for dc in range(8)],
                               [wb[:, dc, :] for dc in range(8)], WK(wk) + [("hT", t8)])
                        s2 = nb_
                        S.op('act', lambda: nc.scalar.activation(out=sgm[:, s2, :], in_=bank(gb), func=AF.Sigmoid),
                             r=[bkey(gb)], w=[("sgm", s2)], war=["hidT"])
                        mm_acc(bank(pb), bkey(pb), [pT[:, kc, t * 128:(t + 1) * 128] for kc in range(2)],
                               [wpp[:, kc, nb_ * 512:(nb_ + 1) * 512] for kc in range(2)], ["wpp", ("pT", t)])
                        S.op('dve', lambda: nc.vector.tensor_tensor(out=sgm[:, s2, :], in0=bank(pb), in1=sgm[:, s2, :], op=ALU.mult),
                             r=[bkey(pb), ("sgm", s2)], w=[("sgm", s2)])
                        ae = 'pool' if nb_ == 0 else 'dve'
                        S.op(ae, lambda: (nc.gpsimd if ae == 'pool' else nc.vector).tensor_tensor(
                            out=h[:, t, nb_ * 512:(nb_ + 1) * 512], in0=h[:, t, nb_ * 512:(nb_ + 1) * 512], in1=sgm[:, s2, :],
                            op=ALU.add), r=[("sgm", s2), ("h", t)], w=[("h", t)])
                    if between is not None:
                        between(t8)

            ple_half(0, between=lambda t8: epi(1, t8))
            ple_half(1)
            pre_evs = gla_prefetch(l + 1) if l + 1 < n_layers else []
            S.barrier(exclude=pre_evs)

    ov = out_d.rearrange("(t p) d -> p t d", p=128)
    evs = []
    for g in range(8):
        evs.append(S.dma('sp', ov[:, 2 * g:2 * g + 2, :], h[:, 2 * g:2 * g + 2, :],
                         r=[("h", t) for t in range(2 * g, 2 * g + 2)]))
    S._wait('sp', evs)
    return nc


_CONST = {}


def _consts():
    if not _CONST:
        kinds = _bias_tables()
        _CONST['idx'] = np.stack([k[0] for k in kinds], 0)
        _CONST['mask'] = np.ascontiguousarray(np.stack([k[1] for k in kinds], 1).reshape(128, 5 * 128)).astype(np.float32)
        _CONST['ident'] = np.eye(128, dtype=np.float32)
        kk = np.arange(128)
        _CONST['caus'] = (kk[:, None] <= kk[None, :]).astype(np.float32)
    return _CONST


def kernel(x, p, w_in, w_gla_a2, b_gla_a, gla_norm_g, w_out, rel_bias, pre_mix_g, post_mix_g, pre_mlp_g, post_mlp_g,
           w_mlp_in, w_mlp_out, w_ple_gate, w_ple_proj, _n_layers=DEPTH, _cores=8):
    f = lambda a: np.ascontiguousarray(np.asarray(a, dtype=np.float32))
    c = _consts()
    rb = f(rel_bias)
    bt = rb[c['idx']]
    bt = np.ascontiguousarray(bt.transpose(1, 0, 3, 2)).reshape(128, 5 * 8 * 128)
    shared = {
        "w_in": f(w_in), "w_gla_a2": f(w_gla_a2), "b_gla_a": f(b_gla_a), "gla_norm_g": f(gla_norm_g), "w_out": f(w_out),
        "pre_mix_g": f(pre_mix_g), "post_mix_g": f(post_mix_g), "pre_mlp_g": f(pre_mlp_g), "post_mlp_g": f(post_mlp_g),
        "w_mlp_in": f(w_mlp_in), "w_mlp_out": f(w_mlp_out), "w_ple_gate": f(w_ple_gate), "w_ple_proj": f(w_ple_proj),
        "bias_tab": bt, "mask_tab": c['mask'], "ident": c['ident'], "caus": c['caus'],
    }
    x = f(x)
    p = f(p)
    nc = build(_n_layers)
    in_maps = []
    for b in range(_cores):
        m = dict(shared)
        m["x"] = x[b]
        m["p"] = np.ascontiguousarray(p[:, b])
        in_maps.append(m)
    res = run_bass_kernel_spmd(nc, in_maps, core_ids=list(range(_cores)))
    return np.stack([np.asarray(r["out"], dtype=np.float32) for r in res.results], 0)
```

```python
import numpy as np
from contextlib import ExitStack
import concourse.bass as bass
import concourse.mybir as mybir
from concourse.bass_utils import run_bass_kernel_spmd

F32 = mybir.dt.float32
BF16 = mybir.dt.bfloat16
ALU = mybir.AluOpType
AF = mybir.ActivationFunctionType

SEQ = 2048
D = 1024
DEPTH = 2
IN_W = 3088
EPS = 1e-6
NDS = 12


class Sched:
    def __init__(self, nc):
        self.nc = nc
        self.E = {'pe': nc.tensor, 'act': nc.scalar, 'dve': nc.vector, 'pool': nc.gpsimd, 'sp': nc.sync}
        self.semh = {}
        for k in ['pe', 'act', 'dve', 'pool']:
            self.semh[k] = nc.alloc_semaphore("sem_" + k)
        for i in range(NDS):
            self.semh['dma%d' % i] = nc.alloc_semaphore("semdma%d" % i)
        self.cnt = {k: 0 for k in self.semh}
        self.known = {k: {} for k in self.E}
        self.lastw = {}
        self.readers = {}
        self.dnext = {'pool': 0, 'sp': 0}

    def _deps(self, r, w):
        evs = []
        for k in r:
            if k in self.lastw:
                evs.append(self.lastw[k])
        for k in w:
            if k in self.lastw:
                evs.append(self.lastw[k])
            rd = self.readers.get(k)
            if rd:
                evs.extend(rd.items())
        return evs

    def _wait(self, eng, evs):
        need = {}
        for (sn, v) in evs:
            if sn == 'pe' and eng == 'pe':
                continue
            if v > need.get(sn, 0):
                need[sn] = v
        kn = self.known[eng]
        for sn, v in need.items():
            if kn.get(sn, 0) >= v:
                continue
            self.E[eng].wait_ge(self.semh[sn], v)
            kn[sn] = v

    def _record(self, ev, r, w):
        for k in r:
            d = self.readers.setdefault(k, {})
            if ev[1] > d.get(ev[0], 0):
                d[ev[0]] = ev[1]
        for k in w:
            self.lastw[k] = ev
            self.readers[k] = {}

    def _war(self, keys):
        evs = []
        for k in keys:
            if k in self.lastw:
                evs.append(self.lastw[k])
            rd = self.readers.get(k)
            if rd:
                evs.extend(rd.items())
        return evs

    def op(self, eng, fn, r=(), w=(), inc=True, war=()):
        self._wait(eng, self._deps(r, w) + self._war(war))
        ins = fn()
        if inc:
            self.cnt[eng] += 1
            ins.then_inc(self.semh[eng], 1)
            ev = (eng, self.cnt[eng])
        else:
            ev = (eng, self.cnt[eng] + 1)
        self._record(ev, r, w)
        return ins

    def dma(self, q, out, in_, r=(), w=(), war=()):
        half = NDS // 2
        j = self.dnext[q]
        self.dnext[q] = (j + 1) % half
        i = j + (half if q == 'pool' else 0)
        sn = 'dma%d' % i
        evs = self._deps(r, w) + self._war(war)
        if self.cnt[sn] > 0:
            evs.append((sn, self.cnt[sn]))
        self._wait(q, evs)
        self.cnt[sn] += 16
        self.E[q].dma_start(out=out, in_=in_).then_inc(self.semh[sn], 16)
        ev = (sn, self.cnt[sn])
        self._record(ev, r, w)
        return ev

    def barrier(self, exclude=(), engines=None):
        ex = set(exclude)
        evs = [(k, v) for k, v in self.cnt.items() if v > 0 and (k, v) not in ex]
        for e in (engines if engines is not None else self.E):
            self._wait(e, evs)

    def fence(self, keys, exclude=()):
        ex = set(exclude)
        snap = {k: v for k, v in self.cnt.items() if v > 0 and (k, v) not in ex}
        for k in keys:
            d = self.readers.setdefault(k, {})
            for sn, v in snap.items():
                if v > d.get(sn, 0):
                    d[sn] = v


def _t5_bucket_np(dist):
    dist = np.asarray(dist, np.int64)
    d = np.maximum(dist, 1).astype(np.float32)
    large = 16 + (np.log(d / np.float32(16)) / np.float32(np.log(128.0)) * np.float32(16)).astype(np.int32)
    large = np.minimum(large, 31)
    return np.where(dist < 16, dist, large).astype(np.int64)


def _bias_tables():
    k = np.arange(128)[:, None]
    q = np.arange(128)[None, :]
    kinds = []
    for (dil, prev) in [(1, True), (1, False), (4, True), (4, False), (16, False)]:
        if prev:
            j = q + 128 - k
            valid = (k >= q)
        else:
            j = q - k
            valid = (k <= q)
        idx = _t5_bucket_np(np.maximum(j, 0) * dil)
        kinds.append((idx, valid.astype(np.float32)))
    return kinds


def build(n_layers=DEPTH):
    nc = bass.Bass("TRN2", target_bir_lowering=False)
    S = Sched(nc)

    def din(name, shape):
        return nc.dram_tensor(name, shape, F32, kind="ExternalInput").ap()

    x_d = din("x", [SEQ, D])
    p_d = din("p", [DEPTH, SEQ, 256])
    w_in_d = din("w_in", [DEPTH, D, IN_W])
    w_a2_d = din("w_gla_a2", [DEPTH, 16, 256])
    b_a_d = din("b_gla_a", [DEPTH, 256])
    gng_d = din("gla_norm_g", [DEPTH, 128])
    w_out_d = din("w_out", [DEPTH, D, D])
    pre_mix_d = din("pre_mix_g", [DEPTH, D])
    post_mix_d = din("post_mix_g", [DEPTH, D])
    pre_mlp_d = din("pre_mlp_g", [DEPTH, D])
    post_mlp_d = din("post_mlp_g", [DEPTH, D])
    w1_d = din("w_mlp_in", [DEPTH, D, 4 * D])
    w2_d = din("w_mlp_out", [DEPTH, 4 * D, D])
    wg_d = din("w_ple_gate", [DEPTH, D, D])
    wp_d = din("w_ple_proj", [DEPTH, 256, D])
    bias_d = din("bias_tab", [128, 5 * 8 * 128])
    mask_d = din("mask_tab", [128, 5 * 128])
    ident_d = din("ident", [128, 128])
    caus_d = din("caus", [128, 128])
    out_d = nc.dram_tensor("out", [SEQ, D], F32, kind="ExternalOutput").ap()

    sb = nc.alloc_sbuf_tensor
    h = sb("h", [128, 16, D], F32)
    ident = sb("ident_sb", [128, 128], BF16)
    caus = sb("caus_sb", [128, 128], F32)
    onesf = sb("onesf", [128, 128], F32)
    gcols = sb("gcols", [128, DEPTH, 2, 8], F32)
    negb = sb("negb", [128, DEPTH, 2], F32)
    gng = sb("gng", [128, DEPTH], F32)
    ss = sb("ss", [128, 16], F32)
    sq = sb("sq", [128, 16], F32)
    rstd = sb("rstd", [128, 16], F32)
    ssn = sb("ssn", [128, 16], F32)
    sqn = sb("sqn", [128, 16], F32)
    rstdn = sb("rstdn", [128, 16], F32)
    junk = sb("junk", [128, D], BF16)
    xsb = sb("xsb", [128, 2, D], BF16)
    xs = [xsb[:, j, :] for j in range(2)]
    tmp4 = sb("tmp4", [128, D], F32)
    gbc = sb("gbc", [128, 1, D], F32)
    wbA = sb("wbA", [128, 8, 512], BF16)
    wbB = sb("wbB", [128, 8, 512], BF16)
    KEYS = {"wbA": ["wbA"], "wbB": ["wbB", "wbB0", "wbB1"]}

    def WK(k):
        return KEYS[k]

    PP = [nc.alloc_psum_tensor("pp%d" % i, [128, 1024], F32) for i in range(4)]

    def bank(i):
        return PP[i // 2][:, (i % 2) * 512:(i % 2) * 512 + 512]

    def bkey(i):
        return ("bank", i)

    with nc.allow_non_contiguous_dma(reason="small constant loads"):
        S.dma('pool', ident[:, :], ident_d, w=["ident"])
        S.dma('sp', caus[:, :], caus_d, w=["caus"])
        for l in range(n_layers):
            S.dma('sp', gcols[:, l, 0, :], pre_mix_d[l].rearrange("(dc p) -> p dc", p=128), w=["gcols"])
            S.dma('sp', gcols[:, l, 1, :], pre_mlp_d[l].rearrange("(dc p) -> p dc", p=128), w=["gcols"])
            S.dma('sp', negb[:, l, :], b_a_d[l].rearrange("(c p) -> p c", p=128), w=["negb"])
            S.dma('sp', gng[:, l:l + 1], gng_d[l].rearrange("(p o) -> p o", o=1), w=["gng"])
    S.op('dve', lambda: nc.vector.memset(onesf[:, :], 1.0), w=["onesf"])
    S.op('dve', lambda: nc.vector.tensor_scalar(out=negb[:, :, :], in0=negb[:, :, :], scalar1=-1.0, scalar2=None,
                                                op0=ALU.mult), r=["negb"], w=["negb"])

    xv = x_d.rearrange("(t p) d -> p t d", p=128)
    for g in range(8):
        S.dma('sp', h[:, 2 * g:2 * g + 2, :], xv[:, 2 * g:2 * g + 2, :], w=[("h", t) for t in range(2 * g, 2 * g + 2)])

    def norm_stats(t0, nt):
        for t in range(t0, t0 + nt):
            S.op('act', lambda: nc.scalar.activation(out=junk[:, :], in_=h[:, t, :], func=AF.Square,
                                                     accum_out=ssn[:, t:t + 1]), r=[("h", t)], w=[("ssn", t), "junk"])
        S.op('act', lambda: nc.scalar.activation(out=sqn[:, t0:t0 + nt], in_=ssn[:, t0:t0 + nt], func=AF.Sqrt,
                                                 scale=1.0 / D, bias=epsc[:, 0:1]),
             r=[("ssn", t) for t in range(t0, t0 + nt)] + ["epsc"], w=[("sq", t0)])
        S.op('dve', lambda: nc.vector.reciprocal(out=rstdn[:, t0:t0 + nt], in_=sqn[:, t0:t0 + nt]), r=[("sq", t0)], w=[("rstdg", t0)])

    def norm_tile(dstT, dkey, t, t0, gcol, do_norm, war=(), dbase=0):
        j = t % 2
        if do_norm:
            S.op('act', lambda: nc.scalar.mul(out=xs[j], in_=h[:, t, :], mul=rstdn[:, t:t + 1]),
                 r=[("h", t), ("rstdg", t0)], w=[("xs", j)])
        else:
            S.op('act', lambda: nc.scalar.copy(out=xs[j], in_=h[:, t, :]), r=[("h", t)], w=[("xs", j)])
        pt = bank(6 + j).bitcast(BF16).rearrange("p (a b) -> p a b", b=128)
        for dc in range(8):
            S.op('pe', lambda: nc.tensor.transpose(out=pt[:, dc, :], in_=xs[j][:, dc * 128:(dc + 1) * 128],
                                                   identity=ident[:, :]),
                 r=[("xs", j), "ident"], w=[bkey(6 + j)], inc=(dc == 7))
        dst = dstT[:, :, (t - dbase) * 128:(t - dbase + 1) * 128]
        if gcol is not None:
            S.op('dve', lambda: nc.vector.tensor_tensor(out=dst, in0=pt, in1=gcol.unsqueeze(2).to_broadcast([128, 8, 128]),
                                                        op=ALU.mult), r=[bkey(6 + j), "gcols"], w=[(dkey, t - dbase)], war=war)
        else:
            S.op('dve', lambda: nc.vector.tensor_copy(out=dst, in_=pt), r=[bkey(6 + j)], w=[(dkey, t - dbase)], war=war)

    def norm_T(dstT, dkey, t0, nt, gcol, do_norm, war=(), dbase=None):
        if dbase is None:
            dbase = t0
        if do_norm:
            norm_stats(t0, nt)
        for t in range(t0, t0 + nt):
            norm_tile(dstT, dkey, t, t0, gcol, do_norm, war=war, dbase=dbase)

    epsc = sb("epsc", [128, 2], F32)
    S.op('dve', lambda: nc.vector.memset(epsc[:, 0:1], EPS), w=["epsc"])
    S.op('dve', lambda: nc.vector.memset(epsc[:, 1:2], 1.0), w=["epsc"])

    def load_w(wb, wkey, src_ap, ncols, c0=0):
        kc = src_ap.shape[0] // 128
        return S.dma('pool', wb[:, 0:kc, c0:c0 + ncols], src_ap.rearrange("(kc p) c -> p kc c", p=128),
                     w=(WK(wkey) if wkey in KEYS else [wkey]))

    SPLIT = [False]

    def mm_acc(ps, pskey, lhs_list, rhs_list, rkeys):
        n = len(lhs_list)
        wide = SPLIT[0] and len(rhs_list[0].shape) == 2 and rhs_list[0].shape[-1] == 512
        for i in range(n):
            if wide:
                for ch in range(4):
                    cs = slice(ch * 128, (ch + 1) * 128)
                    last = (i == n - 1 and ch == 3)
                    S.op('pe', lambda: nc.tensor.matmul(ps[:, cs], lhsT=lhs_list[i], rhs=rhs_list[i][:, cs],
                                                        start=(i == 0 and ch == 0), stop=last, skip_group_check=True),
                         r=rkeys, w=[pskey], inc=last)
            else:
                S.op('pe', lambda: nc.tensor.matmul(ps, lhsT=lhs_list[i], rhs=rhs_list[i], start=(i == 0), stop=(i == n - 1)),
                     r=rkeys, w=[pskey], inc=(i == n - 1))

    def resid_A(src_ap, srckeys, t):
        S.op('act', lambda: nc.scalar.activation(out=junk[:, :], in_=src_ap, func=AF.Square, accum_out=ss[:, t:t + 1]),
             r=srckeys, w=[("ss", t), "junk"])
        S.op('act', lambda: nc.scalar.activation(out=sq[:, t:t + 1], in_=ss[:, t:t + 1], func=AF.Sqrt, scale=1.0 / D,
                                                 bias=epsc[:, 0:1]), r=[("ss", t), "epsc"], w=[("sq", t)])

    def resid_B(src_ap, srckeys, t, gkey, add_eng='pool'):
        S.op('dve', lambda: nc.vector.reciprocal(out=rstd[:, t:t + 1], in_=sq[:, t:t + 1]), r=[("sq", t)], w=[("rstd", t)])
        S.op('dve', lambda: nc.vector.scalar_tensor_tensor(out=tmp4[:, :], in0=src_ap, scalar=rstd[:, t:t + 1],
                                                           in1=gbc[:, 0, :], op0=ALU.mult, op1=ALU.mult),
             r=srckeys + [("rstd", t), gkey], w=["tmp4", ("sqr", 0), ("sqr", 1)])
        S.op(add_eng, lambda: (nc.gpsimd if add_eng == 'pool' else nc.vector).tensor_tensor(out=h[:, t, :], in0=h[:, t, :], in1=tmp4[:, :], op=ALU.add),
             r=["tmp4", ("sqr", 0), ("sqr", 1), ("h", t)], w=[("h", t)])

    def resid_epilogue(src_ap, srckeys, t, gkey, src_is_psum, add_eng='pool'):
        resid_A(src_ap, srckeys, t)
        resid_B(src_ap, srckeys, t, gkey, add_eng)

    def gla_prefetch(l):
        evs = [load_w(wbB, "wbB1", w_in_d[l][:, 1536:1664], 128, c0=256),
               load_w(wbB, "wbB0", w_in_d[l][:, 0:128], 128, c0=0),
               load_w(wbB, "wbB0", w_in_d[l][:, 256:384], 128, c0=128),
               load_w(wbA, "wbA", w_in_d[l][:, 512:1024], 512)]
        return evs

    pre_evs = gla_prefetch(0)
    for l in range(n_layers):
        w_in_l = w_in_d[l]
        with ExitStack() as es1:
            xnT = es1.enter_context(nc.sbuf_tensor("xnT_%d" % l, [128, 8, SEQ], BF16))
            mixT = es1.enter_context(nc.sbuf_tensor("mixT_%d" % l, [128, 8, SEQ], BF16))
            for g_ in range(4):
                norm_T(xnT, "xnT", 4 * g_, 4, gcols[:, l, 0, :], True, dbase=0)
            xkeys = [("xnT", t) for t in range(16)]

            def xk(tok0, ntok):
                return [("xnT", t) for t in range(tok0 // 128, (tok0 + ntok + 127) // 128)]

            with ExitStack() as es2:
                def A2(name, shape, dt):
                    return es2.enter_context(nc.sbuf_tensor("%s_%d" % (name, l), shape, dt))
                glrT = A2("glrT", [128, SEQ], BF16)
                wa2 = A2("wa2", [128, 256], BF16)
                qgz = A2("qgz", [128, 2, SEQ], BF16)
                kdT = A2("kdT", [128, SEQ], BF16)
                kg = A2("kg", [128, 16, 128], BF16)
                gv = A2("gv", [128, 16, 256], BF16)
                alast = A2("alast", [128, 16], F32)
                scr = A2("scr", [128, 2, 3, 512], F32)
                kgT = A2("kgT", [128, 2, 512], BF16)
                Sf = A2("Sf", [128, 256], F32)
                Sb = A2("Sb", [128, 3, 256], BF16)
                attm = A2("attm", [128, 2, 2, 128], BF16)

                def drive(gens, width):
                    gens = list(gens)
                    active = []
                    while gens or active:
                        while len(active) < width and gens:
                            active.append(gens.pop(0))
                        for g_ in list(active):
                            try:
                                next(g_)
                            except StopIteration:
                                active.remove(g_)

                S.op('dve', lambda: nc.vector.memset(wa2[:, :], 0.0), w=["wa2"])
                S.op('dve', lambda: nc.vector.memset(qgz[:, :, :], 0.0), w=["qgz"])
                S.dma('pool', wa2[0:16, :], w_a2_d[l], r=[], w=["wa2"])
                for tg in range(4):
                    b = tg % 2
                    mm_acc(bank(b), bkey(b), [wbB[:, dc, 256:384] for dc in range(8)],
                           [xnT[:, dc, tg * 512:(tg + 1) * 512] for dc in range(8)], ["wbB1"] + xk(tg * 512, 512))
                    S.op('act', lambda: nc.scalar.copy(out=glrT[:, tg * 512:(tg + 1) * 512], in_=bank(b)),
                         r=[bkey(b)], w=[("glrT", tg)])

                def prep_gen(pr, tg):
                    s_ = tg % 2
                    t1, cc, Eq = scr[:, s_, 0, :], scr[:, s_, 1, :], scr[:, s_, 2, :]
                    cc3 = cc.rearrange("p (a b) -> p a b", b=128)
                    k1, kc, kq, kk = ("t1", s_), ("cc", s_), ("Eq", s_), ("kgT", s_)
                    tsl = slice(tg * 512, (tg + 1) * 512)
                    bx, bq, bk_ = s_, 2 + s_, 4 + s_
                    S.op('pe', lambda: nc.tensor.matmul(bank(bx), lhsT=wa2[:, pr * 128:(pr + 1) * 128], rhs=glrT[:, tsl],
                                                        start=True, stop=True), r=["wa2", ("glrT", tg)], w=[bkey(bx)])
                    yield
                    mm_acc(bank(bq), bkey(bq), [wbB[:, dc, 0:128] for dc in range(8)],
                           [xnT[:, dc, tsl] for dc in range(8)], ["wbB0"] + xk(tg * 512, 512))
                    yield
                    mm_acc(bank(bk_), bkey(bk_), [wbB[:, dc, 128:256] for dc in range(8)],
                           [xnT[:, dc, tsl] for dc in range(8)], ["wbB0"] + xk(tg * 512, 512))
                    yield
                    S.op('act', lambda: nc.scalar.activation(out=t1, in_=bank(bx), func=AF.Exp, scale=-1.0,
                                                             bias=negb[:, l, pr:pr + 1]), r=[bkey(bx), "negb"], w=[k1])
                    yield
                    S.op('act', lambda: nc.scalar.activation(out=t1, in_=t1, func=AF.Ln, scale=1.0, bias=epsc[:, 1:2]),
                         r=[k1, "epsc"], w=[k1])
                    yield
                    for ch in range(4):
                        S.op('dve', lambda: nc.vector.tensor_tensor_scan(
                            out=cc3[:, ch, :], data0=onesf[:, :], data1=t1[:, ch * 128:(ch + 1) * 128], initial=0.0,
                            op0=ALU.mult, op1=ALU.add), r=[k1, "onesf"], w=[kc])
                    yield
                    S.op('act', lambda: nc.scalar.activation(out=Eq, in_=cc, func=AF.Exp, scale=-1.0 / 16.0), r=[kc], w=[kq])
                    yield
                    S.op('dve', lambda: nc.vector.tensor_tensor(
                        out=t1.rearrange("p (a b) -> p a b", b=128), in0=cc3, in1=cc3[:, :, 127:128].to_broadcast([128, 4, 128]),
                        op=ALU.subtract), r=[kc], w=[k1])
                    yield
                    S.op('act', lambda: nc.scalar.activation(out=t1, in_=t1, func=AF.Exp, scale=1.0 / 16.0), r=[k1], w=[k1])
                    yield
                    S.op('act', lambda: nc.scalar.activation(out=cc, in_=cc, func=AF.Exp, scale=1.0 / 16.0), r=[kc], w=[kc])
                    yield
                    S.op('act', lambda: nc.scalar.copy(out=alast[:, tg * 4:(tg + 1) * 4],
                                                       in_=Eq.rearrange("p (a b) -> p a b", b=128)[:, :, 127]), r=[kq], w=[("alast", tg)])
                    yield
                    for hh in range(2):
                        ps_ = slice(hh * 64, hh * 64 + 64)
                        S.op('dve', lambda: nc.vector.scalar_tensor_tensor(
                            out=qgz[ps_, hh, tsl], in0=bank(bq)[ps_, :], scalar=0.125, in1=Eq[ps_, :],
                            op0=ALU.mult, op1=ALU.mult), r=[bkey(bq), kq], w=[("qgz", tg)])
                    yield
                    S.op('dve', lambda: nc.vector.tensor_tensor(out=kdT[:, tsl], in0=bank(bk_), in1=cc, op=ALU.mult),
                         r=[bkey(bk_), kc], w=[("kdT", tg)])
                    yield
                    S.op('dve', lambda: nc.vector.tensor_tensor(out=kgT[:, s_, :], in0=bank(bk_), in1=t1, op=ALU.mult),
                         r=[bkey(bk_), k1], w=[kk])
                    yield
                    pt = bank(6).bitcast(BF16).rearrange("p (a b) -> p a b", b=128)
                    for ch in range(4):
                        S.op('pe', lambda: nc.tensor.transpose(out=pt[:, ch, :], in_=kgT[:, s_, ch * 128:(ch + 1) * 128],
                                                               identity=ident[:, :]), r=[kk, "ident"], w=[bkey(6)], inc=(ch == 3))
                    S.op('act', lambda: nc.scalar.copy(out=kg[:, tg * 4:(tg + 1) * 4, :], in_=pt[:, 0:4, :]),
                         r=[bkey(6)], w=[("kg", tg)])
                    yield

                def gv_gen(pr):
                    for t in range(16):
                        mm_acc(bank(7)[:, 0:256], bkey(7), [xnT[:, dc, t * 128:(t + 1) * 128] for dc in range(8)],
                               [wbA[:, dc, pr * 256:(pr + 1) * 256] for dc in range(8)], WK("wbA") + [("xnT", t)])
                        yield
                        S.op('act', lambda: nc.scalar.copy(out=gv[:, t, :], in_=bank(7)[:, 0:256]), r=[bkey(7)], w=[("gv", t)])
                        yield
                        yield

                def epi_ops(pr, tg, hh):
                    hd = 2 * pr + hh
                    osb, osq, sgt = scr[:, hh, 0, :], scr[:, hh, 1, :], scr[:, hh, 2, :]
                    ko, kq2, kg2 = ("t1", hh), ("cc", hh), ("Eq", hh)
                    tsl = slice(tg * 512, (tg + 1) * 512)
                    ops = []
                    ops.append(lambda: S.op('act', lambda: nc.scalar.copy(out=osb, in_=bank(4 + hh)), r=[bkey(4 + hh)], w=[ko]))
                    ops.append(lambda: S.op('act', lambda: nc.scalar.activation(out=osq, in_=osb, func=AF.Square), r=[ko], w=[kq2]))
                    ops.append(lambda: S.op('pe', lambda: nc.tensor.matmul(bank(hh), lhsT=onesf[:, :], rhs=osq, start=True, stop=True),
                                            r=["onesf", kq2], w=[bkey(hh)]))
                    ops.append(lambda: S.op('act', lambda: nc.scalar.activation(out=osq, in_=bank(hh), func=AF.Ln, scale=1.0 / 128.0,
                                                                             bias=epsc[:, 0:1]), r=[bkey(hh), "epsc"], w=[kq2]))
                    ops.append(lambda: S.op('act', lambda: nc.scalar.activation(out=osq, in_=osq, func=AF.Exp, scale=-0.5), r=[kq2], w=[kq2]))
                    ops.append(lambda: S.op('dve', lambda: nc.vector.tensor_tensor(out=osb, in0=osb, in1=osq, op=ALU.mult),
                                            r=[ko, kq2], w=[ko]))
                    ops.append(lambda: mm_acc(bank(hh), bkey(hh), [wbB[:, dc, (2 + hh) * 128:(3 + hh) * 128] for dc in range(8)],
                                              [xnT[:, dc, tsl] for dc in range(8)], ["wbB1"] + xk(tg * 512, 512)))
                    ops.append(lambda: S.op('act', lambda: nc.scalar.activation(out=sgt, in_=bank(hh), func=AF.Silu), r=[bkey(hh)], w=[kg2]))
                    ops.append(lambda: S.op('dve', lambda: nc.vector.scalar_tensor_tensor(
                        out=mixT[:, hd, tsl], in0=osb, scalar=gng[:, l:l + 1], in1=sgt, op0=ALU.mult, op1=ALU.mult),
                        r=[ko, kg2, "gng"], w=[("mixT", hd, tg)]))
                    return ops

                for pr in range(2):
                    for hh_ in range(2):
                        load_w(wbB, "wbB1", w_in_l[:, 1024 + (2 * pr + hh_) * 128:1024 + (2 * pr + hh_ + 1) * 128], 128, c0=256 + hh_ * 128)
                    drive([prep_gen(pr, 0), prep_gen(pr, 1), gv_gen(pr), prep_gen(pr, 2), prep_gen(pr, 3)], 3)
                    if pr == 0:
                        load_w(wbB, "wbB0", w_in_l[:, 128:256], 128, c0=0)
                        load_w(wbB, "wbB0", w_in_l[:, 384:512], 128, c0=128)

                    S.op('dve', lambda: nc.vector.memset(Sf[:, :], 0.0), w=["Sf"])
                    S.op('dve', lambda: nc.vector.memset(Sb[:, 0, :], 0.0), w=[("Sb", 0)])
                    pend = []

                    def drip(k=1):
                        for _ in range(k):
                            if pend:
                                pend.pop(0)()

                    def stage_A(n):
                        tg = n // 4
                        csl = slice(n * 128, (n + 1) * 128)
                        a2 = n % 2
                        ab = 2 + a2
                        kb = 6 + a2
                        for hh in range(2):
                            S.op('pe', lambda: nc.tensor.matmul(bank(ab)[:, hh * 128:(hh + 1) * 128], lhsT=kdT[:, csl],
                                                                rhs=qgz[:, hh, csl], start=True, stop=True, skip_group_check=True),
                                 r=[("kdT", tg), ("qgz", tg)], w=[bkey(ab)], inc=(hh == 1))
                        S.op('pe', lambda: nc.tensor.matmul(bank(kb)[:, 0:256], lhsT=kg[:, n, :], rhs=gv[:, n, :], start=True, stop=True),
                             r=[("kg", tg), ("gv", n)], w=[bkey(kb)])
                        S.op('dve', lambda: nc.vector.tensor_tensor(
                            out=attm[:, a2, :, :], in0=bank(ab)[:, 0:256].rearrange("p (a b) -> p a b", b=128),
                            in1=caus[:, :].unsqueeze(1).to_broadcast([128, 2, 128]), op=ALU.mult),
                            r=[bkey(ab), "caus"], w=[("attm", a2)])
                        S.op('dve', lambda: nc.vector.scalar_tensor_tensor(
                            out=Sf[:, :], in0=Sf[:, :], scalar=alast[:, n:n + 1], in1=bank(kb)[:, 0:256],
                            op0=ALU.mult, op1=ALU.add), r=["Sf", bkey(kb), ("alast", tg)], w=["Sf"])
                        S.op('act', lambda: nc.scalar.copy(out=Sb[:, (n + 1) % 3, :], in_=Sf[:, :]), r=["Sf"], w=[("Sb", (n + 1) % 3)])

                    def stage_B(n):
                        tg = n // 4
                        csl = slice(n * 128, (n + 1) * 128)
                        a2 = n % 2
                        for hh in range(2):
                            ob = bank(4 + hh)[:, (n % 4) * 128:(n % 4 + 1) * 128]
                            S.op('pe', lambda: nc.tensor.matmul(ob, lhsT=gv[:, n, hh * 128:(hh + 1) * 128], rhs=attm[:, a2, hh, :],
                                                                start=True, stop=False, skip_group_check=True),
                                 r=[("gv", n), ("attm", a2)], w=[bkey(4 + hh)], inc=False)
                            S.op('pe', lambda: nc.tensor.matmul(ob, lhsT=Sb[:, n % 3, hh * 128:(hh + 1) * 128], rhs=qgz[:, hh, csl],
                                                                start=False, stop=True, skip_group_check=True),
                                 r=[("Sb", n % 3), ("qgz", tg)], w=[bkey(4 + hh)])

                    stage_A(0)
                    for n in range(16):
                        if n + 1 < 16:
                            stage_A(n + 1)
                        drip(2)
                        stage_B(n)
                        drip(2)
                        if n % 4 == 3:
                            drip(len(pend))
                            e0, e1 = epi_ops(pr, n // 4, 0), epi_ops(pr, n // 4, 1)
                            e0[0]()
                            e1[0]()
                            for x0, x1 in zip(e0[1:], e1[1:]):
                                pend.append(x0)
                                pend.append(x1)
                    drip(len(pend))
                S.barrier(engines=('act', 'dve', 'sp'))

            with ExitStack() as es3:
                qz = es3.enter_context(nc.sbuf_tensor("qz_%d" % l, [128, 2, SEQ], BF16))
                kT = es3.enter_context(nc.sbuf_tensor("kT_%d" % l, [128, SEQ], BF16))
                Vn = es3.enter_context(nc.sbuf_tensor("Vn_%d" % l, [128, 3, 16, 192], BF16))
                ebf = es3.enter_context(nc.sbuf_tensor("ebf_%d" % l, [128, 2, 512], BF16))
                acc = es3.enter_context(nc.sbuf_tensor("acc_%d" % l, [128, 2, 512], F32))
                rs = es3.enter_context(nc.sbuf_tensor("rs_%d" % l, [128, 512], F32))
                biasM = es3.enter_context(nc.sbuf_tensor("biasM_%d" % l, [128, 5, 2, 128], BF16))
                madd = es3.enter_context(nc.sbuf_tensor("madd_%d" % l, [128, 5, 128], F32))
                et = es3.enter_context(nc.sbuf_tensor("et_%d" % l, [128, 2, 128], F32))
                S.dma('sp', madd[:, :, :], mask_d.rearrange("p (k q) -> p k q", q=128), w=["madd"])
                S.op('dve', lambda: nc.vector.tensor_scalar(out=madd[:, :, :], in0=madd[:, :, :], scalar1=-1.0, scalar2=30000.0,
                                                            op0=ALU.add, op1=ALU.mult), r=["madd"], w=["madd"])
                S.op('dve', lambda: nc.vector.memset(qz[:, :, :], 0.0), w=["qz"])
                S.op('dve', lambda: nc.vector.memset(Vn[:, :, :, 64:128], 1.0), w=["Vn"])
                for pr in range(4):
                    wb = wbA if pr % 2 == 0 else wbB
                    wk = "wbA" if pr % 2 == 0 else "wbB"
                    bv_ = bias_d.rearrange("p (k h q) -> p k h q", k=5, h=8)
                    for kd in range(5):
                        S.dma('sp', et[:, :, :], bv_[:, kd, 2 * pr:2 * pr + 2, :], w=["et"])
                        S.op('dve', lambda: nc.vector.tensor_tensor(
                            out=biasM[:, kd, :, :], in0=et[:, :, :], in1=madd[:, kd, :].unsqueeze(1).to_broadcast([128, 2, 128]),
                            op=ALU.add), r=["et", "madd"], w=["biasM"])
                    for i, c0 in enumerate([1552, 2064, 2576]):
                        S.dma('pool', wb[:, :, i * 128:(i + 1) * 128],
                              w_in_l[:, c0 + pr * 128:c0 + (pr + 1) * 128].rearrange("(kc p) c -> p kc c", p=128), w=WK(wk))
                    for tg in range(4):
                        tsl = slice(tg * 512, (tg + 1) * 512)
                        mm_acc(bank(5), bkey(5), [wb[:, dc, 0:128] for dc in range(8)], [xnT[:, dc, tsl] for dc in range(8)],
                               WK(wk) + xk(tg * 512, 512))
                        for hh in range(2):
                            ps_ = slice(hh * 64, hh * 64 + 64)
                            S.op('act', lambda: nc.scalar.mul(out=qz[ps_, hh, tsl], in_=bank(5)[ps_, :], mul=0.125),
                                 r=[bkey(5)], w=["qz"])
                        mm_acc(bank(6), bkey(6), [wb[:, dc, 128:256] for dc in range(8)], [xnT[:, dc, tsl] for dc in range(8)],
                               WK(wk) + xk(tg * 512, 512))
                        S.op('act', lambda: nc.scalar.copy(out=kT[:, tsl], in_=bank(6)), r=[bkey(6)], w=["kT"])
                    vT = xsb[:, :, :].rearrange("p a b -> p (a b)")
                    for tg in range(4):
                        tsl = slice(tg * 512, (tg + 1) * 512)
                        b = 5 + (tg % 2)
                        mm_acc(bank(b), bkey(b), [wb[:, dc, 256:384] for dc in range(8)], [xnT[:, dc, tsl] for dc in range(8)],
                               WK(wk) + xk(tg * 512, 512))
                        S.op('act', lambda: nc.scalar.copy(out=vT[:, tsl], in_=bank(b)), r=[bkey(b)], w=[("xs", tg // 2)])
                    for lay in range(3):
                        for g8 in range(2):
                            b = 5 + ((lay * 2 + g8) % 2)
                            ptb = bank(b).bitcast(BF16).rearrange("p (a b) -> p a b", b=128)
                            for tt in range(8):
                                ti = g8 * 8 + tt
                                if lay == 0:
                                    tok = slice(ti * 128, (ti + 1) * 128)
                                elif lay == 1:
                                    c_, r_ = ti // 4, ti % 4
                                    tok = slice(512 * c_ + r_, 512 * c_ + 512, 4)
                                else:
                                    tok = slice(ti, SEQ, 16)
                                S.op('pe', lambda: nc.tensor.transpose(out=ptb[:, tt, :], in_=vT[:, tok], identity=ident[:, :]),
                                     r=[("xs", 0), ("xs", 1), "ident"], w=[bkey(b)], inc=(tt == 7))
                            evac_eng = 'act' if (lay * 2 + g8) % 2 == 0 else 'dve'
                            dstv = Vn[:, lay, g8 * 8:(g8 + 1) * 8, :].rearrange("p t (h e) -> p t h e", e=64)[:, :, 0:3:2, :]
                            srcv = ptb.rearrange("p t (h e) -> p t h e", e=64)
                            if evac_eng == 'act':
                                S.op('act', lambda: nc.scalar.copy(out=dstv, in_=srcv), r=[bkey(b)], w=["Vn"])
                            else:
                                S.op('dve', lambda: nc.vector.tensor_copy(out=dstv, in_=srcv), r=[bkey(b)], w=["Vn"])
                    if pr == 3:
                        load_w(wbA, "wbA", w_out_d[l][:, 0:512], 512)
                        load_w(wbB, "wbB", w_out_d[l][:, 512:1024], 512)
                    stages = []
                    for hh in range(2):
                        for c in range(4):
                            g = (hh * 4 + c)
                            ub0 = 2 if g % 2 == 0 else 5
                            cur1, prev1, cur4, prev4, d16 = [], [], [], [], []
                            for n2 in range(4):
                                nb = 4 * c + n2
                                qs = slice(nb * 128, (nb + 1) * 128)
                                cur1.append((qs, qs, (0, nb), n2 * 128, 128))
                                if nb > 0:
                                    prev1.append((slice((nb - 1) * 128, nb * 128), qs, (0, nb - 1), n2 * 128, 128))
                            for r_ in range(4):
                                qs = slice(512 * c + r_, 512 * c + 512, 4)
                                cur4.append((qs, qs, (1, 4 * c + r_), r_ * 128, 128))
                                if c > 0:
                                    prev4.append((slice(512 * (c - 1) + r_, 512 * c, 4), qs, (1, 4 * (c - 1) + r_), r_ * 128, 128))
                            for r16 in range(16):
                                d16.append((slice(r16, SEQ, 16), slice(512 * c + r16, 512 * c + 512, 16), (2, r16), r16 * 32, 32))
                            grp = [(1, ub0, cur1), (0, ub0, prev1), (3, ub0 + 1, cur4), (2, ub0 + 1, prev4), (4, ub0 + 2, d16)]
                            grp = [x for x in grp if x[2]]
                            started = set()
                            for gi, (kind, ub, items) in enumerate(grp):
                                first = ub not in started
                                started.add(ub)
                                stages.append(dict(hh=hh, c=c, kind=kind, ub=ub, items=items, first=first,
                                                   last=(gi == len(grp) - 1), ub0=ub0))

                    def emit_L(i, st):
                        lb = i % 2
                        hh, c, kind, items = st['hh'], st['c'], st['kind'], st['items']
                        lo = items[0][3]
                        if kind == 4:
                            bv = biasM[:, 4, hh, 32 * c:32 * c + 32].unsqueeze(1).to_broadcast([128, 16, 32])
                        else:
                            bv = biasM[:, kind, hh, :].unsqueeze(1).to_broadcast([128, (512 - lo) // 128, 128])
                        S.op('pe', lambda: nc.tensor.matmul(bank(lb)[:, lo:512], lhsT=ident[:, :], rhs=bv, start=True, stop=False,
                                                            skip_group_check=True), r=["ident", "biasM"], w=[bkey(lb)], inc=False)
                        for (ks, qs, vt, c0, ncol) in items:
                            S.op('pe', lambda: nc.tensor.matmul(bank(lb)[:, c0:c0 + ncol], lhsT=kT[:, ks], rhs=qz[:, hh, qs],
                                                                start=False, stop=(c0 + ncol == 512), skip_group_check=True),
                                 r=["kT", "qz"], w=[bkey(lb)], inc=(c0 + ncol == 512))
                        S.op('act', lambda: nc.scalar.activation(out=ebf[:, lb, lo:512], in_=bank(lb)[:, lo:512], func=AF.Exp),
                             r=[bkey(lb)], w=[("ebf", lb)])

                    def emit_PV(i, st):
                        lb = i % 2
                        hh, c, ub, items = st['hh'], st['c'], st['ub'], st['items']
                        for ii, (ks, qs, vt, c0, ncol) in enumerate(items):
                            S.op('pe', lambda: nc.tensor.matmul(bank(ub)[:, c0:c0 + ncol], lhsT=Vn[:, vt[0], vt[1], 64 * hh:64 * hh + 128],
                                                                rhs=ebf[:, lb, c0:c0 + ncol], start=(st['first'] and ii == 0), stop=False,
                                                                skip_group_check=True),
                                 r=["Vn", ("ebf", lb)], w=[bkey(ub)], inc=(ii == len(items) - 1))
                        if st['last']:
                            u0 = st['ub0']
                            a_ = acc[:, (hh * 4 + c) % 2, :]
                            ak = ("acc", (hh * 4 + c) % 2)
                            S.op('dve', lambda: nc.vector.tensor_copy(out=a_, in_=bank(u0)), r=[bkey(u0)], w=[ak])
                            S.op('dve', lambda: nc.vector.tensor_tensor(
                                out=a_.rearrange("p (i r) -> p i r", r=4), in0=bank(u0 + 1).rearrange("p (r i) -> p i r", r=4),
                                in1=a_.rearrange("p (i r) -> p i r", r=4), op=ALU.add), r=[bkey(u0 + 1), ak], w=[ak])
                            S.op('dve', lambda: nc.vector.tensor_tensor(
                                out=a_.rearrange("p (i r) -> p i r", r=16), in0=bank(u0 + 2).rearrange("p (r i) -> p i r", r=16),
                                in1=a_.rearrange("p (i r) -> p i r", r=16), op=ALU.add), r=[bkey(u0 + 2), ak], w=[ak])
                            up = slice(64 * hh, 64 * hh + 64)
                            sp_ = slice(64 * (1 - hh), 64 * (1 - hh) + 64)

                            def fin():
                                S.op('act', lambda: nc.scalar.activation(out=rs[up, :], in_=a_[sp_, :], func=AF.Ln), r=[ak], w=["rs"])
                                S.op('act', lambda: nc.scalar.activation(out=rs[up, :], in_=rs[up, :], func=AF.Exp, scale=-1.0),
                                     r=["rs"], w=["rs"])
                                S.op('dve', lambda: nc.vector.tensor_tensor(
                                    out=mixT[up, 4 + pr, c * 512:(c + 1) * 512], in0=a_[up, :], in1=rs[up, :],
                                    op=ALU.mult), r=[ak, "rs"], w=[("mixT", 4 + pr, c, hh)])
                            pend_n.append([i + 2, fin])

                    pend_n = []
                    for i, st in enumerate(stages):
                        emit_L(i, st)
                        while pend_n and pend_n[0][0] <= i:
                            pend_n.pop(0)[1]()
                        if i >= 1:
                            emit_PV(i - 1, stages[i - 1])
                    emit_PV(len(stages) - 1, stages[-1])
                    while pend_n:
                        pend_n.pop(0)[1]()

            with nc.allow_non_contiguous_dma(reason="gain broadcast"):
                S.dma('sp', gbc[:, :, :], post_mix_d[l:l + 1, :].partition_broadcast(128), w=["gbc"])
            for t in range(16):
                pp = PP[t % 2]
                pks = [bkey(2 * (t % 2)), bkey(2 * (t % 2) + 1)]
                mkeys = [("mixT", hd_, t // 4) for hd_ in range(4)] + [("mixT", 4 + p_, t // 4, h_) for p_ in range(4) for h_ in range(2)]
                for nb_, (wb, wk) in enumerate([(wbA, "wbA"), (wbB, "wbB")]):
                    for kc in range(8):
                        S.op('pe', lambda: nc.tensor.matmul(pp[:, nb_ * 512:(nb_ + 1) * 512], lhsT=mixT[:, kc, t * 128:(t + 1) * 128],
                                                            rhs=wb[:, kc, :], start=(kc == 0), stop=(kc == 7)),
                             r=WK(wk) + mkeys, w=pks, inc=(kc == 7))
                resid_epilogue(pp[:, :], pks, t, "gbc", True)
            pre_evs = [load_w(wbA, "wbA", w1_d[l][:, 0:512], 512), load_w(wbB, "wbB", w1_d[l][:, 512:1024], 512)]
            mlp_fence = True

        with ExitStack() as es4:
            hidT = es4.enter_context(nc.sbuf_tensor("hidT_%d" % l, [128, 32, 1024], BF16))
            fbuf = es4.enter_context(nc.sbuf_tensor("fbuf_%d" % l, [128, 8, D], F32))
            x2T = es4.enter_context(nc.sbuf_tensor("x2T_%d" % l, [128, 8, 1024], BF16))
            sqr = tmp4[:, :].rearrange("p (a b) -> p a b", b=512)
            wbs = [(wbA, "wbA"), (wbB, "wbB")]
            with nc.allow_non_contiguous_dma(reason="gain broadcast"):
                S.dma('sp', gbc[:, :, :], post_mlp_d[l:l + 1, :].partition_broadcast(128), w=["gbc"])
            x2keys = [("x2T", t) for t in range(8)]
            S.fence(x2keys + ["hidT", ("fbuf", 0), ("fbuf", 1)], exclude=pre_evs)
            wi = [0]

            def mm1(hf, after_fb=None):
                for fb in range(8):
                    wb, wk = wbs[wi[0] % 2]
                    wi[0] += 1
                    if not (hf == 0 and fb < 2):
                        load_w(wb, wk, w1_d[l][:, fb * 512:(fb + 1) * 512], 512)
                    for fc in range(4):
                        for tg in range(2):
                            b = (fc * 2 + tg) % 4
                            mm_acc(bank(b), bkey(b), [wb[:, dc, fc * 128:(fc + 1) * 128] for dc in range(8)],
                                   [x2T[:, dc, tg * 512:(tg + 1) * 512] for dc in range(8)], WK(wk) + x2keys[4 * tg:4 * tg + 4])
                            s2 = b % 2
                            S.op('act', lambda: nc.scalar.activation(out=sqr[:, s2, :], in_=bank(b), func=AF.Square),
                                 r=[bkey(b)], w=[("sqr", s2)])
                            S.op('dve', lambda: nc.vector.scalar_tensor_tensor(
                                out=hidT[:, fb * 4 + fc, tg * 512:(tg + 1) * 512], in0=bank(b), scalar=0.0, in1=sqr[:, s2, :],
                                op0=ALU.is_gt, op1=ALU.mult), r=[bkey(b), ("sqr", s2)], w=["hidT"])
                    if after_fb is not None:
                        after_fb(fb)

            def mm2(hf, after_cb=None):
                for cb in range(8):
                    if after_cb is not None:
                        after_cb(cb)
                    wb, wk = wbs[wi[0] % 2]
                    wi[0] += 1
                    wv = wb[:, :, :].rearrange("p a (b c) -> p (a b) c", c=128)
                    S.dma('pool', wv, w2_d[l][:, cb * 128:(cb + 1) * 128].rearrange("(kc p) c -> p kc c", p=128), w=WK(wk))
                    for g4 in range(2):
                        b = 4 + (cb * 2 + g4) % 2
                        for tt in range(4):
                            t = g4 * 4 + tt
                            mm_acc(bank(b)[:, tt * 128:(tt + 1) * 128], bkey(b),
                                   [hidT[:, fc, t * 128:(t + 1) * 128] for fc in range(32)], [wv[:, fc, :] for fc in range(32)],
                                   WK(wk) + ["hidT"])
                        S.op('act', lambda: nc.scalar.copy(out=fbuf[:, g4 * 4:(g4 + 1) * 4, cb * 128:(cb + 1) * 128],
                                                           in_=bank(b).rearrange("p (t c) -> p t c", c=128)),
                             r=[bkey(b)], w=[("fbuf", g4)])

            def epi(hf, t8):
                resid_epilogue(fbuf[:, t8, :], [("fbuf", t8 // 4)], 8 * hf + t8, "gbc", False, add_eng=('dve' if hf == 0 else 'pool'))

            norm_T(x2T, "x2T", 0, 4, gcols[:, l, 1, :], True, dbase=0)
            norm_T(x2T, "x2T", 4, 4, gcols[:, l, 1, :], True, dbase=0)
            mm1(0)
            norm_stats(8, 4)
            norm_stats(12, 4)
            mm2(0, after_cb=lambda cb: norm_tile(x2T, "x2T", 8 + cb, 8 + 4 * (cb // 4), gcols[:, l, 1, :], True, dbase=8))
            mm1(1, after_fb=lambda fb: epi(0, fb))
            pbf = x2T[:, 0:4, :].rearrange("p a (b c) -> p (a b) c", c=256)
            pT = x2T[:, 4:8, :].rearrange("p (a b) c -> p a (b c)", a=2)
            def p_prep(cb):
                if cb == 4:
                    S.dma('pool', pbf, p_d[l].rearrange("(t p) k -> p t k", p=128), w=["pbf"], war=x2keys)
                if cb != 7:
                    return
                for t in range(16):
                    pt = bank(6 + t % 2).bitcast(BF16).rearrange("p (a b) -> p a b", b=128)
                    for kc in range(2):
                        S.op('pe', lambda: nc.tensor.transpose(out=pt[:, kc, :], in_=pbf[:, t, kc * 128:(kc + 1) * 128],
                                                               identity=ident[:, :]), r=["pbf", "ident"], w=[bkey(6 + t % 2)],
                             inc=(kc == 1))
                    S.op('act', lambda: nc.scalar.copy(out=pT[:, :, t * 128:(t + 1) * 128], in_=pt[:, 0:2, :]),
                         r=[bkey(6 + t % 2)], w=[("pT", t)], war=x2keys)

            mm2(1, after_cb=p_prep)
            load_w(wbA, "wbA", wg_d[l][:, 0:512], 512)
            load_w(wbB, "wbB", wg_d[l][:, 512:1024], 512)

            hT = hidT[:, 0:8, :]
            wpp = hidT[:, 16:18, :]
            sgm = hidT[:, 18:20, :].rearrange("p a c -> p (a c)").bitcast(F32).rearrange("p (a c) -> p a c", a=2)
            S.dma('pool', wpp, wp_d[l].rearrange("(kc p) c -> p kc c", p=128), w=["wpp"], war=["hidT"])

            def ple_half(hf, between=None):
                norm_T(hT, "hT", 8 * hf, 8, None, False, war=["hidT"])
                for t8 in range(8):
                    t = 8 * hf + t8
                    for nb_, (wb, wk) in enumerate([(wbA, "wbA"), (wbB, "wbB")]):
                        gb = (t % 2) * 2 + nb_
                        pb = 4 + (t % 2)
                        mm_acc(bank(gb), bkey(gb), [hT[:, dc, t8 * 128:(t8 + 1) * 128] for dc in range(8)],
                               [wb[:, dc, :] for dc in range(8)], WK(wk) + [("hT", t8)])
                        s2 = nb_
                        S.op('act', lambda: nc.scalar.activation(out=sgm[:, s2, :], in_=bank(gb), func=AF.Sigmoid),
                             r=[bkey(gb)], w=[("sgm", s2)], war=["hidT"])
                        mm_acc(bank(pb), bkey(pb), [pT[:, kc, t * 128:(t + 1) * 128] for kc in range(2)],
                               [wpp[:, kc, nb_ * 512:(nb_ + 1) * 512] for kc in range(2)], ["wpp", ("pT", t)])
                        S.op('dve', lambda: nc.vector.tensor_tensor(out=sgm[:, s2, :], in0=bank(pb), in1=sgm[:, s2, :], op=ALU.mult),
                             r=[bkey(pb), ("sgm", s2)], w=[("sgm", s2)])
                        ae = 'pool' if nb_ == 0 else 'dve'
                        S.op(ae, lambda: (nc.gpsimd if ae == 'pool' else nc.vector).tensor_tensor(
                            out=h[:, t, nb_ * 512:(nb_ + 1) * 512], in0=h[:, t, nb_ * 512:(nb_ + 1) * 512], in1=sgm[:, s2, :],
                            op=ALU.add), r=[("sgm", s2), ("h", t)], w=[("h", t)])
                    if between is not None:
                        between(t8)

            for t8 in range(8):
                resid_A(fbuf[:, t8, :], [("fbuf", t8 // 4)], 8 + t8)
            ple_half(0, between=lambda t8: resid_B(fbuf[:, t8, :], [("fbuf", t8 // 4)], 8 + t8, "gbc", 'pool'))
            ple_half(1)
            pre_evs = gla_prefetch(l + 1) if l + 1 < n_layers else []
            S.barrier(exclude=pre_evs)

    ov = out_d.rearrange("(t p) d -> p t d", p=128)
    evs = []
    for g in range(8):
        evs.append(S.dma('sp', ov[:, 2 * g:2 * g + 2, :], h[:, 2 * g:2 * g + 2, :],
                         r=[("h", t) for t in range(2 * g, 2 * g + 2)]))
    S._wait('sp', evs)
    return nc


_CONST = {}


def _consts():
    if not _CONST:
        kinds = _bias_tables()
        _CONST['idx'] = np.stack([k[0] for k in kinds], 0)
        _CONST['mask'] = np.ascontiguousarray(np.stack([k[1] for k in kinds], 1).reshape(128, 5 * 128)).astype(np.float32)
        _CONST['ident'] = np.eye(128, dtype=np.float32)
        kk = np.arange(128)
        _CONST['caus'] = (kk[:, None] <= kk[None, :]).astype(np.float32)
    return _CONST


def kernel(x, p, w_in, w_gla_a2, b_gla_a, gla_norm_g, w_out, rel_bias, pre_mix_g, post_mix_g, pre_mlp_g, post_mlp_g,
           w_mlp_in, w_mlp_out, w_ple_gate, w_ple_proj, _n_layers=DEPTH, _cores=8):
    f = lambda a: np.ascontiguousarray(np.asarray(a, dtype=np.float32))
    c = _consts()
    rb = f(rel_bias)
    bt = rb[c['idx']]
    bt = np.ascontiguousarray(bt.transpose(1, 0, 3, 2)).reshape(128, 5 * 8 * 128)
    shared = {
        "w_in": f(w_in), "w_gla_a2": f(w_gla_a2), "b_gla_a": f(b_gla_a), "gla_norm_g": f(gla_norm_g), "w_out": f(w_out),
        "pre_mix_g": f(pre_mix_g), "post_mix_g": f(post_mix_g), "pre_mlp_g": f(pre_mlp_g), "post_mlp_g": f(post_mlp_g),
        "w_mlp_in": f(w_mlp_in), "w_mlp_out": f(w_mlp_out), "w_ple_gate": f(w_ple_gate), "w_ple_proj": f(w_ple_proj),
        "bias_tab": bt, "mask_tab": c['mask'], "ident": c['ident'], "caus": c['caus'],
    }
    x = f(x)
    p = f(p)
    nc = build(_n_layers)
    in_maps = []
    for b in range(_cores):
        m = dict(shared)
        m["x"] = x[b]
        m["p"] = np.ascontiguousarray(p[:, b])
        in_maps.append(m)
    res = run_bass_kernel_spmd(nc, in_maps, core_ids=list(range(_cores)))
    return np.stack([np.asarray(r["out"], dtype=np.float32) for r in res.results], 0)
```

```python
import numpy as np
from contextlib import ExitStack
import concourse.bass as bass
import concourse.mybir as mybir
from concourse.bass_utils import run_bass_kernel_spmd

F32 = mybir.dt.float32
BF16 = mybir.dt.bfloat16
ALU = mybir.AluOpType
AF = mybir.ActivationFunctionType

SEQ = 2048
D = 1024
DEPTH = 2
IN_W = 3088
EPS = 1e-6
NDS = 12


class Sched:
    def __init__(self, nc):
        self.nc = nc
        self.E = {'pe': nc.tensor, 'act': nc.scalar, 'dve': nc.vector, 'pool': nc.gpsimd, 'sp': nc.sync}
        self.semh = {}
        for k in ['pe', 'act', 'dve', 'pool']:
            self.semh[k] = nc.alloc_semaphore("sem_" + k)
        for i in range(NDS):
            self.semh['dma%d' % i] = nc.alloc_semaphore("semdma%d" % i)
        self.cnt = {k: 0 for k in self.semh}
        self.known = {k: {} for k in self.E}
        self.lastw = {}
        self.readers = {}
        self.dnext = {'pool': 0, 'sp': 0}

    def _deps(self, r, w):
        evs = []
        for k in r:
            if k in self.lastw:
                evs.append(self.lastw[k])
        for k in w:
            if k in self.lastw:
                evs.append(self.lastw[k])
            rd = self.readers.get(k)
            if rd:
                evs.extend(rd.items())
        return evs

    def _wait(self, eng, evs):
        need = {}
        for (sn, v) in evs:
            if sn == 'pe' and eng == 'pe':
                continue
            if v > need.get(sn, 0):
                need[sn] = v
        kn = self.known[eng]
        for sn, v in need.items():
            if kn.get(sn, 0) >= v:
                continue
            self.E[eng].wait_ge(self.semh[sn], v)
            kn[sn] = v

    def _record(self, ev, r, w):
        for k in r:
            d = self.readers.setdefault(k, {})
            if ev[1] > d.get(ev[0], 0):
                d[ev[0]] = ev[1]
        for k in w:
            self.lastw[k] = ev
            self.readers[k] = {}

    def _war(self, keys):
        evs = []
        for k in keys:
            if k in self.lastw:
                evs.append(self.lastw[k])
            rd = self.readers.get(k)
            if rd:
                evs.extend(rd.items())
        return evs

    def op(self, eng, fn, r=(), w=(), inc=True, war=()):
        self._wait(eng, self._deps(r, w) + self._war(war))
        ins = fn()
        if inc:
            self.cnt[eng] += 1
            ins.then_inc(self.semh[eng], 1)
            ev = (eng, self.cnt[eng])
        else:
            ev = (eng, self.cnt[eng] + 1)
        self._record(ev, r, w)
        return ins

    def dma(self, q, out, in_, r=(), w=(), war=()):
        half = NDS // 2
        j = self.dnext[q]
        self.dnext[q] = (j + 1) % half
        i = j + (half if q == 'pool' else 0)
        sn = 'dma%d' % i
        evs = self._deps(r, w) + self._war(war)
        if self.cnt[sn] > 0:
            evs.append((sn, self.cnt[sn]))
        self._wait(q, evs)
        self.cnt[sn] += 16
        self.E[q].dma_start(out=out, in_=in_).then_inc(self.semh[sn], 16)
        ev = (sn, self.cnt[sn])
        self._record(ev, r, w)
        return ev

    def barrier(self, exclude=(), engines=None):
        ex = set(exclude)
        evs = [(k, v) for k, v in self.cnt.items() if v > 0 and (k, v) not in ex]
        for e in (engines if engines is not None else self.E):
            self._wait(e, evs)

    def fence(self, keys, exclude=()):
        ex = set(exclude)
        snap = {k: v for k, v in self.cnt.items() if v > 0 and (k, v) not in ex}
        for k in keys:
            d = self.readers.setdefault(k, {})
            for sn, v in snap.items():
                if v > d.get(sn, 0):
                    d[sn] = v


def _t5_bucket_np(dist):
    dist = np.asarray(dist, np.int64)
    d = np.maximum(dist, 1).astype(np.float32)
    large = 16 + (np.log(d / np.float32(16)) / np.float32(np.log(128.0)) * np.float32(16)).astype(np.int32)
    large = np.minimum(large, 31)
    return np.where(dist < 16, dist, large).astype(np.int64)


def _bias_tables():
    k = np.arange(128)[:, None]
    q = np.arange(128)[None, :]
    kinds = []
    for (dil, prev) in [(1, True), (1, False), (4, True), (4, False), (16, False)]:
        if prev:
            j = q + 128 - k
            valid = (k >= q)
        else:
            j = q - k
            valid = (k <= q)
        idx = _t5_bucket_np(np.maximum(j, 0) * dil)
        kinds.append((idx, valid.astype(np.float32)))
    return kinds


def build(n_layers=DEPTH):
    nc = bass.Bass("TRN2", target_bir_lowering=False)
    S = Sched(nc)

    def din(name, shape):
        return nc.dram_tensor(name, shape, F32, kind="ExternalInput").ap()

    x_d = din("x", [SEQ, D])
    p_d = din("p", [DEPTH, SEQ, 256])
    w_in_d = din("w_in", [DEPTH, D, IN_W])
    w_a2_d = din("w_gla_a2", [DEPTH, 16, 256])
    b_a_d = din("b_gla_a", [DEPTH, 256])
    gng_d = din("gla_norm_g", [DEPTH, 128])
    w_out_d = din("w_out", [DEPTH, D, D])
    pre_mix_d = din("pre_mix_g", [DEPTH, D])
    post_mix_d = din("post_mix_g", [DEPTH, D])
    pre_mlp_d = din("pre_mlp_g", [DEPTH, D])
    post_mlp_d = din("post_mlp_g", [DEPTH, D])
    w1_d = din("w_mlp_in", [DEPTH, D, 4 * D])
    w2_d = din("w_mlp_out", [DEPTH, 4 * D, D])
    wg_d = din("w_ple_gate", [DEPTH, D, D])
    wp_d = din("w_ple_proj", [DEPTH, 256, D])
    bias_d = din("bias_tab", [128, 5 * 8 * 128])
    mask_d = din("mask_tab", [128, 5 * 128])
    ident_d = din("ident", [128, 128])
    caus_d = din("caus", [128, 128])
    out_d = nc.dram_tensor("out", [SEQ, D], F32, kind="ExternalOutput").ap()

    sb = nc.alloc_sbuf_tensor
    h = sb("h", [128, 16, D], F32)
    ident = sb("ident_sb", [128, 128], BF16)
    caus = sb("caus_sb", [128, 128], F32)
    onesf = sb("onesf", [128, 128], F32)
    gcols = sb("gcols", [128, DEPTH, 2, 8], F32)
    negb = sb("negb", [128, DEPTH, 2], F32)
    gng = sb("gng", [128, DEPTH], F32)
    ss = sb("ss", [128, 16], F32)
    sq = sb("sq", [128, 16], F32)
    rstd = sb("rstd", [128, 16], F32)
    ssn = sb("ssn", [128, 16], F32)
    sqn = sb("sqn", [128, 16], F32)
    rstdn = sb("rstdn", [128, 16], F32)
    junk = sb("junk", [128, D], BF16)
    xsb = sb("xsb", [128, 2, D], BF16)
    xs = [xsb[:, j, :] for j in range(2)]
    tmp4 = sb("tmp4", [128, D], F32)
    gbc = sb("gbc", [128, 1, D], F32)
    wbA = sb("wbA", [128, 8, 512], BF16)
    wbB = sb("wbB", [128, 8, 512], BF16)
    KEYS = {"wbA": ["wbA"], "wbB": ["wbB", "wbB0", "wbB1"]}

    def WK(k):
        return KEYS[k]

    PP = [nc.alloc_psum_tensor("pp%d" % i, [128, 1024], F32) for i in range(4)]

    def bank(i):
        return PP[i // 2][:, (i % 2) * 512:(i % 2) * 512 + 512]

    def bkey(i):
        return ("bank", i)

    with nc.allow_non_contiguous_dma(reason="small constant loads"):
        S.dma('pool', ident[:, :], ident_d, w=["ident"])
        S.dma('sp', caus[:, :], caus_d, w=["caus"])
        for l in range(n_layers):
            S.dma('sp', gcols[:, l, 0, :], pre_mix_d[l].rearrange("(dc p) -> p dc", p=128), w=["gcols"])
            S.dma('sp', gcols[:, l, 1, :], pre_mlp_d[l].rearrange("(dc p) -> p dc", p=128), w=["gcols"])
            S.dma('sp', negb[:, l, :], b_a_d[l].rearrange("(c p) -> p c", p=128), w=["negb"])
            S.dma('sp', gng[:, l:l + 1], gng_d[l].rearrange("(p o) -> p o", o=1), w=["gng"])
    S.op('dve', lambda: nc.vector.memset(onesf[:, :], 1.0), w=["onesf"])
    S.op('dve', lambda: nc.vector.tensor_scalar(out=negb[:, :, :], in0=negb[:, :, :], scalar1=-1.0, scalar2=None,
                                                op0=ALU.mult), r=["negb"], w=["negb"])

    xv = x_d.rearrange("(t p) d -> p t d", p=128)
    for g in range(8):
        S.dma('sp', h[:, 2 * g:2 * g + 2, :], xv[:, 2 * g:2 * g + 2, :], w=[("h", t) for t in range(2 * g, 2 * g + 2)])

    def norm_T(dstT, dkey, t0, nt, gcol, do_norm, war=(), dbase=None):
        if dbase is None:
            dbase = t0
        if do_norm:
            for t in range(t0, t0 + nt):
                S.op('act', lambda: nc.scalar.activation(out=junk[:, :], in_=h[:, t, :], func=AF.Square,
                                                         accum_out=ssn[:, t:t + 1]), r=[("h", t)], w=[("ssn", t), "junk"])
            S.op('act', lambda: nc.scalar.activation(out=sqn[:, t0:t0 + nt], in_=ssn[:, t0:t0 + nt], func=AF.Sqrt,
                                                     scale=1.0 / D, bias=epsc[:, 0:1]),
                 r=[("ssn", t) for t in range(t0, t0 + nt)] + ["epsc"], w=[("sq", t0)])
            S.op('dve', lambda: nc.vector.reciprocal(out=rstdn[:, t0:t0 + nt], in_=sqn[:, t0:t0 + nt]), r=[("sq", t0)], w=[("rstdg", t0)])
        for t in range(t0, t0 + nt):
            j = t % 2
            if do_norm:
                S.op('act', lambda: nc.scalar.mul(out=xs[j], in_=h[:, t, :], mul=rstdn[:, t:t + 1]),
                     r=[("h", t), ("rstdg", t0)], w=[("xs", j)])
            else:
                S.op('act', lambda: nc.scalar.copy(out=xs[j], in_=h[:, t, :]), r=[("h", t)], w=[("xs", j)])
            pt = bank(6 + j).bitcast(BF16).rearrange("p (a b) -> p a b", b=128)
            for dc in range(8):
                S.op('pe', lambda: nc.tensor.transpose(out=pt[:, dc, :], in_=xs[j][:, dc * 128:(dc + 1) * 128],
                                                       identity=ident[:, :]),
                     r=[("xs", j), "ident"], w=[bkey(6 + j)], inc=(dc == 7))
            dst = dstT[:, :, (t - dbase) * 128:(t - dbase + 1) * 128]
            if gcol is not None:
                S.op('dve', lambda: nc.vector.tensor_tensor(out=dst, in0=pt, in1=gcol.unsqueeze(2).to_broadcast([128, 8, 128]),
                                                            op=ALU.mult), r=[bkey(6 + j), "gcols"], w=[(dkey, t - dbase)])
            else:
                S.op('dve', lambda: nc.vector.tensor_copy(out=dst, in_=pt), r=[bkey(6 + j)], w=[(dkey, t - dbase)], war=war)

    epsc = sb("epsc", [128, 2], F32)
    S.op('dve', lambda: nc.vector.memset(epsc[:, 0:1], EPS), w=["epsc"])
    S.op('dve', lambda: nc.vector.memset(epsc[:, 1:2], 1.0), w=["epsc"])

    def load_w(wb, wkey, src_ap, ncols, c0=0):
        kc = src_ap.shape[0] // 128
        return S.dma('pool', wb[:, 0:kc, c0:c0 + ncols], src_ap.rearrange("(kc p) c -> p kc c", p=128),
                     w=(WK(wkey) if wkey in KEYS else [wkey]))

    SPLIT = [False]

    def mm_acc(ps, pskey, lhs_list, rhs_list, rkeys):
        n = len(lhs_list)
        wide = SPLIT[0] and len(rhs_list[0].shape) == 2 and rhs_list[0].shape[-1] == 512
        for i in range(n):
            if wide:
                for ch in range(4):
                    cs = slice(ch * 128, (ch + 1) * 128)
                    last = (i == n - 1 and ch == 3)
                    S.op('pe', lambda: nc.tensor.matmul(ps[:, cs], lhsT=lhs_list[i], rhs=rhs_list[i][:, cs],
                                                        start=(i == 0 and ch == 0), stop=last, skip_group_check=True),
                         r=rkeys, w=[pskey], inc=last)
            else:
                S.op('pe', lambda: nc.tensor.matmul(ps, lhsT=lhs_list[i], rhs=rhs_list[i], start=(i == 0), stop=(i == n - 1)),
                     r=rkeys, w=[pskey], inc=(i == n - 1))

    def resid_epilogue(src_ap, srckeys, t, gkey, src_is_psum, add_eng='pool'):
        S.op('act', lambda: nc.scalar.activation(out=junk[:, :], in_=src_ap, func=AF.Square, accum_out=ss[:, t:t + 1]),
             r=srckeys, w=[("ss", t), "junk"])
        S.op('act', lambda: nc.scalar.activation(out=sq[:, t:t + 1], in_=ss[:, t:t + 1], func=AF.Sqrt, scale=1.0 / D,
                                                 bias=epsc[:, 0:1]), r=[("ss", t), "epsc"], w=[("sq", t)])
        S.op('dve', lambda: nc.vector.reciprocal(out=rstd[:, t:t + 1], in_=sq[:, t:t + 1]), r=[("sq", t)], w=[("rstd", t)])
        S.op('dve', lambda: nc.vector.scalar_tensor_tensor(out=tmp4[:, :], in0=src_ap, scalar=rstd[:, t:t + 1],
                                                           in1=gbc[:, 0, :], op0=ALU.mult, op1=ALU.mult),
             r=srckeys + [("rstd", t), gkey], w=["tmp4", ("sqr", 0), ("sqr", 1)])
        S.op(add_eng, lambda: (nc.gpsimd if add_eng == 'pool' else nc.vector).tensor_tensor(out=h[:, t, :], in0=h[:, t, :], in1=tmp4[:, :], op=ALU.add),
             r=["tmp4", ("sqr", 0), ("sqr", 1), ("h", t)], w=[("h", t)])

    def gla_prefetch(l):
        evs = [load_w(wbB, "wbB1", w_in_d[l][:, 1536:1664], 128, c0=256),
               load_w(wbB, "wbB0", w_in_d[l][:, 0:128], 128, c0=0),
               load_w(wbB, "wbB0", w_in_d[l][:, 256:384], 128, c0=128),
               load_w(wbA, "wbA", w_in_d[l][:, 512:1024], 512)]
        return evs

    pre_evs = gla_prefetch(0)
    for l in range(n_layers):
        w_in_l = w_in_d[l]
        with ExitStack() as es1:
            xnT = es1.enter_context(nc.sbuf_tensor("xnT_%d" % l, [128, 8, SEQ], BF16))
            mixT = es1.enter_context(nc.sbuf_tensor("mixT_%d" % l, [128, 8, SEQ], BF16))
            for g_ in range(4):
                norm_T(xnT, "xnT", 4 * g_, 4, gcols[:, l, 0, :], True, dbase=0)
            xkeys = [("xnT", t) for t in range(16)]

            def xk(tok0, ntok):
                return [("xnT", t) for t in range(tok0 // 128, (tok0 + ntok + 127) // 128)]

            with ExitStack() as es2:
                def A2(name, shape, dt):
                    return es2.enter_context(nc.sbuf_tensor("%s_%d" % (name, l), shape, dt))
                glrT = A2("glrT", [128, SEQ], BF16)
                wa2 = A2("wa2", [128, 256], BF16)
                qgz = A2("qgz", [128, 2, SEQ], BF16)
                kdT = A2("kdT", [128, SEQ], BF16)
                kg = A2("kg", [128, 16, 128], BF16)
                gv = A2("gv", [128, 16, 256], BF16)
                alast = A2("alast", [128, 16], F32)
                scr = A2("scr", [128, 2, 3, 512], F32)
                kgT = A2("kgT", [128, 2, 512], BF16)
                Sf = A2("Sf", [128, 256], F32)
                Sb = A2("Sb", [128, 3, 256], BF16)
                attm = A2("attm", [128, 2, 2, 128], BF16)

                def drive(gens, width):
                    gens = list(gens)
                    active = []
                    while gens or active:
                        while len(active) < width and gens:
                            active.append(gens.pop(0))
                        for g_ in list(active):
                            try:
                                next(g_)
                            except StopIteration:
                                active.remove(g_)

                S.op('dve', lambda: nc.vector.memset(wa2[:, :], 0.0), w=["wa2"])
                S.op('dve', lambda: nc.vector.memset(qgz[:, :, :], 0.0), w=["qgz"])
                S.dma('pool', wa2[0:16, :], w_a2_d[l], r=[], w=["wa2"])
                for tg in range(4):
                    b = tg % 2
                    mm_acc(bank(b), bkey(b), [wbB[:, dc, 256:384] for dc in range(8)],
                           [xnT[:, dc, tg * 512:(tg + 1) * 512] for dc in range(8)], ["wbB1"] + xk(tg * 512, 512))
                    S.op('act', lambda: nc.scalar.copy(out=glrT[:, tg * 512:(tg + 1) * 512], in_=bank(b)),
                         r=[bkey(b)], w=[("glrT", tg)])

                def prep_gen(pr, tg):
                    s_ = tg % 2
                    t1, cc, Eq = scr[:, s_, 0, :], scr[:, s_, 1, :], scr[:, s_, 2, :]
                    cc3 = cc.rearrange("p (a b) -> p a b", b=128)
                    k1, kc, kq, kk = ("t1", s_), ("cc", s_), ("Eq", s_), ("kgT", s_)
                    tsl = slice(tg * 512, (tg + 1) * 512)
                    bx, bq, bk_ = s_, 2 + s_, 4 + s_
                    S.op('pe', lambda: nc.tensor.matmul(bank(bx), lhsT=wa2[:, pr * 128:(pr + 1) * 128], rhs=glrT[:, tsl],
                                                        start=True, stop=True), r=["wa2", ("glrT", tg)], w=[bkey(bx)])
                    yield
                    mm_acc(bank(bq), bkey(bq), [wbB[:, dc, 0:128] for dc in range(8)],
                           [xnT[:, dc, tsl] for dc in range(8)], ["wbB0"] + xk(tg * 512, 512))
                    yield
                    mm_acc(bank(bk_), bkey(bk_), [wbB[:, dc, 128:256] for dc in range(8)],
                           [xnT[:, dc, tsl] for dc in range(8)], ["wbB0"] + xk(tg * 512, 512))
                    yield
                    S.op('act', lambda: nc.scalar.activation(out=t1, in_=bank(bx), func=AF.Exp, scale=-1.0,
                                                             bias=negb[:, l, pr:pr + 1]), r=[bkey(bx), "negb"], w=[k1])
                    yield
                    S.op('act', lambda: nc.scalar.activation(out=t1, in_=t1, func=AF.Ln, scale=1.0, bias=epsc[:, 1:2]),
                         r=[k1, "epsc"], w=[k1])
                    yield
                    for ch in range(4):
                        S.op('dve', lambda: nc.vector.tensor_tensor_scan(
                            out=cc3[:, ch, :], data0=onesf[:, :], data1=t1[:, ch * 128:(ch + 1) * 128], initial=0.0,
                            op0=ALU.mult, op1=ALU.add), r=[k1, "onesf"], w=[kc])
                    yield
                    S.op('act', lambda: nc.scalar.activation(out=Eq, in_=cc, func=AF.Exp, scale=-1.0 / 16.0), r=[kc], w=[kq])
                    yield
                    S.op('dve', lambda: nc.vector.tensor_tensor(
                        out=t1.rearrange("p (a b) -> p a b", b=128), in0=cc3, in1=cc3[:, :, 127:128].to_broadcast([128, 4, 128]),
                        op=ALU.subtract), r=[kc], w=[k1])
                    yield
                    S.op('act', lambda: nc.scalar.activation(out=t1, in_=t1, func=AF.Exp, scale=1.0 / 16.0), r=[k1], w=[k1])
                    yield
                    S.op('act', lambda: nc.scalar.activation(out=cc, in_=cc, func=AF.Exp, scale=1.0 / 16.0), r=[kc], w=[kc])
                    yield
                    S.op('act', lambda: nc.scalar.copy(out=alast[:, tg * 4:(tg + 1) * 4],
                                                       in_=Eq.rearrange("p (a b) -> p a b", b=128)[:, :, 127]), r=[kq], w=[("alast", tg)])
                    yield
                    for hh in range(2):
                        ps_ = slice(hh * 64, hh * 64 + 64)
                        S.op('dve', lambda: nc.vector.scalar_tensor_tensor(
                            out=qgz[ps_, hh, tsl], in0=bank(bq)[ps_, :], scalar=0.125, in1=Eq[ps_, :],
                            op0=ALU.mult, op1=ALU.mult), r=[bkey(bq), kq], w=[("qgz", tg)])
                    yield
                    S.op('dve', lambda: nc.vector.tensor_tensor(out=kdT[:, tsl], in0=bank(bk_), in1=cc, op=ALU.mult),
                         r=[bkey(bk_), kc], w=[("kdT", tg)])
                    yield
                    S.op('dve', lambda: nc.vector.tensor_tensor(out=kgT[:, s_, :], in0=bank(bk_), in1=t1, op=ALU.mult),
                         r=[bkey(bk_), k1], w=[kk])
                    yield
                    pt = bank(6).bitcast(BF16).rearrange("p (a b) -> p a b", b=128)
                    for ch in range(4):
                        S.op('pe', lambda: nc.tensor.transpose(out=pt[:, ch, :], in_=kgT[:, s_, ch * 128:(ch + 1) * 128],
                                                               identity=ident[:, :]), r=[kk, "ident"], w=[bkey(6)], inc=(ch == 3))
                    S.op('act', lambda: nc.scalar.copy(out=kg[:, tg * 4:(tg + 1) * 4, :], in_=pt[:, 0:4, :]),
                         r=[bkey(6)], w=[("kg", tg)])
                    yield

                def gv_gen(pr):
                    for t in range(16):
                        mm_acc(bank(7)[:, 0:256], bkey(7), [xnT[:, dc, t * 128:(t + 1) * 128] for dc in range(8)],
                               [wbA[:, dc, pr * 256:(pr + 1) * 256] for dc in range(8)], WK("wbA") + [("xnT", t)])
                        yield
                        S.op('act', lambda: nc.scalar.copy(out=gv[:, t, :], in_=bank(7)[:, 0:256]), r=[bkey(7)], w=[("gv", t)])
                        yield
                        yield

                def epi_ops(pr, tg, hh):
                    hd = 2 * pr + hh
                    osb, osq, sgt = scr[:, hh, 0, :], scr[:, hh, 1, :], scr[:, hh, 2, :]
                    ko, kq2, kg2 = ("t1", hh), ("cc", hh), ("Eq", hh)
                    tsl = slice(tg * 512, (tg + 1) * 512)
                    ops = []
                    ops.append(lambda: S.op('act', lambda: nc.scalar.copy(out=osb, in_=bank(4 + hh)), r=[bkey(4 + hh)], w=[ko]))
                    ops.append(lambda: S.op('act', lambda: nc.scalar.activation(out=osq, in_=osb, func=AF.Square), r=[ko], w=[kq2]))
                    ops.append(lambda: S.op('pe', lambda: nc.tensor.matmul(bank(hh), lhsT=onesf[:, :], rhs=osq, start=True, stop=True),
                                            r=["onesf", kq2], w=[bkey(hh)]))
                    ops.append(lambda: S.op('act', lambda: nc.scalar.activation(out=osq, in_=bank(hh), func=AF.Ln, scale=1.0 / 128.0,
                                                                             bias=epsc[:, 0:1]), r=[bkey(hh), "epsc"], w=[kq2]))
                    ops.append(lambda: S.op('act', lambda: nc.scalar.activation(out=osq, in_=osq, func=AF.Exp, scale=-0.5), r=[kq2], w=[kq2]))
                    ops.append(lambda: S.op('dve', lambda: nc.vector.tensor_tensor(out=osb, in0=osb, in1=osq, op=ALU.mult),
                                            r=[ko, kq2], w=[ko]))
                    ops.append(lambda: mm_acc(bank(hh), bkey(hh), [wbB[:, dc, (2 + hh) * 128:(3 + hh) * 128] for dc in range(8)],
                                              [xnT[:, dc, tsl] for dc in range(8)], ["wbB1"] + xk(tg * 512, 512)))
                    ops.append(lambda: S.op('act', lambda: nc.scalar.activation(out=sgt, in_=bank(hh), func=AF.Silu), r=[bkey(hh)], w=[kg2]))
                    ops.append(lambda: S.op('dve', lambda: nc.vector.scalar_tensor_tensor(
                        out=mixT[:, hd, tsl], in0=osb, scalar=gng[:, l:l + 1], in1=sgt, op0=ALU.mult, op1=ALU.mult),
                        r=[ko, kg2, "gng"], w=[("mixT", hd, tg)]))
                    return ops

                for pr in range(2):
                    for hh_ in range(2):
                        load_w(wbB, "wbB1", w_in_l[:, 1024 + (2 * pr + hh_) * 128:1024 + (2 * pr + hh_ + 1) * 128], 128, c0=256 + hh_ * 128)
                    drive([prep_gen(pr, 0), prep_gen(pr, 1), gv_gen(pr), prep_gen(pr, 2), prep_gen(pr, 3)], 3)
                    if pr == 0:
                        load_w(wbB, "wbB0", w_in_l[:, 128:256], 128, c0=0)
                        load_w(wbB, "wbB0", w_in_l[:, 384:512], 128, c0=128)

                    S.op('dve', lambda: nc.vector.memset(Sf[:, :], 0.0), w=["Sf"])
                    S.op('dve', lambda: nc.vector.memset(Sb[:, 0, :], 0.0), w=[("Sb", 0)])
                    pend = []

                    def drip(k=1):
                        for _ in range(k):
                            if pend:
                                pend.pop(0)()

                    def stage_A(n):
                        tg = n // 4
                        csl = slice(n * 128, (n + 1) * 128)
                        a2 = n % 2
                        ab = 2 + a2
                        kb = 6 + a2
                        for hh in range(2):
                            S.op('pe', lambda: nc.tensor.matmul(bank(ab)[:, hh * 128:(hh + 1) * 128], lhsT=kdT[:, csl],
                                                                rhs=qgz[:, hh, csl], start=True, stop=True, skip_group_check=True),
                                 r=[("kdT", tg), ("qgz", tg)], w=[bkey(ab)], inc=(hh == 1))
                        S.op('pe', lambda: nc.tensor.matmul(bank(kb)[:, 0:256], lhsT=kg[:, n, :], rhs=gv[:, n, :], start=True, stop=True),
                             r=[("kg", tg), ("gv", n)], w=[bkey(kb)])
                        S.op('dve', lambda: nc.vector.tensor_tensor(
                            out=attm[:, a2, :, :], in0=bank(ab)[:, 0:256].rearrange("p (a b) -> p a b", b=128),
                            in1=caus[:, :].unsqueeze(1).to_broadcast([128, 2, 128]), op=ALU.mult),
                            r=[bkey(ab), "caus"], w=[("attm", a2)])
                        S.op('dve', lambda: nc.vector.scalar_tensor_tensor(
                            out=Sf[:, :], in0=Sf[:, :], scalar=alast[:, n:n + 1], in1=bank(kb)[:, 0:256],
                            op0=ALU.mult, op1=ALU.add), r=["Sf", bkey(kb), ("alast", tg)], w=["Sf"])
                        S.op('act', lambda: nc.scalar.copy(out=Sb[:, (n + 1) % 3, :], in_=Sf[:, :]), r=["Sf"], w=[("Sb", (n + 1) % 3)])

                    def stage_B(n):
                        tg = n // 4
                        csl = slice(n * 128, (n + 1) * 128)
                        a2 = n % 2
                        for hh in range(2):
                            ob = bank(4 + hh)[:, (n % 4) * 128:(n % 4 + 1) * 128]
                            S.op('pe', lambda: nc.tensor.matmul(ob, lhsT=gv[:, n, hh * 128:(hh + 1) * 128], rhs=attm[:, a2, hh, :],
                                                                start=True, stop=False, skip_group_check=True),
                                 r=[("gv", n), ("attm", a2)], w=[bkey(4 + hh)], inc=False)
                            S.op('pe', lambda: nc.tensor.matmul(ob, lhsT=Sb[:, n % 3, hh * 128:(hh + 1) * 128], rhs=qgz[:, hh, csl],
                                                                start=False, stop=True, skip_group_check=True),
                                 r=[("Sb", n % 3), ("qgz", tg)], w=[bkey(4 + hh)])

                    stage_A(0)
                    for n in range(16):
                        if n + 1 < 16:
                            stage_A(n + 1)
                        drip(2)
                        stage_B(n)
                        drip(2)
                        if n % 4 == 3:
                            drip(len(pend))
                            e0, e1 = epi_ops(pr, n // 4, 0), epi_ops(pr, n // 4, 1)
                            e0[0]()
                            e1[0]()
                            for x0, x1 in zip(e0[1:], e1[1:]):
                                pend.append(x0)
                                pend.append(x1)
                    drip(len(pend))
                S.barrier(engines=('act', 'dve', 'sp'))

            with ExitStack() as es3:
                qz = es3.enter_context(nc.sbuf_tensor("qz_%d" % l, [128, 2, SEQ], BF16))
                kT = es3.enter_context(nc.sbuf_tensor("kT_%d" % l, [128, SEQ], BF16))
                Vn = es3.enter_context(nc.sbuf_tensor("Vn_%d" % l, [128, 3, 16, 192], BF16))
                ebf = es3.enter_context(nc.sbuf_tensor("ebf_%d" % l, [128, 3, 512], BF16))
                acc = es3.enter_context(nc.sbuf_tensor("acc_%d" % l, [128, 2, 512], F32))
                rs = es3.enter_context(nc.sbuf_tensor("rs_%d" % l, [128, 512], F32))
                biasM = es3.enter_context(nc.sbuf_tensor("biasM_%d" % l, [128, 5, 2, 128], BF16))
                madd = es3.enter_context(nc.sbuf_tensor("madd_%d" % l, [128, 5, 128], F32))
                et = es3.enter_context(nc.sbuf_tensor("et_%d" % l, [128, 2, 128], F32))
                S.dma('sp', madd[:, :, :], mask_d.rearrange("p (k q) -> p k q", q=128), w=["madd"])
                S.op('dve', lambda: nc.vector.tensor_scalar(out=madd[:, :, :], in0=madd[:, :, :], scalar1=-1.0, scalar2=30000.0,
                                                            op0=ALU.add, op1=ALU.mult), r=["madd"], w=["madd"])
                S.op('dve', lambda: nc.vector.memset(qz[:, :, :], 0.0), w=["qz"])
                S.op('dve', lambda: nc.vector.memset(Vn[:, :, :, 64:128], 1.0), w=["Vn"])
                for pr in range(4):
                    wb = wbA if pr % 2 == 0 else wbB
                    wk = "wbA" if pr % 2 == 0 else "wbB"
                    bv_ = bias_d.rearrange("p (k h q) -> p k h q", k=5, h=8)
                    for kd in range(5):
                        S.dma('sp', et[:, :, :], bv_[:, kd, 2 * pr:2 * pr + 2, :], w=["et"])
                        S.op('dve', lambda: nc.vector.tensor_tensor(
                            out=biasM[:, kd, :, :], in0=et[:, :, :], in1=madd[:, kd, :].unsqueeze(1).to_broadcast([128, 2, 128]),
                            op=ALU.add), r=["et", "madd"], w=["biasM"])
                    for i, c0 in enumerate([1552, 2064, 2576]):
                        S.dma('pool', wb[:, :, i * 128:(i + 1) * 128],
                              w_in_l[:, c0 + pr * 128:c0 + (pr + 1) * 128].rearrange("(kc p) c -> p kc c", p=128), w=WK(wk))
                    for tg in range(4):
                        tsl = slice(tg * 512, (tg + 1) * 512)
                        mm_acc(bank(5), bkey(5), [wb[:, dc, 0:128] for dc in range(8)], [xnT[:, dc, tsl] for dc in range(8)],
                               WK(wk) + xk(tg * 512, 512))
                        for hh in range(2):
                            ps_ = slice(hh * 64, hh * 64 + 64)
                            S.op('act', lambda: nc.scalar.mul(out=qz[ps_, hh, tsl], in_=bank(5)[ps_, :], mul=0.125),
                                 r=[bkey(5)], w=["qz"])
                        mm_acc(bank(6), bkey(6), [wb[:, dc, 128:256] for dc in range(8)], [xnT[:, dc, tsl] for dc in range(8)],
                               WK(wk) + xk(tg * 512, 512))
                        S.op('act', lambda: nc.scalar.copy(out=kT[:, tsl], in_=bank(6)), r=[bkey(6)], w=["kT"])
                    vT = xsb[:, :, :].rearrange("p a b -> p (a b)")
                    for tg in range(4):
                        tsl = slice(tg * 512, (tg + 1) * 512)
                        b = 5 + (tg % 2)
                        mm_acc(bank(b), bkey(b), [wb[:, dc, 256:384] for dc in range(8)], [xnT[:, dc, tsl] for dc in range(8)],
                               WK(wk) + xk(tg * 512, 512))
                        S.op('act', lambda: nc.scalar.copy(out=vT[:, tsl], in_=bank(b)), r=[bkey(b)], w=[("xs", tg // 2)])
                    for lay in range(3):
                        for g8 in range(2):
                            b = 5 + ((lay * 2 + g8) % 2)
                            ptb = bank(b).bitcast(BF16).rearrange("p (a b) -> p a b", b=128)
                            for tt in range(8):
                                ti = g8 * 8 + tt
                                if lay == 0:
                                    tok = slice(ti * 128, (ti + 1) * 128)
                                elif lay == 1:
                                    c_, r_ = ti // 4, ti % 4
                                    tok = slice(512 * c_ + r_, 512 * c_ + 512, 4)
                                else:
                                    tok = slice(ti, SEQ, 16)
                                S.op('pe', lambda: nc.tensor.transpose(out=ptb[:, tt, :], in_=vT[:, tok], identity=ident[:, :]),
                                     r=[("xs", 0), ("xs", 1), "ident"], w=[bkey(b)], inc=(tt == 7))
                            evac_eng = 'act' if (lay * 2 + g8) % 2 == 0 else 'dve'
                            dstv = Vn[:, lay, g8 * 8:(g8 + 1) * 8, :].rearrange("p t (h e) -> p t h e", e=64)[:, :, 0:3:2, :]
                            srcv = ptb.rearrange("p t (h e) -> p t h e", e=64)
                            if evac_eng == 'act':
                                S.op('act', lambda: nc.scalar.copy(out=dstv, in_=srcv), r=[bkey(b)], w=["Vn"])
                            else:
                                S.op('dve', lambda: nc.vector.tensor_copy(out=dstv, in_=srcv), r=[bkey(b)], w=["Vn"])
                    if pr == 3:
                        load_w(wbA, "wbA", w_out_d[l][:, 0:512], 512)
                        load_w(wbB, "wbB", w_out_d[l][:, 512:1024], 512)
                    stages = []
                    for hh in range(2):
                        for c in range(4):
                            g = (hh * 4 + c)
                            u1b, u4b, u16b = (2, 3, 4) if g % 2 == 0 else (5, 6, 4)
                            cur1, prev1, cur4, prev4, d16 = [], [], [], [], []
                            for n2 in range(4):
                                nb = 4 * c + n2
                                qs = slice(nb * 128, (nb + 1) * 128)
                                cur1.append((qs, qs, (0, nb), n2 * 128, 128))
                                if nb > 0:
                                    prev1.append((slice((nb - 1) * 128, nb * 128), qs, (0, nb - 1), n2 * 128, 128))
                            for r_ in range(4):
                                qs = slice(512 * c + r_, 512 * c + 512, 4)
                                cur4.append((qs, qs, (1, 4 * c + r_), r_ * 128, 128))
                                if c > 0:
                                    prev4.append((slice(512 * (c - 1) + r_, 512 * c, 4), qs, (1, 4 * (c - 1) + r_), r_ * 128, 128))
                            for r16 in range(16):
                                d16.append((slice(r16, SEQ, 16), slice(512 * c + r16, 512 * c + 512, 16), (2, r16), r16 * 32, 32))
                            grp = [(1, u1b, cur1), (0, u1b, prev1), (3, u4b, cur4), (2, u4b, prev4), (4, u16b, d16)]
                            grp = [x for x in grp if x[2]]
                            started = set()
                            for gi, (kind, ub, items) in enumerate(grp):
                                first = ub not in started
                                started.add(ub)
                                stages.append(dict(hh=hh, c=c, kind=kind, ub=ub, items=items, first=first,
                                                   last=(gi == len(grp) - 1), ubs=(u1b, u4b, u16b)))

                    def emit_L(i, st):
                        lb = (0, 1, 7)[i % 3]
                        es_ = i % 3
                        hh, c, kind, items = st['hh'], st['c'], st['kind'], st['items']
                        lo = items[0][3]
                        if kind == 4:
                            bv = biasM[:, 4, hh, 32 * c:32 * c + 32].unsqueeze(1).to_broadcast([128, 16, 32])
                        else:
                            bv = biasM[:, kind, hh, :].unsqueeze(1).to_broadcast([128, (512 - lo) // 128, 128])
                        S.op('pe', lambda: nc.tensor.matmul(bank(lb)[:, lo:512], lhsT=ident[:, :], rhs=bv, start=True, stop=False,
                                                            skip_group_check=True), r=["ident", "biasM"], w=[bkey(lb)], inc=False)
                        for (ks, qs, vt, c0, ncol) in items:
                            S.op('pe', lambda: nc.tensor.matmul(bank(lb)[:, c0:c0 + ncol], lhsT=kT[:, ks], rhs=qz[:, hh, qs],
                                                                start=False, stop=(c0 + ncol == 512), skip_group_check=True),
                                 r=["kT", "qz"], w=[bkey(lb)], inc=(c0 + ncol == 512))
                        S.op('act', lambda: nc.scalar.activation(out=ebf[:, es_, lo:512], in_=bank(lb)[:, lo:512], func=AF.Exp),
                             r=[bkey(lb)], w=[("ebf", es_)])

                    def emit_PV(i, st):
                        lb = (0, 1, 7)[i % 3]
                        es_ = i % 3
                        hh, c, ub, items = st['hh'], st['c'], st['ub'], st['items']
                        for ii, (ks, qs, vt, c0, ncol) in enumerate(items):
                            S.op('pe', lambda: nc.tensor.matmul(bank(ub)[:, c0:c0 + ncol], lhsT=Vn[:, vt[0], vt[1], 64 * hh:64 * hh + 128],
                                                                rhs=ebf[:, es_, c0:c0 + ncol], start=(st['first'] and ii == 0), stop=False,
                                                                skip_group_check=True),
                                 r=["Vn", ("ebf", es_)], w=[bkey(ub)], inc=(ii == len(items) - 1))
                        if st['last']:
                            u1_, u4_, u16_ = st['ubs']
                            a_ = acc[:, (hh * 4 + c) % 2, :]
                            ak = ("acc", (hh * 4 + c) % 2)
                            S.op('dve', lambda: nc.vector.tensor_copy(out=a_, in_=bank(u1_)), r=[bkey(u1_)], w=[ak])
                            S.op('dve', lambda: nc.vector.tensor_tensor(
                                out=a_.rearrange("p (i r) -> p i r", r=4), in0=bank(u4_).rearrange("p (r i) -> p i r", r=4),
                                in1=a_.rearrange("p (i r) -> p i r", r=4), op=ALU.add), r=[bkey(u4_), ak], w=[ak])
                            S.op('dve', lambda: nc.vector.tensor_tensor(
                                out=a_.rearrange("p (i r) -> p i r", r=16), in0=bank(u16_).rearrange("p (r i) -> p i r", r=16),
                                in1=a_.rearrange("p (i r) -> p i r", r=16), op=ALU.add), r=[bkey(u16_), ak], w=[ak])
                            up = slice(64 * hh, 64 * hh + 64)
                            sp_ = slice(64 * (1 - hh), 64 * (1 - hh) + 64)

                            def fin():
                                S.op('act', lambda: nc.scalar.activation(out=rs[up, :], in_=a_[sp_, :], func=AF.Ln), r=[ak], w=["rs"])
                                S.op('act', lambda: nc.scalar.activation(out=rs[up, :], in_=rs[up, :], func=AF.Exp, scale=-1.0),
                                     r=["rs"], w=["rs"])
                                S.op('dve', lambda: nc.vector.tensor_tensor(
                                    out=mixT[up, 4 + pr, c * 512:(c + 1) * 512], in0=a_[up, :], in1=rs[up, :],
                                    op=ALU.mult), r=[ak, "rs"], w=[("mixT", 4 + pr, c, hh)])
                            pend_n.append([i + 4, fin])

                    pend_n = []
                    for i, st in enumerate(stages):
                        emit_L(i, st)
                        while pend_n and pend_n[0][0] <= i:
                            pend_n.pop(0)[1]()
                        if i >= 2:
                            emit_PV(i - 2, stages[i - 2])
                    emit_PV(len(stages) - 2, stages[-2])
                    emit_PV(len(stages) - 1, stages[-1])
                    while pend_n:
                        pend_n.pop(0)[1]()

            with nc.allow_non_contiguous_dma(reason="gain broadcast"):
                S.dma('sp', gbc[:, :, :], post_mix_d[l:l + 1, :].partition_broadcast(128), w=["gbc"])
            for t in range(16):
                pp = PP[t % 2]
                pks = [bkey(2 * (t % 2)), bkey(2 * (t % 2) + 1)]
                mkeys = [("mixT", hd_, t // 4) for hd_ in range(4)] + [("mixT", 4 + p_, t // 4, h_) for p_ in range(4) for h_ in range(2)]
                for nb_, (wb, wk) in enumerate([(wbA, "wbA"), (wbB, "wbB")]):
                    for kc in range(8):
                        S.op('pe', lambda: nc.tensor.matmul(pp[:, nb_ * 512:(nb_ + 1) * 512], lhsT=mixT[:, kc, t * 128:(t + 1) * 128],
                                                            rhs=wb[:, kc, :], start=(kc == 0), stop=(kc == 7)),
                             r=WK(wk) + mkeys, w=pks, inc=(kc == 7))
                resid_epilogue(pp[:, :], pks, t, "gbc", True)
            pre_evs = [load_w(wbA, "wbA", w1_d[l][:, 0:512], 512), load_w(wbB, "wbB", w1_d[l][:, 512:1024], 512)]
            mlp_fence = True

        with ExitStack() as es4:
            hidT = es4.enter_context(nc.sbuf_tensor("hidT_%d" % l, [128, 32, 1024], BF16))
            fbuf = es4.enter_context(nc.sbuf_tensor("fbuf_%d" % l, [128, 8, D], F32))
            x2T = es4.enter_context(nc.sbuf_tensor("x2T_%d" % l, [128, 8, 1024], BF16))
            sqr = tmp4[:, :].rearrange("p (a b) -> p a b", b=512)
            wbs = [(wbA, "wbA"), (wbB, "wbB")]
            with nc.allow_non_contiguous_dma(reason="gain broadcast"):
                S.dma('sp', gbc[:, :, :], post_mlp_d[l:l + 1, :].partition_broadcast(128), w=["gbc"])
            x2keys = [("x2T", t) for t in range(8)]
            S.fence(x2keys + ["hidT", ("fbuf", 0), ("fbuf", 1)], exclude=pre_evs)
            wi = [0]

            def mm1(hf, after_fb=None):
                for fb in range(8):
                    wb, wk = wbs[wi[0] % 2]
                    wi[0] += 1
                    if not (hf == 0 and fb < 2):
                        load_w(wb, wk, w1_d[l][:, fb * 512:(fb + 1) * 512], 512)
                    for fc in range(4):
                        for tg in range(2):
                            b = (fc * 2 + tg) % 4
                            mm_acc(bank(b), bkey(b), [wb[:, dc, fc * 128:(fc + 1) * 128] for dc in range(8)],
                                   [x2T[:, dc, tg * 512:(tg + 1) * 512] for dc in range(8)], WK(wk) + x2keys[4 * tg:4 * tg + 4])
                            s2 = b % 2
                            S.op('act', lambda: nc.scalar.activation(out=sqr[:, s2, :], in_=bank(b), func=AF.Square),
                                 r=[bkey(b)], w=[("sqr", s2)])
                            S.op('dve', lambda: nc.vector.scalar_tensor_tensor(
                                out=hidT[:, fb * 4 + fc, tg * 512:(tg + 1) * 512], in0=bank(b), scalar=0.0, in1=sqr[:, s2, :],
                                op0=ALU.is_gt, op1=ALU.mult), r=[bkey(b), ("sqr", s2)], w=["hidT"])
                    if after_fb is not None:
                        after_fb(fb)

            def mm2(hf, after_cb=None):
                for cb in range(8):
                    if after_cb is not None:
                        after_cb(cb)
                    wb, wk = wbs[wi[0] % 2]
                    wi[0] += 1
                    wv = wb[:, :, :].rearrange("p a (b c) -> p (a b) c", c=128)
                    S.dma('pool', wv, w2_d[l][:, cb * 128:(cb + 1) * 128].rearrange("(kc p) c -> p kc c", p=128), w=WK(wk))
                    for g4 in range(2):
                        b = 4 + (cb * 2 + g4) % 2
                        for tt in range(4):
                            t = g4 * 4 + tt
                            mm_acc(bank(b)[:, tt * 128:(tt + 1) * 128], bkey(b),
                                   [hidT[:, fc, t * 128:(t + 1) * 128] for fc in range(32)], [wv[:, fc, :] for fc in range(32)],
                                   WK(wk) + ["hidT"])
                        S.op('act', lambda: nc.scalar.copy(out=fbuf[:, g4 * 4:(g4 + 1) * 4, cb * 128:(cb + 1) * 128],
                                                           in_=bank(b).rearrange("p (t c) -> p t c", c=128)),
                             r=[bkey(b)], w=[("fbuf", g4)])

            def epi(hf, t8):
                resid_epilogue(fbuf[:, t8, :], [("fbuf", t8 // 4)], 8 * hf + t8, "gbc", False, add_eng=('dve' if hf == 0 else 'pool'))

            norm_T(x2T, "x2T", 0, 4, gcols[:, l, 1, :], True, dbase=0)
            norm_T(x2T, "x2T", 4, 4, gcols[:, l, 1, :], True, dbase=0)
            mm1(0)
            norm_T(x2T, "x2T", 8, 4, gcols[:, l, 1, :], True, dbase=8)
            norm_T(x2T, "x2T", 12, 4, gcols[:, l, 1, :], True, dbase=8)
            mm2(0)
            mm1(1, after_fb=lambda fb: epi(0, fb))
            pbf = x2T[:, 0:4, :].rearrange("p a (b c) -> p (a b) c", c=256)
            pT = x2T[:, 4:8, :].rearrange("p (a b) c -> p a (b c)", a=2)
            def p_prep(cb):
                if cb == 4:
                    S.dma('pool', pbf, p_d[l].rearrange("(t p) k -> p t k", p=128), w=["pbf"], war=x2keys)
                if cb != 7:
                    return
                for t in range(16):
                    pt = bank(6 + t % 2).bitcast(BF16).rearrange("p (a b) -> p a b", b=128)
                    for kc in range(2):
                        S.op('pe', lambda: nc.tensor.transpose(out=pt[:, kc, :], in_=pbf[:, t, kc * 128:(kc + 1) * 128],
                                                               identity=ident[:, :]), r=["pbf", "ident"], w=[bkey(6 + t % 2)],
                             inc=(kc == 1))
                    S.op('act', lambda: nc.scalar.copy(out=pT[:, :, t * 128:(t + 1) * 128], in_=pt[:, 0:2, :]),
                         r=[bkey(6 + t % 2)], w=[("pT", t)], war=x2keys)

            mm2(1, after_cb=p_prep)
            load_w(wbA, "wbA", wg_d[l][:, 0:512], 512)
            load_w(wbB, "wbB", wg_d[l][:, 512:1024], 512)

            hT = hidT[:, 0:8, :]
            wpp = hidT[:, 16:18, :]
            sgm = hidT[:, 18:20, :].rearrange("p a c -> p (a c)").bitcast(F32).rearrange("p (a c) -> p a c", a=2)
            S.dma('pool', wpp, wp_d[l].rearrange("(kc p) c -> p kc c", p=128), w=["wpp"], war=["hidT"])

            def ple_half(hf, between=None):
                norm_T(hT, "hT", 8 * hf, 8, None, False, war=["hidT"])
                for t8 in range(8):
                    t = 8 * hf + t8
                    for nb_, (wb, wk) in enumerate([(wbA, "wbA"), (wbB, "wbB")]):
                        gb = (t % 2) * 2 + nb_
                        pb = 4 + (t % 2)
                        mm_acc(bank(gb), bkey(gb), [hT[:, dc, t8 * 128:(t8 + 1) * 128] for dc in range(8)],
                               [wb[:, dc, :] for dc in range(8)], WK(wk) + [("hT", t8)])
                        s2 = nb_
                        S.op('act', lambda: nc.scalar.activation(out=sgm[:, s2, :], in_=bank(gb), func=AF.Sigmoid),
                             r=[bkey(gb)], w=[("sgm", s2)], war=["hidT"])
                        mm_acc(bank(pb), bkey(pb), [pT[:, kc, t * 128:(t + 1) * 128] for kc in range(2)],
                               [wpp[:, kc, nb_ * 512:(nb_ + 1) * 512] for kc in range(2)], ["wpp", ("pT", t)])
                        S.op('dve', lambda: nc.vector.tensor_tensor(out=sgm[:, s2, :], in0=bank(pb), in1=sgm[:, s2, :], op=ALU.mult),
                             r=[bkey(pb), ("sgm", s2)], w=[("sgm", s2)])
                        ae = 'pool' if nb_ == 0 else 'dve'
                        S.op(ae, lambda: (nc.gpsimd if ae == 'pool' else nc.vector).tensor_tensor(
                            out=h[:, t, nb_ * 512:(nb_ + 1) * 512], in0=h[:, t, nb_ * 512:(nb_ + 1) * 512], in1=sgm[:, s2, :],
                            op=ALU.add), r=[("sgm", s2), ("h", t)], w=[("h", t)])
                    if between is not None:
                        between(t8)

            ple_half(0, between=lambda t8: epi(1, t8))
            ple_half(1)
            pre_evs = gla_prefetch(l + 1) if l + 1 < n_layers else []
            S.barrier(exclude=pre_evs)

    ov = out_d.rearrange("(t p) d -> p t d", p=128)
    evs = []
    for g in range(8):
        evs.append(S.dma('sp', ov[:, 2 * g:2 * g + 2, :], h[:, 2 * g:2 * g + 2, :],
                         r=[("h", t) for t in range(2 * g, 2 * g + 2)]))
    S._wait('sp', evs)
    return nc


_CONST = {}


def _consts():
    if not _CONST:
        kinds = _bias_tables()
        _CONST['idx'] = np.stack([k[0] for k in kinds], 0)
        _CONST['mask'] = np.ascontiguousarray(np.stack([k[1] for k in kinds], 1).reshape(128, 5 * 128)).astype(np.float32)
        _CONST['ident'] = np.eye(128, dtype=np.float32)
        kk = np.arange(128)
        _CONST['caus'] = (kk[:, None] <= kk[None, :]).astype(np.float32)
    return _CONST


def kernel(x, p, w_in, w_gla_a2, b_gla_a, gla_norm_g, w_out, rel_bias, pre_mix_g, post_mix_g, pre_mlp_g, post_mlp_g,
           w_mlp_in, w_mlp_out, w_ple_gate, w_ple_proj, _n_layers=DEPTH, _cores=8):
    f = lambda a: np.ascontiguousarray(np.asarray(a, dtype=np.float32))
    c = _consts()
    rb = f(rel_bias)
    bt = rb[c['idx']]
    bt = np.ascontiguousarray(bt.transpose(1, 0, 3, 2)).reshape(128, 5 * 8 * 128)
    shared = {
        "w_in": f(w_in), "w_gla_a2": f(w_gla_a2), "b_gla_a": f(b_gla_a), "gla_norm_g": f(gla_norm_g), "w_out": f(w_out),
        "pre_mix_g": f(pre_mix_g), "post_mix_g": f(post_mix_g), "pre_mlp_g": f(pre_mlp_g), "post_mlp_g": f(post_mlp_g),
        "w_mlp_in": f(w_mlp_in), "w_mlp_out": f(w_mlp_out), "w_ple_gate": f(w_ple_gate), "w_ple_proj": f(w_ple_proj),
        "bias_tab": bt, "mask_tab": c['mask'], "ident": c['ident'], "caus": c['caus'],
    }
    x = f(x)
    p = f(p)
    nc = build(_n_layers)
    in_maps = []
    for b in range(_cores):
        m = dict(shared)
        m["x"] = x[b]
        m["p"] = np.ascontiguousarray(p[:, b])
        in_maps.append(m)
    res = run_bass_kernel_spmd(nc, in_maps, core_ids=list(range(_cores)))
    return np.stack([np.asarray(r["out"], dtype=np.float32) for r in res.results], 0)
```

```python
import numpy as np
from contextlib import ExitStack
import concourse.bass as bass
import concourse.mybir as mybir
from concourse.bass_utils import run_bass_kernel_spmd

F32 = mybir.dt.float32
BF16 = mybir.dt.bfloat16
ALU = mybir.AluOpType
AF = mybir.ActivationFunctionType

SEQ = 2048
D = 1024
DEPTH = 2
IN_W = 3088
EPS = 1e-6
NDS = 12


class Sched:
    def __init__(self, nc):
        self.nc = nc
        self.E = {'pe': nc.tensor, 'act': nc.scalar, 'dve': nc.vector, 'pool': nc.gpsimd, 'sp': nc.sync}
        self.semh = {}
        for k in ['pe', 'act', 'dve', 'pool']:
            self.semh[k] = nc.alloc_semaphore("sem_" + k)
        for i in range(NDS):
            self.semh['dma%d' % i] = nc.alloc_semaphore("semdma%d" % i)
        self.cnt = {k: 0 for k in self.semh}
        self.known = {k: {} for k in self.E}
        self.lastw = {}
        self.readers = {}
        self.dnext = {'pool': 0, 'sp': 0}

    def _deps(self, r, w):
        evs = []
        for k in r:
            if k in self.lastw:
                evs.append(self.lastw[k])
        for k in w:
            if k in self.lastw:
                evs.append(self.lastw[k])
            rd = self.readers.get(k)
            if rd:
                evs.extend(rd.items())
        return evs

    def _wait(self, eng, evs):
        need = {}
        for (sn, v) in evs:
            if sn == 'pe' and eng == 'pe':
                continue
            if v > need.get(sn, 0):
                need[sn] = v
        kn = self.known[eng]
        for sn, v in need.items():
            if kn.get(sn, 0) >= v:
                continue
            self.E[eng].wait_ge(self.semh[sn], v)
            kn[sn] = v

    def _record(self, ev, r, w):
        for k in r:
            d = self.readers.setdefault(k, {})
            if ev[1] > d.get(ev[0], 0):
                d[ev[0]] = ev[1]
        for k in w:
            self.lastw[k] = ev
            self.readers[k] = {}

    def _war(self, keys):
        evs = []
        for k in keys:
            if k in self.lastw:
                evs.append(self.lastw[k])
            rd = self.readers.get(k)
            if rd:
                evs.extend(rd.items())
        return evs

    def op(self, eng, fn, r=(), w=(), inc=True, war=()):
        self._wait(eng, self._deps(r, w) + self._war(war))
        ins = fn()
        if inc:
            self.cnt[eng] += 1
            ins.then_inc(self.semh[eng], 1)
            ev = (eng, self.cnt[eng])
        else:
            ev = (eng, self.cnt[eng] + 1)
        self._record(ev, r, w)
        return ins

    def dma(self, q, out, in_, r=(), w=(), war=()):
        half = NDS // 2
        j = self.dnext[q]
        self.dnext[q] = (j + 1) % half
        i = j + (half if q == 'pool' else 0)
        sn = 'dma%d' % i
        evs = self._deps(r, w) + self._war(war)
        if self.cnt[sn] > 0:
            evs.append((sn, self.cnt[sn]))
        self._wait(q, evs)
        self.cnt[sn] += 16
        self.E[q].dma_start(out=out, in_=in_).then_inc(self.semh[sn], 16)
        ev = (sn, self.cnt[sn])
        self._record(ev, r, w)
        return ev

    def barrier(self, exclude=(), engines=None):
        ex = set(exclude)
        evs = [(k, v) for k, v in self.cnt.items() if v > 0 and (k, v) not in ex]
        for e in (engines if engines is not None else self.E):
            self._wait(e, evs)

    def fence(self, keys, exclude=()):
        ex = set(exclude)
        snap = {k: v for k, v in self.cnt.items() if v > 0 and (k, v) not in ex}
        for k in keys:
            d = self.readers.setdefault(k, {})
            for sn, v in snap.items():
                if v > d.get(sn, 0):
                    d[sn] = v


def _t5_bucket_np(dist):
    dist = np.asarray(dist, np.int64)
    d = np.maximum(dist, 1).astype(np.float32)
    large = 16 + (np.log(d / np.float32(16)) / np.float32(np.log(128.0)) * np.float32(16)).astype(np.int32)
    large = np.minimum(large, 31)
    return np.where(dist < 16, dist, large).astype(np.int64)


def _bias_tables():
    k = np.arange(128)[:, None]
    q = np.arange(128)[None, :]
    kinds = []
    for (dil, prev) in [(1, True), (1, False), (4, True), (4, False), (16, False)]:
        if prev:
            j = q + 128 - k
            valid = (k >= q)
        else:
            j = q - k
            valid = (k <= q)
        idx = _t5_bucket_np(np.maximum(j, 0) * dil)
        kinds.append((idx, valid.astype(np.float32)))
    return kinds


def build(n_layers=DEPTH):
    nc = bass.Bass("TRN2", target_bir_lowering=False)
    S = Sched(nc)

    def din(name, shape):
        return nc.dram_tensor(name, shape, F32, kind="ExternalInput").ap()

    x_d = din("x", [SEQ, D])
    p_d = din("p", [DEPTH, SEQ, 256])
    w_in_d = din("w_in", [DEPTH, D, IN_W])
    w_a2_d = din("w_gla_a2", [DEPTH, 16, 256])
    b_a_d = din("b_gla_a", [DEPTH, 256])
    gng_d = din("gla_norm_g", [DEPTH, 128])
    w_out_d = din("w_out", [DEPTH, D, D])
    pre_mix_d = din("pre_mix_g", [DEPTH, D])
    post_mix_d = din("post_mix_g", [DEPTH, D])
    pre_mlp_d = din("pre_mlp_g", [DEPTH, D])
    post_mlp_d = din("post_mlp_g", [DEPTH, D])
    w1_d = din("w_mlp_in", [DEPTH, D, 4 * D])
    w2_d = din("w_mlp_out", [DEPTH, 4 * D, D])
    wg_d = din("w_ple_gate", [DEPTH, D, D])
    wp_d = din("w_ple_proj", [DEPTH, 256, D])
    bias_d = din("bias_tab", [128, 5 * 8 * 128])
    mask_d = din("mask_tab", [128, 5 * 128])
    ident_d = din("ident", [128, 128])
    caus_d = din("caus", [128, 128])
    out_d = nc.dram_tensor("out", [SEQ, D], F32, kind="ExternalOutput").ap()

    sb = nc.alloc_sbuf_tensor
    h = sb("h", [128, 16, D], F32)
    ident = sb("ident_sb", [128, 128], BF16)
    caus = sb("caus_sb", [128, 128], F32)
    onesf = sb("onesf", [128, 128], F32)
    gcols = sb("gcols", [128, DEPTH, 2, 8], F32)
    negb = sb("negb", [128, DEPTH, 2], F32)
    gng = sb("gng", [128, DEPTH], F32)
    ss = sb("ss", [128, 16], F32)
    sq = sb("sq", [128, 16], F32)
    rstd = sb("rstd", [128, 16], F32)
    ssn = sb("ssn", [128, 16], F32)
    sqn = sb("sqn", [128, 16], F32)
    rstdn = sb("rstdn", [128, 16], F32)
    junk = sb("junk", [128, D], BF16)
    xsb = sb("xsb", [128, 2, D], BF16)
    xs = [xsb[:, j, :] for j in range(2)]
    tmp4 = sb("tmp4", [128, D], F32)
    gbc = sb("gbc", [128, 1, D], F32)
    wbA = sb("wbA", [128, 8, 512], BF16)
    wbB = sb("wbB", [128, 8, 512], BF16)
    KEYS = {"wbA": ["wbA"], "wbB": ["wbB", "wbB0", "wbB1"]}

    def WK(k):
        return KEYS[k]

    PP = [nc.alloc_psum_tensor("pp%d" % i, [128, 1024], F32) for i in range(4)]

    def bank(i):
        return PP[i // 2][:, (i % 2) * 512:(i % 2) * 512 + 512]

    def bkey(i):
        return ("bank", i)

    with nc.allow_non_contiguous_dma(reason="small constant loads"):
        S.dma('pool', ident[:, :], ident_d, w=["ident"])
        S.dma('sp', caus[:, :], caus_d, w=["caus"])
        for l in range(n_layers):
            S.dma('sp', gcols[:, l, 0, :], pre_mix_d[l].rearrange("(dc p) -> p dc", p=128), w=["gcols"])
            S.dma('sp', gcols[:, l, 1, :], pre_mlp_d[l].rearrange("(dc p) -> p dc", p=128), w=["gcols"])
            S.dma('sp', negb[:, l, :], b_a_d[l].rearrange("(c p) -> p c", p=128), w=["negb"])
            S.dma('sp', gng[:, l:l + 1], gng_d[l].rearrange("(p o) -> p o", o=1), w=["gng"])
    S.op('dve', lambda: nc.vector.memset(onesf[:, :], 1.0), w=["onesf"])
    S.op('dve', lambda: nc.vector.tensor_scalar(out=negb[:, :, :], in0=negb[:, :, :], scalar1=-1.0, scalar2=None,
                                                op0=ALU.mult), r=["negb"], w=["negb"])

    xv = x_d.rearrange("(t p) d -> p t d", p=128)
    for g in range(8):
        S.dma('sp', h[:, 2 * g:2 * g + 2, :], xv[:, 2 * g:2 * g + 2, :], w=[("h", t) for t in range(2 * g, 2 * g + 2)])

    def norm_T(dstT, dkey, t0, nt, gcol, do_norm, war=(), dbase=None):
        if dbase is None:
            dbase = t0
        if do_norm:
            for t in range(t0, t0 + nt):
                S.op('act', lambda: nc.scalar.activation(out=junk[:, :], in_=h[:, t, :], func=AF.Square,
                                                         accum_out=ssn[:, t:t + 1]), r=[("h", t)], w=[("ssn", t), "junk"])
            S.op('act', lambda: nc.scalar.activation(out=sqn[:, t0:t0 + nt], in_=ssn[:, t0:t0 + nt], func=AF.Sqrt,
                                                     scale=1.0 / D, bias=epsc[:, 0:1]),
                 r=[("ssn", t) for t in range(t0, t0 + nt)] + ["epsc"], w=[("sq", t0)])
            S.op('dve', lambda: nc.vector.reciprocal(out=rstdn[:, t0:t0 + nt], in_=sqn[:, t0:t0 + nt]), r=[("sq", t0)], w=[("rstdg", t0)])
        for t in range(t0, t0 + nt):
            j = t % 2
            if do_norm:
                S.op('act', lambda: nc.scalar.mul(out=xs[j], in_=h[:, t, :], mul=rstdn[:, t:t + 1]),
                     r=[("h", t), ("rstdg", t0)], w=[("xs", j)])
            else:
                S.op('act', lambda: nc.scalar.copy(out=xs[j], in_=h[:, t, :]), r=[("h", t)], w=[("xs", j)])
            pt = bank(6 + j).bitcast(BF16).rearrange("p (a b) -> p a b", b=128)
            for dc in range(8):
                S.op('pe', lambda: nc.tensor.transpose(out=pt[:, dc, :], in_=xs[j][:, dc * 128:(dc + 1) * 128],
                                                       identity=ident[:, :]),
                     r=[("xs", j), "ident"], w=[bkey(6 + j)], inc=(dc == 7))
            dst = dstT[:, :, (t - dbase) * 128:(t - dbase + 1) * 128]
            if gcol is not None:
                S.op('dve', lambda: nc.vector.tensor_tensor(out=dst, in0=pt, in1=gcol.unsqueeze(2).to_broadcast([128, 8, 128]),
                                                            op=ALU.mult), r=[bkey(6 + j), "gcols"], w=[(dkey, t - dbase)])
            else:
                S.op('dve', lambda: nc.vector.tensor_copy(out=dst, in_=pt), r=[bkey(6 + j)], w=[(dkey, t - dbase)], war=war)

    epsc = sb("epsc", [128, 2], F32)
    S.op('dve', lambda: nc.vector.memset(epsc[:, 0:1], EPS), w=["epsc"])
    S.op('dve', lambda: nc.vector.memset(epsc[:, 1:2], 1.0), w=["epsc"])

    def load_w(wb, wkey, src_ap, ncols, c0=0):
        kc = src_ap.shape[0] // 128
        return S.dma('pool', wb[:, 0:kc, c0:c0 + ncols], src_ap.rearrange("(kc p) c -> p kc c", p=128),
                     w=(WK(wkey) if wkey in KEYS else [wkey]))

    SPLIT = [False]

    def mm_acc(ps, pskey, lhs_list, rhs_list, rkeys):
        n = len(lhs_list)
        wide = SPLIT[0] and len(rhs_list[0].shape) == 2 and rhs_list[0].shape[-1] == 512
        for i in range(n):
            if wide:
                for ch in range(4):
                    cs = slice(ch * 128, (ch + 1) * 128)
                    last = (i == n - 1 and ch == 3)
                    S.op('pe', lambda: nc.tensor.matmul(ps[:, cs], lhsT=lhs_list[i], rhs=rhs_list[i][:, cs],
                                                        start=(i == 0 and ch == 0), stop=last, skip_group_check=True),
                         r=rkeys, w=[pskey], inc=last)
            else:
                S.op('pe', lambda: nc.tensor.matmul(ps, lhsT=lhs_list[i], rhs=rhs_list[i], start=(i == 0), stop=(i == n - 1)),
                     r=rkeys, w=[pskey], inc=(i == n - 1))

    def resid_epilogue(src_ap, srckeys, t, gkey, src_is_psum, add_eng='pool'):
        S.op('act', lambda: nc.scalar.activation(out=junk[:, :], in_=src_ap, func=AF.Square, accum_out=ss[:, t:t + 1]),
             r=srckeys, w=[("ss", t), "junk"])
        S.op('act', lambda: nc.scalar.activation(out=sq[:, t:t + 1], in_=ss[:, t:t + 1], func=AF.Sqrt, scale=1.0 / D,
                                                 bias=epsc[:, 0:1]), r=[("ss", t), "epsc"], w=[("sq", t)])
        S.op('dve', lambda: nc.vector.reciprocal(out=rstd[:, t:t + 1], in_=sq[:, t:t + 1]), r=[("sq", t)], w=[("rstd", t)])
        S.op('dve', lambda: nc.vector.scalar_tensor_tensor(out=tmp4[:, :], in0=src_ap, scalar=rstd[:, t:t + 1],
                                                           in1=gbc[:, 0, :], op0=ALU.mult, op1=ALU.mult),
             r=srckeys + [("rstd", t), gkey], w=["tmp4", ("sqr", 0), ("sqr", 1)])
        S.op(add_eng, lambda: (nc.gpsimd if add_eng == 'pool' else nc.vector).tensor_tensor(out=h[:, t, :], in0=h[:, t, :], in1=tmp4[:, :], op=ALU.add),
             r=["tmp4", ("sqr", 0), ("sqr", 1), ("h", t)], w=[("h", t)])

    def gla_prefetch(l):
        evs = [load_w(wbB, "wbB1", w_in_d[l][:, 1536:1664], 128, c0=256),
               load_w(wbB, "wbB0", w_in_d[l][:, 0:128], 128, c0=0),
               load_w(wbB, "wbB0", w_in_d[l][:, 256:384], 128, c0=128),
               load_w(wbA, "wbA", w_in_d[l][:, 512:1024], 512)]
        return evs

    pre_evs = gla_prefetch(0)
    for l in range(n_layers):
        w_in_l = w_in_d[l]
        with ExitStack() as es1:
            xnT = es1.enter_context(nc.sbuf_tensor("xnT_%d" % l, [128, 8, SEQ], BF16))
            mixT = es1.enter_context(nc.sbuf_tensor("mixT_%d" % l, [128, 8, SEQ], BF16))
            for g_ in range(4):
                norm_T(xnT, "xnT", 4 * g_, 4, gcols[:, l, 0, :], True, dbase=0)
            xkeys = [("xnT", t) for t in range(16)]

            def xk(tok0, ntok):
                return [("xnT", t) for t in range(tok0 // 128, (tok0 + ntok + 127) // 128)]

            with ExitStack() as es2:
                def A2(name, shape, dt):
                    return es2.enter_context(nc.sbuf_tensor("%s_%d" % (name, l), shape, dt))
                glrT = A2("glrT", [128, SEQ], BF16)
                wa2 = A2("wa2", [128, 256], BF16)
                qgz = A2("qgz", [128, 2, SEQ], BF16)
                kdT = A2("kdT", [128, SEQ], BF16)
                kg = A2("kg", [128, 16, 128], BF16)
                gv = A2("gv", [128, 16, 256], BF16)
                alast = A2("alast", [128, 16], F32)
                scr = A2("scr", [128, 2, 3, 512], F32)
                kgT = A2("kgT", [128, 2, 512], BF16)
                Sf = A2("Sf", [128, 256], F32)
                Sb = A2("Sb", [128, 3, 256], BF16)
                attm = A2("attm", [128, 2, 2, 128], BF16)

                def drive(gens, width):
                    gens = list(gens)
                    active = []
                    while gens or active:
                        while len(active) < width and gens:
                            active.append(gens.pop(0))
                        for g_ in list(active):
                            try:
                                next(g_)
                            except StopIteration:
                                active.remove(g_)

                S.op('dve', lambda: nc.vector.memset(wa2[:, :], 0.0), w=["wa2"])
                S.op('dve', lambda: nc.vector.memset(qgz[:, :, :], 0.0), w=["qgz"])
                S.dma('pool', wa2[0:16, :], w_a2_d[l], r=[], w=["wa2"])
                for tg in range(4):
                    b = tg % 2
                    mm_acc(bank(b), bkey(b), [wbB[:, dc, 256:384] for dc in range(8)],
                           [xnT[:, dc, tg * 512:(tg + 1) * 512] for dc in range(8)], ["wbB1"] + xk(tg * 512, 512))
                    S.op('act', lambda: nc.scalar.copy(out=glrT[:, tg * 512:(tg + 1) * 512], in_=bank(b)),
                         r=[bkey(b)], w=[("glrT", tg)])

                def prep_gen(pr, tg):
                    s_ = tg % 2
                    t1, cc, Eq = scr[:, s_, 0, :], scr[:, s_, 1, :], scr[:, s_, 2, :]
                    cc3 = cc.rearrange("p (a b) -> p a b", b=128)
                    k1, kc, kq, kk = ("t1", s_), ("cc", s_), ("Eq", s_), ("kgT", s_)
                    tsl = slice(tg * 512, (tg + 1) * 512)
                    bx, bq, bk_ = s_, 2 + s_, 4 + s_
                    S.op('pe', lambda: nc.tensor.matmul(bank(bx), lhsT=wa2[:, pr * 128:(pr + 1) * 128], rhs=glrT[:, tsl],
                                                        start=True, stop=True), r=["wa2", ("glrT", tg)], w=[bkey(bx)])
                    yield
                    mm_acc(bank(bq), bkey(bq), [wbB[:, dc, 0:128] for dc in range(8)],
                           [xnT[:, dc, tsl] for dc in range(8)], ["wbB0"] + xk(tg * 512, 512))
                    yield
                    mm_acc(bank(bk_), bkey(bk_), [wbB[:, dc, 128:256] for dc in range(8)],
                           [xnT[:, dc, tsl] for dc in range(8)], ["wbB0"] + xk(tg * 512, 512))
                    yield
                    S.op('act', lambda: nc.scalar.activation(out=t1, in_=bank(bx), func=AF.Exp, scale=-1.0,
                                                             bias=negb[:, l, pr:pr + 1]), r=[bkey(bx), "negb"], w=[k1])
                    yield
                    S.op('act', lambda: nc.scalar.activation(out=t1, in_=t1, func=AF.Ln, scale=1.0, bias=epsc[:, 1:2]),
                         r=[k1, "epsc"], w=[k1])
                    yield
                    for ch in range(4):
                        S.op('dve', lambda: nc.vector.tensor_tensor_scan(
                            out=cc3[:, ch, :], data0=onesf[:, :], data1=t1[:, ch * 128:(ch + 1) * 128], initial=0.0,
                            op0=ALU.mult, op1=ALU.add), r=[k1, "onesf"], w=[kc])
                    yield
                    S.op('act', lambda: nc.scalar.activation(out=Eq, in_=cc, func=AF.Exp, scale=-1.0 / 16.0), r=[kc], w=[kq])
                    yield
                    S.op('dve', lambda: nc.vector.tensor_tensor(
                        out=t1.rearrange("p (a b) -> p a b", b=128), in0=cc3, in1=cc3[:, :, 127:128].to_broadcast([128, 4, 128]),
                        op=ALU.subtract), r=[kc], w=[k1])
                    yield
                    S.op('act', lambda: nc.scalar.activation(out=t1, in_=t1, func=AF.Exp, scale=1.0 / 16.0), r=[k1], w=[k1])
                    yield
                    S.op('act', lambda: nc.scalar.activation(out=cc, in_=cc, func=AF.Exp, scale=1.0 / 16.0), r=[kc], w=[kc])
                    yield
                    S.op('act', lambda: nc.scalar.copy(out=alast[:, tg * 4:(tg + 1) * 4],
                                                       in_=Eq.rearrange("p (a b) -> p a b", b=128)[:, :, 127]), r=[kq], w=[("alast", tg)])
                    yield
                    for hh in range(2):
                        ps_ = slice(hh * 64, hh * 64 + 64)
                        S.op('dve', lambda: nc.vector.scalar_tensor_tensor(
                            out=qgz[ps_, hh, tsl], in0=bank(bq)[ps_, :], scalar=0.125, in1=Eq[ps_, :],
                            op0=ALU.mult, op1=ALU.mult), r=[bkey(bq), kq], w=[("qgz", tg)])
                    yield
                    S.op('dve', lambda: nc.vector.tensor_tensor(out=kdT[:, tsl], in0=bank(bk_), in1=cc, op=ALU.mult),
                         r=[bkey(bk_), kc], w=[("kdT", tg)])
                    yield
                    S.op('dve', lambda: nc.vector.tensor_tensor(out=kgT[:, s_, :], in0=bank(bk_), in1=t1, op=ALU.mult),
                         r=[bkey(bk_), k1], w=[kk])
                    yield
                    pt = bank(6).bitcast(BF16).rearrange("p (a b) -> p a b", b=128)
                    for ch in range(4):
                        S.op('pe', lambda: nc.tensor.transpose(out=pt[:, ch, :], in_=kgT[:, s_, ch * 128:(ch + 1) * 128],
                                                               identity=ident[:, :]), r=[kk, "ident"], w=[bkey(6)], inc=(ch == 3))
                    S.op('act', lambda: nc.scalar.copy(out=kg[:, tg * 4:(tg + 1) * 4, :], in_=pt[:, 0:4, :]),
                         r=[bkey(6)], w=[("kg", tg)])
                    yield

                def gv_gen(pr):
                    for t in range(16):
                        mm_acc(bank(7)[:, 0:256], bkey(7), [xnT[:, dc, t * 128:(t + 1) * 128] for dc in range(8)],
                               [wbA[:, dc, pr * 256:(pr + 1) * 256] for dc in range(8)], WK("wbA") + [("xnT", t)])
                        yield
                        S.op('act', lambda: nc.scalar.copy(out=gv[:, t, :], in_=bank(7)[:, 0:256]), r=[bkey(7)], w=[("gv", t)])
                        yield
                        yield

                def epi_ops(pr, tg, hh):
                    hd = 2 * pr + hh
                    osb, osq, sgt = scr[:, hh, 0, :], scr[:, hh, 1, :], scr[:, hh, 2, :]
                    ko, kq2, kg2 = ("t1", hh), ("cc", hh), ("Eq", hh)
                    tsl = slice(tg * 512, (tg + 1) * 512)
                    ops = []
                    ops.append(lambda: S.op('act', lambda: nc.scalar.copy(out=osb, in_=bank(4 + hh)), r=[bkey(4 + hh)], w=[ko]))
                    ops.append(lambda: S.op('act', lambda: nc.scalar.activation(out=osq, in_=osb, func=AF.Square), r=[ko], w=[kq2]))
                    ops.append(lambda: S.op('pe', lambda: nc.tensor.matmul(bank(hh), lhsT=onesf[:, :], rhs=osq, start=True, stop=True),
                                            r=["onesf", kq2], w=[bkey(hh)]))
                    ops.append(lambda: S.op('act', lambda: nc.scalar.activation(out=osq, in_=bank(hh), func=AF.Ln, scale=1.0 / 128.0,
                                                                             bias=epsc[:, 0:1]), r=[bkey(hh), "epsc"], w=[kq2]))
                    ops.append(lambda: S.op('act', lambda: nc.scalar.activation(out=osq, in_=osq, func=AF.Exp, scale=-0.5), r=[kq2], w=[kq2]))
                    ops.append(lambda: S.op('dve', lambda: nc.vector.tensor_tensor(out=osb, in0=osb, in1=osq, op=ALU.mult),
                                            r=[ko, kq2], w=[ko]))
                    ops.append(lambda: mm_acc(bank(hh), bkey(hh), [wbB[:, dc, (2 + hh) * 128:(3 + hh) * 128] for dc in range(8)],
                                              [xnT[:, dc, tsl] for dc in range(8)], ["wbB1"] + xk(tg * 512, 512)))
                    ops.append(lambda: S.op('act', lambda: nc.scalar.activation(out=sgt, in_=bank(hh), func=AF.Silu), r=[bkey(hh)], w=[kg2]))
                    ops.append(lambda: S.op('dve', lambda: nc.vector.scalar_tensor_tensor(
                        out=mixT[:, hd, tsl], in0=osb, scalar=gng[:, l:l + 1], in1=sgt, op0=ALU.mult, op1=ALU.mult),
                        r=[ko, kg2, "gng"], w=[("mixT", hd, tg)]))
                    return ops

                for pr in range(2):
                    for hh_ in range(2):
                        load_w(wbB, "wbB1", w_in_l[:, 1024 + (2 * pr + hh_) * 128:1024 + (2 * pr + hh_ + 1) * 128], 128, c0=256 + hh_ * 128)
                    drive([prep_gen(pr, 0), prep_gen(pr, 1), gv_gen(pr), prep_gen(pr, 2), prep_gen(pr, 3)], 3)
                    if pr == 0:
                        load_w(wbB, "wbB0", w_in_l[:, 128:256], 128, c0=0)
                        load_w(wbB, "wbB0", w_in_l[:, 384:512], 128, c0=128)

                    S.op('dve', lambda: nc.vector.memset(Sf[:, :], 0.0), w=["Sf"])
                    S.op('dve', lambda: nc.vector.memset(Sb[:, 0, :], 0.0), w=[("Sb", 0)])
                    pend = []

                    def drip(k=1):
                        for _ in range(k):
                            if pend:
                                pend.pop(0)()

                    def stage_A(n):
                        tg = n // 4
                        csl = slice(n * 128, (n + 1) * 128)
                        a2 = n % 2
                        ab = 2 + a2
                        kb = 6 + a2
                        for hh in range(2):
                            S.op('pe', lambda: nc.tensor.matmul(bank(ab)[:, hh * 128:(hh + 1) * 128], lhsT=kdT[:, csl],
                                                                rhs=qgz[:, hh, csl], start=True, stop=True, skip_group_check=True),
                                 r=[("kdT", tg), ("qgz", tg)], w=[bkey(ab)], inc=(hh == 1))
                        S.op('pe', lambda: nc.tensor.matmul(bank(kb)[:, 0:256], lhsT=kg[:, n, :], rhs=gv[:, n, :], start=True, stop=True),
                             r=[("kg", tg), ("gv", n)], w=[bkey(kb)])
                        S.op('dve', lambda: nc.vector.tensor_tensor(
                            out=attm[:, a2, :, :], in0=bank(ab)[:, 0:256].rearrange("p (a b) -> p a b", b=128),
                            in1=caus[:, :].unsqueeze(1).to_broadcast([128, 2, 128]), op=ALU.mult),
                            r=[bkey(ab), "caus"], w=[("attm", a2)])
                        S.op('dve', lambda: nc.vector.scalar_tensor_tensor(
                            out=Sf[:, :], in0=Sf[:, :], scalar=alast[:, n:n + 1], in1=bank(kb)[:, 0:256],
                            op0=ALU.mult, op1=ALU.add), r=["Sf", bkey(kb), ("alast", tg)], w=["Sf"])
                        S.op('act', lambda: nc.scalar.copy(out=Sb[:, (n + 1) % 3, :], in_=Sf[:, :]), r=["Sf"], w=[("Sb", (n + 1) % 3)])

                    def stage_B(n):
                        tg = n // 4
                        csl = slice(n * 128, (n + 1) * 128)
                        a2 = n % 2
                        for hh in range(2):
                            ob = bank(4 + hh)[:, (n % 4) * 128:(n % 4 + 1) * 128]
                            S.op('pe', lambda: nc.tensor.matmul(ob, lhsT=gv[:, n, hh * 128:(hh + 1) * 128], rhs=attm[:, a2, hh, :],
                                                                start=True, stop=False, skip_group_check=True),
                                 r=[("gv", n), ("attm", a2)], w=[bkey(4 + hh)], inc=False)
                            S.op('pe', lambda: nc.tensor.matmul(ob, lhsT=Sb[:, n % 3, hh * 128:(hh + 1) * 128], rhs=qgz[:, hh, csl],
                                                                start=False, stop=True, skip_group_check=True),
                                 r=[("Sb", n % 3), ("qgz", tg)], w=[bkey(4 + hh)])

                    stage_A(0)
                    for n in range(16):
                        if n + 1 < 16:
                            stage_A(n + 1)
                        drip(2)
                        stage_B(n)
                        drip(2)
                        if n % 4 == 3:
                            drip(len(pend))
                            e0, e1 = epi_ops(pr, n // 4, 0), epi_ops(pr, n // 4, 1)
                            e0[0]()
                            e1[0]()
                            for x0, x1 in zip(e0[1:], e1[1:]):
                                pend.append(x0)
                                pend.append(x1)
                    drip(len(pend))
                S.barrier(engines=('act', 'dve', 'sp'))

            with ExitStack() as es3:
                qz = es3.enter_context(nc.sbuf_tensor("qz_%d" % l, [128, 2, SEQ], BF16))
                kT = es3.enter_context(nc.sbuf_tensor("kT_%d" % l, [128, SEQ], BF16))
                Vn = es3.enter_context(nc.sbuf_tensor("Vn_%d" % l, [128, 3, 16, 192], BF16))
                ebf = es3.enter_context(nc.sbuf_tensor("ebf_%d" % l, [128, 3, 512], BF16))
                acc = es3.enter_context(nc.sbuf_tensor("acc_%d" % l, [128, 2, 512], F32))
                rs = es3.enter_context(nc.sbuf_tensor("rs_%d" % l, [128, 512], F32))
                biasM = es3.enter_context(nc.sbuf_tensor("biasM_%d" % l, [128, 5, 2, 128], BF16))
                madd = es3.enter_context(nc.sbuf_tensor("madd_%d" % l, [128, 5, 128], F32))
                et = es3.enter_context(nc.sbuf_tensor("et_%d" % l, [128, 2, 128], F32))
                S.dma('sp', madd[:, :, :], mask_d.rearrange("p (k q) -> p k q", q=128), w=["madd"])
                S.op('dve', lambda: nc.vector.tensor_scalar(out=madd[:, :, :], in0=madd[:, :, :], scalar1=-1.0, scalar2=30000.0,
                                                            op0=ALU.add, op1=ALU.mult), r=["madd"], w=["madd"])
                S.op('dve', lambda: nc.vector.memset(qz[:, :, :], 0.0), w=["qz"])
                S.op('dve', lambda: nc.vector.memset(Vn[:, :, :, 64:128], 1.0), w=["Vn"])
                for pr in range(4):
                    wb = wbA if pr % 2 == 0 else wbB
                    wk = "wbA" if pr % 2 == 0 else "wbB"
                    bv_ = bias_d.rearrange("p (k h q) -> p k h q", k=5, h=8)
                    for kd in range(5):
                        S.dma('sp', et[:, :, :], bv_[:, kd, 2 * pr:2 * pr + 2, :], w=["et"])
                        S.op('dve', lambda: nc.vector.tensor_tensor(
                            out=biasM[:, kd, :, :], in0=et[:, :, :], in1=madd[:, kd, :].unsqueeze(1).to_broadcast([128, 2, 128]),
                            op=ALU.add), r=["et", "madd"], w=["biasM"])
                    for i, c0 in enumerate([1552, 2064, 2576]):
                        S.dma('pool', wb[:, :, i * 128:(i + 1) * 128],
                              w_in_l[:, c0 + pr * 128:c0 + (pr + 1) * 128].rearrange("(kc p) c -> p kc c", p=128), w=WK(wk))
                    for tg in range(4):
                        tsl = slice(tg * 512, (tg + 1) * 512)
                        mm_acc(bank(5), bkey(5), [wb[:, dc, 0:128] for dc in range(8)], [xnT[:, dc, tsl] for dc in range(8)],
                               WK(wk) + xk(tg * 512, 512))
                        for hh in range(2):
                            ps_ = slice(hh * 64, hh * 64 + 64)
                            S.op('act', lambda: nc.scalar.mul(out=qz[ps_, hh, tsl], in_=bank(5)[ps_, :], mul=0.125),
                                 r=[bkey(5)], w=["qz"])
                        mm_acc(bank(6), bkey(6), [wb[:, dc, 128:256] for dc in range(8)], [xnT[:, dc, tsl] for dc in range(8)],
                               WK(wk) + xk(tg * 512, 512))
                        S.op('act', lambda: nc.scalar.copy(out=kT[:, tsl], in_=bank(6)), r=[bkey(6)], w=["kT"])
                    vT = xsb[:, :, :].rearrange("p a b -> p (a b)")
                    for tg in range(4):
                        tsl = slice(tg * 512, (tg + 1) * 512)
                        b = 5 + (tg % 2)
                        mm_acc(bank(b), bkey(b), [wb[:, dc, 256:384] for dc in range(8)], [xnT[:, dc, tsl] for dc in range(8)],
                               WK(wk) + xk(tg * 512, 512))
                        S.op('act', lambda: nc.scalar.copy(out=vT[:, tsl], in_=bank(b)), r=[bkey(b)], w=[("xs", tg // 2)])
                    for lay in range(3):
                        for g8 in range(2):
                            b = 5 + ((lay * 2 + g8) % 2)
                            ptb = bank(b).bitcast(BF16).rearrange("p (a b) -> p a b", b=128)
                            for tt in range(8):
                                ti = g8 * 8 + tt
                                if lay == 0:
                                    tok = slice(ti * 128, (ti + 1) * 128)
                                elif lay == 1:
                                    c_, r_ = ti // 4, ti % 4
                                    tok = slice(512 * c_ + r_, 512 * c_ + 512, 4)
                                else:
                                    tok = slice(ti, SEQ, 16)
                                S.op('pe', lambda: nc.tensor.transpose(out=ptb[:, tt, :], in_=vT[:, tok], identity=ident[:, :]),
                                     r=[("xs", 0), ("xs", 1), "ident"], w=[bkey(b)], inc=(tt == 7))
                            evac_eng = 'act' if (lay * 2 + g8) % 2 == 0 else 'dve'
                            dstv = Vn[:, lay, g8 * 8:(g8 + 1) * 8, :].rearrange("p t (h e) -> p t h e", e=64)[:, :, 0:3:2, :]
                            srcv = ptb.rearrange("p t (h e) -> p t h e", e=64)
                            if evac_eng == 'act':
                                S.op('act', lambda: nc.scalar.copy(out=dstv, in_=srcv), r=[bkey(b)], w=["Vn"])
                            else:
                                S.op('dve', lambda: nc.vector.tensor_copy(out=dstv, in_=srcv), r=[bkey(b)], w=["Vn"])
                    if pr == 3:
                        load_w(wbA, "wbA", w_out_d[l][:, 0:512], 512)
                        load_w(wbB, "wbB", w_out_d[l][:, 512:1024], 512)
                    stages = []
                    for hh in range(2):
                        for c in range(4):
                            g = (hh * 4 + c)
                            u1b, u4b, u16b = (2, 3, 4) if g % 2 == 0 else (5, 6, 4)
                            cur1, prev1, cur4, prev4, d16 = [], [], [], [], []
                            for n2 in range(4):
                                nb = 4 * c + n2
                                qs = slice(nb * 128, (nb + 1) * 128)
                                cur1.append((qs, qs, (0, nb), n2 * 128, 128))
                                if nb > 0:
                                    prev1.append((slice((nb - 1) * 128, nb * 128), qs, (0, nb - 1), n2 * 128, 128))
                            for r_ in range(4):
                                qs = slice(512 * c + r_, 512 * c + 512, 4)
                                cur4.append((qs, qs, (1, 4 * c + r_), r_ * 128, 128))
                                if c > 0:
                                    prev4.append((slice(512 * (c - 1) + r_, 512 * c, 4), qs, (1, 4 * (c - 1) + r_), r_ * 128, 128))
                            for r16 in range(16):
                                d16.append((slice(r16, SEQ, 16), slice(512 * c + r16, 512 * c + 512, 16), (2, r16), r16 * 32, 32))
                            grp = [(1, u1b, cur1), (0, u1b, prev1), (3, u4b, cur4), (2, u4b, prev4), (4, u16b, d16)]
                            grp = [x for x in grp if x[2]]
                            started = set()
                            for gi, (kind, ub, items) in enumerate(grp):
                                first = ub not in started
                                started.add(ub)
                                stages.append(dict(hh=hh, c=c, kind=kind, ub=ub, items=items, first=first,
                                                   last=(gi == len(grp) - 1), ubs=(u1b, u4b, u16b)))

                    def emit_L(i, st):
                        lb = (0, 1, 7)[i % 3]
                        es_ = i % 3
                        hh, c, kind, items = st['hh'], st['c'], st['kind'], st['items']
                        lo = items[0][3]
                        if kind == 4:
                            bv = biasM[:, 4, hh, 32 * c:32 * c + 32].unsqueeze(1).to_broadcast([128, 16, 32])
                        else:
                            bv = biasM[:, kind, hh, :].unsqueeze(1).to_broadcast([128, (512 - lo) // 128, 128])
                        S.op('pe', lambda: nc.tensor.matmul(bank(lb)[:, lo:512], lhsT=ident[:, :], rhs=bv, start=True, stop=False,
                                                            skip_group_check=True), r=["ident", "biasM"], w=[bkey(lb)], inc=False)
                        for (ks, qs, vt, c0, ncol) in items:
                            S.op('pe', lambda: nc.tensor.matmul(bank(lb)[:, c0:c0 + ncol], lhsT=kT[:, ks], rhs=qz[:, hh, qs],
                                                                start=False, stop=(c0 + ncol == 512), skip_group_check=True),
                                 r=["kT", "qz"], w=[bkey(lb)], inc=(c0 + ncol == 512))
                        S.op('act', lambda: nc.scalar.activation(out=ebf[:, es_, lo:512], in_=bank(lb)[:, lo:512], func=AF.Exp),
                             r=[bkey(lb)], w=[("ebf", es_)])

                    def emit_PV(i, st):
                        lb = (0, 1, 7)[i % 3]
                        es_ = i % 3
                        hh, c, ub, items = st['hh'], st['c'], st['ub'], st['items']
                        for ii, (ks, qs, vt, c0, ncol) in enumerate(items):
                            S.op('pe', lambda: nc.tensor.matmul(bank(ub)[:, c0:c0 + ncol], lhsT=Vn[:, vt[0], vt[1], 64 * hh:64 * hh + 128],
                                                                rhs=ebf[:, es_, c0:c0 + ncol], start=(st['first'] and ii == 0), stop=False,
                                                                skip_group_check=True),
                                 r=["Vn", ("ebf", es_)], w=[bkey(ub)], inc=(ii == len(items) - 1))
                        if st['last']:
                            u1_, u4_, u16_ = st['ubs']
                            a_ = acc[:, (hh * 4 + c) % 2, :]
                            ak = ("acc", (hh * 4 + c) % 2)
                            S.op('dve', lambda: nc.vector.tensor_copy(out=a_, in_=bank(u1_)), r=[bkey(u1_)], w=[ak])
                            S.op('dve', lambda: nc.vector.tensor_tensor(
                                out=a_.rearrange("p (i r) -> p i r", r=4), in0=bank(u4_).rearrange("p (r i) -> p i r", r=4),
                                in1=a_.rearrange("p (i r) -> p i r", r=4), op=ALU.add), r=[bkey(u4_), ak], w=[ak])
                            S.op('dve', lambda: nc.vector.tensor_tensor(
                                out=a_.rearrange("p (i r) -> p i r", r=16), in0=bank(u16_).rearrange("p (r i) -> p i r", r=16),
                                in1=a_.rearrange("p (i r) -> p i r", r=16), op=ALU.add), r=[bkey(u16_), ak], w=[ak])
                            up = slice(64 * hh, 64 * hh + 64)
                            sp_ = slice(64 * (1 - hh), 64 * (1 - hh) + 64)

                            def fin():
                                S.op('act', lambda: nc.scalar.activation(out=rs[up, :], in_=a_[sp_, :], func=AF.Ln), r=[ak], w=["rs"])
                                S.op('act', lambda: nc.scalar.activation(out=rs[up, :], in_=rs[up, :], func=AF.Exp, scale=-1.0),
                                     r=["rs"], w=["rs"])
                                S.op('dve', lambda: nc.vector.tensor_tensor(
                                    out=mixT[up, 4 + pr, c * 512:(c + 1) * 512], in0=a_[up, :], in1=rs[up, :],
                                    op=ALU.mult), r=[ak, "rs"], w=[("mixT", 4 + pr, c, hh)])
                            pend_n.append([i + 4, fin])

                    pend_n = []
                    for i, st in enumerate(stages):
                        emit_L(i, st)
                        while pend_n and pend_n[0][0] <= i:
                            pend_n.pop(0)[1]()
                        if i >= 2:
                            emit_PV(i - 2, stages[i - 2])
                    emit_PV(len(stages) - 2, stages[-2])
                    emit_PV(len(stages) - 1, stages[-1])
                    while pend_n:
                        pend_n.pop(0)[1]()

            S.fence([("x2T", t_) for t_ in range(8)] + ["hidT", ("fbuf", 0), ("fbuf", 1)])
            with nc.allow_non_contiguous_dma(reason="gain broadcast"):
                S.dma('sp', gbc[:, :, :], post_mix_d[l:l + 1, :].partition_broadcast(128), w=["gbc"])
            for t in range(16):
                pp = PP[t % 2]
                pks = [bkey(2 * (t % 2)), bkey(2 * (t % 2) + 1)]
                mkeys = [("mixT", hd_, t // 4) for hd_ in range(4)] + [("mixT", 4 + p_, t // 4, h_) for p_ in range(4) for h_ in range(2)]
                for nb_, (wb, wk) in enumerate([(wbA, "wbA"), (wbB, "wbB")]):
                    for kc in range(8):
                        S.op('pe', lambda: nc.tensor.matmul(pp[:, nb_ * 512:(nb_ + 1) * 512], lhsT=mixT[:, kc, t * 128:(t + 1) * 128],
                                                            rhs=wb[:, kc, :], start=(kc == 0), stop=(kc == 7)),
                             r=WK(wk) + mkeys, w=pks, inc=(kc == 7))
                resid_epilogue(pp[:, :], pks, t, "gbc", True, add_eng=('pool' if t % 2 == 0 else 'dve'))
            pre_evs = [load_w(wbA, "wbA", w1_d[l][:, 0:512], 512), load_w(wbB, "wbB", w1_d[l][:, 512:1024], 512)]
            mlp_fence = True

        with ExitStack() as es4:
            hidT = es4.enter_context(nc.sbuf_tensor("hidT_%d" % l, [128, 32, 1024], BF16))
            fbuf = es4.enter_context(nc.sbuf_tensor("fbuf_%d" % l, [128, 8, D], F32))
            x2T = es4.enter_context(nc.sbuf_tensor("x2T_%d" % l, [128, 8, 1024], BF16))
            sqr = tmp4[:, :].rearrange("p (a b) -> p a b", b=512)
            wbs = [(wbA, "wbA"), (wbB, "wbB")]
            with nc.allow_non_contiguous_dma(reason="gain broadcast"):
                S.dma('sp', gbc[:, :, :], post_mlp_d[l:l + 1, :].partition_broadcast(128), w=["gbc"])
            x2keys = [("x2T", t) for t in range(8)]
            wi = [0]

            def mm1(hf, after_fb=None):
                for fb in range(8):
                    wb, wk = wbs[wi[0] % 2]
                    wi[0] += 1
                    if not (hf == 0 and fb < 2):
                        load_w(wb, wk, w1_d[l][:, fb * 512:(fb + 1) * 512], 512)
                    for fc in range(4):
                        for tg in range(2):
                            b = (fc * 2 + tg) % 4
                            mm_acc(bank(b), bkey(b), [wb[:, dc, fc * 128:(fc + 1) * 128] for dc in range(8)],
                                   [x2T[:, dc, tg * 512:(tg + 1) * 512] for dc in range(8)], WK(wk) + x2keys[4 * tg:4 * tg + 4])
                            s2 = b % 2
                            S.op('act', lambda: nc.scalar.activation(out=sqr[:, s2, :], in_=bank(b), func=AF.Square),
                                 r=[bkey(b)], w=[("sqr", s2)])
                            S.op('dve', lambda: nc.vector.scalar_tensor_tensor(
                                out=hidT[:, fb * 4 + fc, tg * 512:(tg + 1) * 512], in0=bank(b), scalar=0.0, in1=sqr[:, s2, :],
                                op0=ALU.is_gt, op1=ALU.mult), r=[bkey(b), ("sqr", s2)], w=["hidT"])
                    if after_fb is not None:
                        after_fb(fb)

            def mm2(hf, after_cb=None):
                for cb in range(8):
                    if after_cb is not None:
                        after_cb(cb)
                    wb, wk = wbs[wi[0] % 2]
                    wi[0] += 1
                    wv = wb[:, :, :].rearrange("p a (b c) -> p (a b) c", c=128)
                    S.dma('pool', wv, w2_d[l][:, cb * 128:(cb + 1) * 128].rearrange("(kc p) c -> p kc c", p=128), w=WK(wk))
                    for g4 in range(2):
                        b = 4 + (cb * 2 + g4) % 2
                        for tt in range(4):
                            t = g4 * 4 + tt
                            mm_acc(bank(b)[:, tt * 128:(tt + 1) * 128], bkey(b),
                                   [hidT[:, fc, t * 128:(t + 1) * 128] for fc in range(32)], [wv[:, fc, :] for fc in range(32)],
                                   WK(wk) + ["hidT"])
                        S.op('act', lambda: nc.scalar.copy(out=fbuf[:, g4 * 4:(g4 + 1) * 4, cb * 128:(cb + 1) * 128],
                                                           in_=bank(b).rearrange("p (t c) -> p t c", c=128)),
                             r=[bkey(b)], w=[("fbuf", g4)])

            def epi(hf, t8):
                resid_epilogue(fbuf[:, t8, :], [("fbuf", t8 // 4)], 8 * hf + t8, "gbc", False, add_eng=('dve' if hf == 0 else 'pool'))

            norm_T(x2T, "x2T", 0, 4, gcols[:, l, 1, :], True, dbase=0)
            norm_T(x2T, "x2T", 4, 4, gcols[:, l, 1, :], True, dbase=0)
            mm1(0)
            norm_T(x2T, "x2T", 8, 4, gcols[:, l, 1, :], True, dbase=8)
            norm_T(x2T, "x2T", 12, 4, gcols[:, l, 1, :], True, dbase=8)
            mm2(0)
            mm1(1, after_fb=lambda fb: epi(0, fb))
            pbf = x2T[:, 0:4, :].rearrange("p a (b c) -> p (a b) c", c=256)
            pT = x2T[:, 4:8, :].rearrange("p (a b) c -> p a (b c)", a=2)
            def p_prep(cb):
                if cb == 4:
                    S.dma('pool', pbf, p_d[l].rearrange("(t p) k -> p t k", p=128), w=["pbf"], war=x2keys)
                if cb != 7:
                    return
                for t in range(16):
                    pt = bank(6 + t % 2).bitcast(BF16).rearrange("p (a b) -> p a b", b=128)
                    for kc in range(2):
                        S.op('pe', lambda: nc.tensor.transpose(out=pt[:, kc, :], in_=pbf[:, t, kc * 128:(kc + 1) * 128],
                                                               identity=ident[:, :]), r=["pbf", "ident"], w=[bkey(6 + t % 2)],
                             inc=(kc == 1))
                    S.op('act', lambda: nc.scalar.copy(out=pT[:, :, t * 128:(t + 1) * 128], in_=pt[:, 0:2, :]),
                         r=[bkey(6 + t % 2)], w=[("pT", t)], war=x2keys)

            mm2(1, after_cb=p_prep)
            load_w(wbA, "wbA", wg_d[l][:, 0:512], 512)
            load_w(wbB, "wbB", wg_d[l][:, 512:1024], 512)

            hT = hidT[:, 0:8, :]
            wpp = hidT[:, 16:18, :]
            sgm = hidT[:, 18:20, :].rearrange("p a c -> p (a c)").bitcast(F32).rearrange("p (a c) -> p a c", a=2)
            S.dma('pool', wpp, wp_d[l].rearrange("(kc p) c -> p kc c", p=128), w=["wpp"], war=["hidT"])

            def ple_half(hf, between=None):
                norm_T(hT, "hT", 8 * hf, 8, None, False, war=["hidT"])
                for t8 in range(8):
                    t = 8 * hf + t8
                    for nb_, (wb, wk) in enumerate([(wbA, "wbA"), (wbB, "wbB")]):
                        gb = (t % 2) * 2 + nb_
                        pb = 4 + (t % 2)
                        mm_acc(bank(gb), bkey(gb), [hT[:, dc, t8 * 128:(t8 + 1) * 128] for dc in range(8)],
                               [wb[:, dc, :] for dc in range(8)], WK(wk) + [("hT", t8)])
                        s2 = nb_
                        S.op('act', lambda: nc.scalar.activation(out=sgm[:, s2, :], in_=bank(gb), func=AF.Sigmoid),
                             r=[bkey(gb)], w=[("sgm", s2)], war=["hidT"])
                        mm_acc(bank(pb), bkey(pb), [pT[:, kc, t * 128:(t + 1) * 128] for kc in range(2)],
                               [wpp[:, kc, nb_ * 512:(nb_ + 1) * 512] for kc in range(2)], ["wpp", ("pT", t)])
                        S.op('dve', lambda: nc.vector.tensor_tensor(out=sgm[:, s2, :], in0=bank(pb), in1=sgm[:, s2, :], op=ALU.mult),
                             r=[bkey(pb), ("sgm", s2)], w=[("sgm", s2)])
                        ae = 'pool' if nb_ == 0 else 'dve'
                        S.op(ae, lambda: (nc.gpsimd if ae == 'pool' else nc.vector).tensor_tensor(
                            out=h[:, t, nb_ * 512:(nb_ + 1) * 512], in0=h[:, t, nb_ * 512:(nb_ + 1) * 512], in1=sgm[:, s2, :],
                            op=ALU.add), r=[("sgm", s2), ("h", t)], w=[("h", t)])
                    if between is not None:
                        between(t8)

            ple_half(0, between=lambda t8: epi(1, t8))
            ple_half(1)
            pre_evs = gla_prefetch(l + 1) if l + 1 < n_layers else []
            S.barrier(exclude=pre_evs)

    ov = out_d.rearrange("(t p) d -> p t d", p=128)
    evs = []
    for g in range(8):
        evs.append(S.dma('sp', ov[:, 2 * g:2 * g + 2, :], h[:, 2 * g:2 * g + 2, :],
                         r=[("h", t) for t in range(2 * g, 2 * g + 2)]))
    S._wait('sp', evs)
    return nc


_CONST = {}


def _consts():
    if not _CONST:
        kinds = _bias_tables()
        _CONST['idx'] = np.stack([k[0] for k in kinds], 0)
        _CONST['mask'] = np.ascontiguousarray(np.stack([k[1] for k in kinds], 1).reshape(128, 5 * 128)).astype(np.float32)
        _CONST['ident'] = np.eye(128, dtype=np.float32)
        kk = np.arange(128)
        _CONST['caus'] = (kk[:, None] <= kk[None, :]).astype(np.float32)
    return _CONST


def kernel(x, p, w_in, w_gla_a2, b_gla_a, gla_norm_g, w_out, rel_bias, pre_mix_g, post_mix_g, pre_mlp_g, post_mlp_g,
           w_mlp_in, w_mlp_out, w_ple_gate, w_ple_proj, _n_layers=DEPTH, _cores=8):
    f = lambda a: np.ascontiguousarray(np.asarray(a, dtype=np.float32))
    c = _consts()
    rb = f(rel_bias)
    bt = rb[c['idx']]
    bt = np.ascontiguousarray(bt.transpose(1, 0, 3, 2)).reshape(128, 5 * 8 * 128)
    shared = {
        "w_in": f(w_in), "w_gla_a2": f(w_gla_a2), "b_gla_a": f(b_gla_a), "gla_norm_g": f(gla_norm_g), "w_out": f(w_out),
        "pre_mix_g": f(pre_mix_g), "post_mix_g": f(post_mix_g), "pre_mlp_g": f(pre_mlp_g), "post_mlp_g": f(post_mlp_g),
        "w_mlp_in": f(w_mlp_in), "w_mlp_out": f(w_mlp_out), "w_ple_gate": f(w_ple_gate), "w_ple_proj": f(w_ple_proj),
        "bias_tab": bt, "mask_tab": c['mask'], "ident": c['ident'], "caus": c['caus'],
    }
    x = f(x)
    p = f(p)
    nc = build(_n_layers)
    in_maps = []
    for b in range(_cores):
        m = dict(shared)
        m["x"] = x[b]
        m["p"] = np.ascontiguousarray(p[:, b])
        in_maps.append(m)
    res = run_bass_kernel_spmd(nc, in_maps, core_ids=list(range(_cores)))
    return np.stack([np.asarray(r["out"], dtype=np.float32) for r in res.results], 0)
```

```python
import numpy as np
from contextlib import ExitStack
import concourse.bass as bass
import concourse.mybir as mybir
from concourse.bass_utils import run_bass_kernel_spmd

F32 = mybir.dt.float32
BF16 = mybir.dt.bfloat16
ALU = mybir.AluOpType
AF = mybir.ActivationFunctionType

SEQ = 2048
D = 1024
DEPTH = 2
IN_W = 3088
EPS = 1e-6
NDS = 12


class Sched:
    def __init__(self, nc):
        self.nc = nc
        self.E = {'pe': nc.tensor, 'act': nc.scalar, 'dve': nc.vector, 'pool': nc.gpsimd, 'sp': nc.sync}
        self.semh = {}
        for k in ['pe', 'act', 'dve', 'pool']:
            self.semh[k] = nc.alloc_semaphore("sem_" + k)
        for i in range(NDS):
            self.semh['dma%d' % i] = nc.alloc_semaphore("semdma%d" % i)
        self.cnt = {k: 0 for k in self.semh}
        self.known = {k: {} for k in self.E}
        self.lastw = {}
        self.readers = {}
        self.dnext = {'pool': 0, 'sp': 0}

    def _deps(self, r, w):
        evs = []
        for k in r:
            if k in self.lastw:
                evs.append(self.lastw[k])
        for k in w:
            if k in self.lastw:
                evs.append(self.lastw[k])
            rd = self.readers.get(k)
            if rd:
                evs.extend(rd.items())
        return evs

    def _wait(self, eng, evs):
        need = {}
        for (sn, v) in evs:
            if sn == 'pe' and eng == 'pe':
                continue
            if v > need.get(sn, 0):
                need[sn] = v
        kn = self.known[eng]
        for sn, v in need.items():
            if kn.get(sn, 0) >= v:
                continue
            self.E[eng].wait_ge(self.semh[sn], v)
            kn[sn] = v

    def _record(self, ev, r, w):
        for k in r:
            d = self.readers.setdefault(k, {})
            if ev[1] > d.get(ev[0], 0):
                d[ev[0]] = ev[1]
        for k in w:
            self.lastw[k] = ev
            self.readers[k] = {}

    def _war(self, keys):
        evs = []
        for k in keys:
            if k in self.lastw:
                evs.append(self.lastw[k])
            rd = self.readers.get(k)
            if rd:
                evs.extend(rd.items())
        return evs

    def op(self, eng, fn, r=(), w=(), inc=True, war=()):
        self._wait(eng, self._deps(r, w) + self._war(war))
        ins = fn()
        if inc:
            self.cnt[eng] += 1
            ins.then_inc(self.semh[eng], 1)
            ev = (eng, self.cnt[eng])
        else:
            ev = (eng, self.cnt[eng] + 1)
        self._record(ev, r, w)
        return ins

    def dma(self, q, out, in_, r=(), w=(), war=()):
        half = NDS // 2
        j = self.dnext[q]
        self.dnext[q] = (j + 1) % half
        i = j + (half if q == 'pool' else 0)
        sn = 'dma%d' % i
        evs = self._deps(r, w) + self._war(war)
        if self.cnt[sn] > 0:
            evs.append((sn, self.cnt[sn]))
        self._wait(q, evs)
        self.cnt[sn] += 16
        self.E[q].dma_start(out=out, in_=in_).then_inc(self.semh[sn], 16)
        ev = (sn, self.cnt[sn])
        self._record(ev, r, w)
        return ev

    def barrier(self, exclude=(), engines=None):
        ex = set(exclude)
        evs = [(k, v) for k, v in self.cnt.items() if v > 0 and (k, v) not in ex]
        for e in (engines if engines is not None else self.E):
            self._wait(e, evs)

    def fence(self, keys, exclude=()):
        ex = set(exclude)
        snap = {k: v for k, v in self.cnt.items() if v > 0 and (k, v) not in ex}
        for k in keys:
            d = self.readers.setdefault(k, {})
            for sn, v in snap.items():
                if v > d.get(sn, 0):
                    d[sn] = v


def _t5_bucket_np(dist):
    dist = np.asarray(dist, np.int64)
    d = np.maximum(dist, 1).astype(np.float32)
    large = 16 + (np.log(d / np.float32(16)) / np.float32(np.log(128.0)) * np.float32(16)).astype(np.int32)
    large = np.minimum(large, 31)
    return np.where(dist < 16, dist, large).astype(np.int64)


def _bias_tables():
    k = np.arange(128)[:, None]
    q = np.arange(128)[None, :]
    kinds = []
    for (dil, prev) in [(1, True), (1, False), (4, True), (4, False), (16, False)]:
        if prev:
            j = q + 128 - k
            valid = (k >= q)
        else:
            j = q - k
            valid = (k <= q)
        idx = _t5_bucket_np(np.maximum(j, 0) * dil)
        kinds.append((idx, valid.astype(np.float32)))
    return kinds


def build(n_layers=DEPTH):
    nc = bass.Bass("TRN2", target_bir_lowering=False)
    S = Sched(nc)

    def din(name, shape):
        return nc.dram_tensor(name, shape, F32, kind="ExternalInput").ap()

    x_d = din("x", [SEQ, D])
    p_d = din("p", [DEPTH, SEQ, 256])
    w_in_d = din("w_in", [DEPTH, D, IN_W])
    w_a2_d = din("w_gla_a2", [DEPTH, 16, 256])
    b_a_d = din("b_gla_a", [DEPTH, 256])
    gng_d = din("gla_norm_g", [DEPTH, 128])
    w_out_d = din("w_out", [DEPTH, D, D])
    pre_mix_d = din("pre_mix_g", [DEPTH, D])
    post_mix_d = din("post_mix_g", [DEPTH, D])
    pre_mlp_d = din("pre_mlp_g", [DEPTH, D])
    post_mlp_d = din("post_mlp_g", [DEPTH, D])
    w1_d = din("w_mlp_in", [DEPTH, D, 4 * D])
    w2_d = din("w_mlp_out", [DEPTH, 4 * D, D])
    wg_d = din("w_ple_gate", [DEPTH, D, D])
    wp_d = din("w_ple_proj", [DEPTH, 256, D])
    bias_d = din("bias_tab", [128, 5 * 8 * 128])
    mask_d = din("mask_tab", [128, 5 * 128])
    ident_d = din("ident", [128, 128])
    caus_d = din("caus", [128, 128])
    out_d = nc.dram_tensor("out", [SEQ, D], F32, kind="ExternalOutput").ap()

    sb = nc.alloc_sbuf_tensor
    h = sb("h", [128, 16, D], F32)
    ident = sb("ident_sb", [128, 128], BF16)
    caus = sb("caus_sb", [128, 128], F32)
    onesf = sb("onesf", [128, 128], F32)
    gcols = sb("gcols", [128, DEPTH, 2, 8], F32)
    negb = sb("negb", [128, DEPTH, 2], F32)
    gng = sb("gng", [128, DEPTH], F32)
    ss = sb("ss", [128, 16], F32)
    sq = sb("sq", [128, 16], F32)
    rstd = sb("rstd", [128, 16], F32)
    ssn = sb("ssn", [128, 16], F32)
    sqn = sb("sqn", [128, 16], F32)
    rstdn = sb("rstdn", [128, 16], F32)
    junk = sb("junk", [128, D], BF16)
    xsb = sb("xsb", [128, 2, D], BF16)
    xs = [xsb[:, j, :] for j in range(2)]
    tmp4 = sb("tmp4", [128, D], F32)
    gbc = sb("gbc", [128, 1, D], F32)
    wbA = sb("wbA", [128, 8, 512], BF16)
    wbB = sb("wbB", [128, 8, 512], BF16)
    KEYS = {"wbA": ["wbA"], "wbB": ["wbB", "wbB0", "wbB1"]}

    def WK(k):
        return KEYS[k]

    PP = [nc.alloc_psum_tensor("pp%d" % i, [128, 1024], F32) for i in range(4)]

    def bank(i):
        return PP[i // 2][:, (i % 2) * 512:(i % 2) * 512 + 512]

    def bkey(i):
        return ("bank", i)

    with nc.allow_non_contiguous_dma(reason="small constant loads"):
        S.dma('pool', ident[:, :], ident_d, w=["ident"])
        S.dma('sp', caus[:, :], caus_d, w=["caus"])
        for l in range(n_layers):
            S.dma('sp', gcols[:, l, 0, :], pre_mix_d[l].rearrange("(dc p) -> p dc", p=128), w=["gcols"])
            S.dma('sp', gcols[:, l, 1, :], pre_mlp_d[l].rearrange("(dc p) -> p dc", p=128), w=["gcols"])
            S.dma('sp', negb[:, l, :], b_a_d[l].rearrange("(c p) -> p c", p=128), w=["negb"])
            S.dma('sp', gng[:, l:l + 1], gng_d[l].rearrange("(p o) -> p o", o=1), w=["gng"])
    S.op('dve', lambda: nc.vector.memset(onesf[:, :], 1.0), w=["onesf"])
    S.op('dve', lambda: nc.vector.tensor_scalar(out=negb[:, :, :], in0=negb[:, :, :], scalar1=-1.0, scalar2=None,
                                                op0=ALU.mult), r=["negb"], w=["negb"])

    xv = x_d.rearrange("(t p) d -> p t d", p=128)
    for g in range(8):
        S.dma('sp', h[:, 2 * g:2 * g + 2, :], xv[:, 2 * g:2 * g + 2, :], w=[("h", t) for t in range(2 * g, 2 * g + 2)])

    def norm_T(dstT, dkey, t0, nt, gcol, do_norm, war=(), dbase=None):
        if dbase is None:
            dbase = t0
        if do_norm:
            for t in range(t0, t0 + nt):
                S.op('act', lambda: nc.scalar.activation(out=junk[:, :], in_=h[:, t, :], func=AF.Square,
                                                         accum_out=ssn[:, t:t + 1]), r=[("h", t)], w=[("ssn", t), "junk"])
            S.op('act', lambda: nc.scalar.activation(out=sqn[:, t0:t0 + nt], in_=ssn[:, t0:t0 + nt], func=AF.Sqrt,
                                                     scale=1.0 / D, bias=epsc[:, 0:1]),
                 r=[("ssn", t) for t in range(t0, t0 + nt)] + ["epsc"], w=[("sq", t0)])
            S.op('dve', lambda: nc.vector.reciprocal(out=rstdn[:, t0:t0 + nt], in_=sqn[:, t0:t0 + nt]), r=[("sq", t0)], w=[("rstdg", t0)])
        for t in range(t0, t0 + nt):
            j = t % 2
            if do_norm:
                S.op('act', lambda: nc.scalar.mul(out=xs[j], in_=h[:, t, :], mul=rstdn[:, t:t + 1]),
                     r=[("h", t), ("rstdg", t0)], w=[("xs", j)])
            else:
                S.op('act', lambda: nc.scalar.copy(out=xs[j], in_=h[:, t, :]), r=[("h", t)], w=[("xs", j)])
            pt = bank(6 + j).bitcast(BF16).rearrange("p (a b) -> p a b", b=128)
            for dc in range(8):
                S.op('pe', lambda: nc.tensor.transpose(out=pt[:, dc, :], in_=xs[j][:, dc * 128:(dc + 1) * 128],
                                                       identity=ident[:, :]),
                     r=[("xs", j), "ident"], w=[bkey(6 + j)], inc=(dc == 7))
            dst = dstT[:, :, (t - dbase) * 128:(t - dbase + 1) * 128]
            if gcol is not None:
                S.op('dve', lambda: nc.vector.tensor_tensor(out=dst, in0=pt, in1=gcol.unsqueeze(2).to_broadcast([128, 8, 128]),
                                                            op=ALU.mult), r=[bkey(6 + j), "gcols"], w=[(dkey, t - dbase)])
            else:
                S.op('dve', lambda: nc.vector.tensor_copy(out=dst, in_=pt), r=[bkey(6 + j)], w=[(dkey, t - dbase)], war=war)

    epsc = sb("epsc", [128, 2], F32)
    S.op('dve', lambda: nc.vector.memset(epsc[:, 0:1], EPS), w=["epsc"])
    S.op('dve', lambda: nc.vector.memset(epsc[:, 1:2], 1.0), w=["epsc"])

    def load_w(wb, wkey, src_ap, ncols, c0=0):
        kc = src_ap.shape[0] // 128
        return S.dma('pool', wb[:, 0:kc, c0:c0 + ncols], src_ap.rearrange("(kc p) c -> p kc c", p=128),
                     w=(WK(wkey) if wkey in KEYS else [wkey]))

    SPLIT = [False]

    def mm_acc(ps, pskey, lhs_list, rhs_list, rkeys):
        n = len(lhs_list)
        wide = SPLIT[0] and len(rhs_list[0].shape) == 2 and rhs_list[0].shape[-1] == 512
        for i in range(n):
            if wide:
                for ch in range(4):
                    cs = slice(ch * 128, (ch + 1) * 128)
                    last = (i == n - 1 and ch == 3)
                    S.op('pe', lambda: nc.tensor.matmul(ps[:, cs], lhsT=lhs_list[i], rhs=rhs_list[i][:, cs],
                                                        start=(i == 0 and ch == 0), stop=last, skip_group_check=True),
                         r=rkeys, w=[pskey], inc=last)
            else:
                S.op('pe', lambda: nc.tensor.matmul(ps, lhsT=lhs_list[i], rhs=rhs_list[i], start=(i == 0), stop=(i == n - 1)),
                     r=rkeys, w=[pskey], inc=(i == n - 1))

    def resid_epilogue(src_ap, srckeys, t, gkey, src_is_psum, add_eng='pool'):
        S.op('act', lambda: nc.scalar.activation(out=junk[:, :], in_=src_ap, func=AF.Square, accum_out=ss[:, t:t + 1]),
             r=srckeys, w=[("ss", t), "junk"])
        S.op('act', lambda: nc.scalar.activation(out=sq[:, t:t + 1], in_=ss[:, t:t + 1], func=AF.Sqrt, scale=1.0 / D,
                                                 bias=epsc[:, 0:1]), r=[("ss", t), "epsc"], w=[("sq", t)])
        S.op('dve', lambda: nc.vector.reciprocal(out=rstd[:, t:t + 1], in_=sq[:, t:t + 1]), r=[("sq", t)], w=[("rstd", t)])
        S.op('dve', lambda: nc.vector.scalar_tensor_tensor(out=tmp4[:, :], in0=src_ap, scalar=rstd[:, t:t + 1],
                                                           in1=gbc[:, 0, :], op0=ALU.mult, op1=ALU.mult),
             r=srckeys + [("rstd", t), gkey], w=["tmp4", ("sqr", 0), ("sqr", 1)])
        S.op(add_eng, lambda: (nc.gpsimd if add_eng == 'pool' else nc.vector).tensor_tensor(out=h[:, t, :], in0=h[:, t, :], in1=tmp4[:, :], op=ALU.add),
             r=["tmp4", ("sqr", 0), ("sqr", 1), ("h", t)], w=[("h", t)])

    def gla_prefetch(l):
        evs = [load_w(wbB, "wbB1", w_in_d[l][:, 1536:1664], 128, c0=256),
               load_w(wbB, "wbB0", w_in_d[l][:, 0:128], 128, c0=0),
               load_w(wbB, "wbB0", w_in_d[l][:, 256:384], 128, c0=128),
               load_w(wbA, "wbA", w_in_d[l][:, 512:1024], 512)]
        return evs

    pre_evs = gla_prefetch(0)
    for l in range(n_layers):
        w_in_l = w_in_d[l]
        with ExitStack() as es1:
            xnT = es1.enter_context(nc.sbuf_tensor("xnT_%d" % l, [128, 8, SEQ], BF16))
            mixT = es1.enter_context(nc.sbuf_tensor("mixT_%d" % l, [128, 8, SEQ], BF16))
            for g_ in range(4):
                norm_T(xnT, "xnT", 4 * g_, 4, gcols[:, l, 0, :], True, dbase=0)
            if l > 0:
                S.barrier(exclude=pre_evs, engines=('act', 'dve', 'sp', 'pool'))
            xkeys = [("xnT", t) for t in range(16)]

            def xk(tok0, ntok):
                return [("xnT", t) for t in range(tok0 // 128, (tok0 + ntok + 127) // 128)]

            with ExitStack() as es2:
                def A2(name, shape, dt):
                    return es2.enter_context(nc.sbuf_tensor("%s_%d" % (name, l), shape, dt))
                glrT = A2("glrT", [128, SEQ], BF16)
                wa2 = A2("wa2", [128, 256], BF16)
                qgz = A2("qgz", [128, 2, SEQ], BF16)
                kdT = A2("kdT", [128, SEQ], BF16)
                kg = A2("kg", [128, 16, 128], BF16)
                gv = A2("gv", [128, 16, 256], BF16)
                alast = A2("alast", [128, 16], F32)
                scr = A2("scr", [128, 2, 3, 512], F32)
                kgT = A2("kgT", [128, 2, 512], BF16)
                Sf = A2("Sf", [128, 256], F32)
                Sb = A2("Sb", [128, 3, 256], BF16)
                attm = A2("attm", [128, 2, 2, 128], BF16)

                def drive(gens, width):
                    gens = list(gens)
                    active = []
                    while gens or active:
                        while len(active) < width and gens:
                            active.append(gens.pop(0))
                        for g_ in list(active):
                            try:
                                next(g_)
                            except StopIteration:
                                active.remove(g_)

                S.op('dve', lambda: nc.vector.memset(wa2[:, :], 0.0), w=["wa2"])
                S.op('dve', lambda: nc.vector.memset(qgz[:, :, :], 0.0), w=["qgz"])
                S.dma('pool', wa2[0:16, :], w_a2_d[l], r=[], w=["wa2"])
                for tg in range(4):
                    b = tg % 2
                    mm_acc(bank(b), bkey(b), [wbB[:, dc, 256:384] for dc in range(8)],
                           [xnT[:, dc, tg * 512:(tg + 1) * 512] for dc in range(8)], ["wbB1"] + xk(tg * 512, 512))
                    S.op('act', lambda: nc.scalar.copy(out=glrT[:, tg * 512:(tg + 1) * 512], in_=bank(b)),
                         r=[bkey(b)], w=[("glrT", tg)])

                def prep_gen(pr, tg):
                    s_ = tg % 2
                    t1, cc, Eq = scr[:, s_, 0, :], scr[:, s_, 1, :], scr[:, s_, 2, :]
                    cc3 = cc.rearrange("p (a b) -> p a b", b=128)
                    k1, kc, kq, kk = ("t1", s_), ("cc", s_), ("Eq", s_), ("kgT", s_)
                    tsl = slice(tg * 512, (tg + 1) * 512)
                    bx, bq, bk_ = s_, 2 + s_, 4 + s_
                    S.op('pe', lambda: nc.tensor.matmul(bank(bx), lhsT=wa2[:, pr * 128:(pr + 1) * 128], rhs=glrT[:, tsl],
                                                        start=True, stop=True), r=["wa2", ("glrT", tg)], w=[bkey(bx)])
                    yield
                    mm_acc(bank(bq), bkey(bq), [wbB[:, dc, 0:128] for dc in range(8)],
                           [xnT[:, dc, tsl] for dc in range(8)], ["wbB0"] + xk(tg * 512, 512))
                    yield
                    mm_acc(bank(bk_), bkey(bk_), [wbB[:, dc, 128:256] for dc in range(8)],
                           [xnT[:, dc, tsl] for dc in range(8)], ["wbB0"] + xk(tg * 512, 512))
                    yield
                    S.op('act', lambda: nc.scalar.activation(out=t1, in_=bank(bx), func=AF.Exp, scale=-1.0,
                                                             bias=negb[:, l, pr:pr + 1]), r=[bkey(bx), "negb"], w=[k1])
                    yield
                    S.op('act', lambda: nc.scalar.activation(out=t1, in_=t1, func=AF.Ln, scale=1.0, bias=epsc[:, 1:2]),
                         r=[k1, "epsc"], w=[k1])
                    yield
                    for ch in range(4):
                        S.op('dve', lambda: nc.vector.tensor_tensor_scan(
                            out=cc3[:, ch, :], data0=onesf[:, :], data1=t1[:, ch * 128:(ch + 1) * 128], initial=0.0,
                            op0=ALU.mult, op1=ALU.add), r=[k1, "onesf"], w=[kc])
                    yield
                    S.op('act', lambda: nc.scalar.activation(out=Eq, in_=cc, func=AF.Exp, scale=-1.0 / 16.0), r=[kc], w=[kq])
                    yield
                    S.op('dve', lambda: nc.vector.tensor_tensor(
                        out=t1.rearrange("p (a b) -> p a b", b=128), in0=cc3, in1=cc3[:, :, 127:128].to_broadcast([128, 4, 128]),
                        op=ALU.subtract), r=[kc], w=[k1])
                    yield
                    S.op('act', lambda: nc.scalar.activation(out=t1, in_=t1, func=AF.Exp, scale=1.0 / 16.0), r=[k1], w=[k1])
                    yield
                    S.op('act', lambda: nc.scalar.activation(out=cc, in_=cc, func=AF.Exp, scale=1.0 / 16.0), r=[kc], w=[kc])
                    yield
                    S.op('act', lambda: nc.scalar.copy(out=alast[:, tg * 4:(tg + 1) * 4],
                                                       in_=Eq.rearrange("p (a b) -> p a b", b=128)[:, :, 127]), r=[kq], w=[("alast", tg)])
                    yield
                    for hh in range(2):
                        ps_ = slice(hh * 64, hh * 64 + 64)
                        S.op('dve', lambda: nc.vector.scalar_tensor_tensor(
                            out=qgz[ps_, hh, tsl], in0=bank(bq)[ps_, :], scalar=0.125, in1=Eq[ps_, :],
                            op0=ALU.mult, op1=ALU.mult), r=[bkey(bq), kq], w=[("qgz", tg)])
                    yield
                    S.op('dve', lambda: nc.vector.tensor_tensor(out=kdT[:, tsl], in0=bank(bk_), in1=cc, op=ALU.mult),
                         r=[bkey(bk_), kc], w=[("kdT", tg)])
                    yield
                    S.op('dve', lambda: nc.vector.tensor_tensor(out=kgT[:, s_, :], in0=bank(bk_), in1=t1, op=ALU.mult),
                         r=[bkey(bk_), k1], w=[kk])
                    yield
                    pt = bank(6).bitcast(BF16).rearrange("p (a b) -> p a b", b=128)
                    for ch in range(4):
                        S.op('pe', lambda: nc.tensor.transpose(out=pt[:, ch, :], in_=kgT[:, s_, ch * 128:(ch + 1) * 128],
                                                               identity=ident[:, :]), r=[kk, "ident"], w=[bkey(6)], inc=(ch == 3))
                    S.op('act', lambda: nc.scalar.copy(out=kg[:, tg * 4:(tg + 1) * 4, :], in_=pt[:, 0:4, :]),
                         r=[bkey(6)], w=[("kg", tg)])
                    yield

                def gv_gen(pr):
                    for t in range(16):
                        mm_acc(bank(7)[:, 0:256], bkey(7), [xnT[:, dc, t * 128:(t + 1) * 128] for dc in range(8)],
                               [wbA[:, dc, pr * 256:(pr + 1) * 256] for dc in range(8)], WK("wbA") + [("xnT", t)])
                        yield
                        S.op('act', lambda: nc.scalar.copy(out=gv[:, t, :], in_=bank(7)[:, 0:256]), r=[bkey(7)], w=[("gv", t)])
                        yield
                        yield

                def epi_ops(pr, tg, hh):
                    hd = 2 * pr + hh
                    osb, osq, sgt = scr[:, hh, 0, :], scr[:, hh, 1, :], scr[:, hh, 2, :]
                    ko, kq2, kg2 = ("t1", hh), ("cc", hh), ("Eq", hh)
                    tsl = slice(tg * 512, (tg + 1) * 512)
                    ops = []
                    ops.append(lambda: S.op('act', lambda: nc.scalar.copy(out=osb, in_=bank(4 + hh)), r=[bkey(4 + hh)], w=[ko]))
                    ops.append(lambda: S.op('act', lambda: nc.scalar.activation(out=osq, in_=osb, func=AF.Square), r=[ko], w=[kq2]))
                    ops.append(lambda: S.op('pe', lambda: nc.tensor.matmul(bank(hh), lhsT=onesf[:, :], rhs=osq, start=True, stop=True),
                                            r=["onesf", kq2], w=[bkey(hh)]))
                    ops.append(lambda: S.op('act', lambda: nc.scalar.activation(out=osq, in_=bank(hh), func=AF.Ln, scale=1.0 / 128.0,
                                                                             bias=epsc[:, 0:1]), r=[bkey(hh), "epsc"], w=[kq2]))
                    ops.append(lambda: S.op('act', lambda: nc.scalar.activation(out=osq, in_=osq, func=AF.Exp, scale=-0.5), r=[kq2], w=[kq2]))
                    ops.append(lambda: S.op('dve', lambda: nc.vector.tensor_tensor(out=osb, in0=osb, in1=osq, op=ALU.mult),
                                            r=[ko, kq2], w=[ko]))
                    ops.append(lambda: mm_acc(bank(hh), bkey(hh), [wbB[:, dc, (2 + hh) * 128:(3 + hh) * 128] for dc in range(8)],
                                              [xnT[:, dc, tsl] for dc in range(8)], ["wbB1"] + xk(tg * 512, 512)))
                    ops.append(lambda: S.op('act', lambda: nc.scalar.activation(out=sgt, in_=bank(hh), func=AF.Silu), r=[bkey(hh)], w=[kg2]))
                    ops.append(lambda: S.op('dve', lambda: nc.vector.scalar_tensor_tensor(
                        out=mixT[:, hd, tsl], in0=osb, scalar=gng[:, l:l + 1], in1=sgt, op0=ALU.mult, op1=ALU.mult),
                        r=[ko, kg2, "gng"], w=[("mixT", hd, tg)]))
                    return ops

                for pr in range(2):
                    for hh_ in range(2):
                        load_w(wbB, "wbB1", w_in_l[:, 1024 + (2 * pr + hh_) * 128:1024 + (2 * pr + hh_ + 1) * 128], 128, c0=256 + hh_ * 128)
                    drive([prep_gen(pr, 0), prep_gen(pr, 1), gv_gen(pr), prep_gen(pr, 2), prep_gen(pr, 3)], 3)
                    if pr == 0:
                        load_w(wbB, "wbB0", w_in_l[:, 128:256], 128, c0=0)
                        load_w(wbB, "wbB0", w_in_l[:, 384:512], 128, c0=128)

                    S.op('dve', lambda: nc.vector.memset(Sf[:, :], 0.0), w=["Sf"])
                    S.op('dve', lambda: nc.vector.memset(Sb[:, 0, :], 0.0), w=[("Sb", 0)])
                    pend = []

                    def drip(k=1):
                        for _ in range(k):
                            if pend:
                                pend.pop(0)()

                    def stage_A(n):
                        tg = n // 4
                        csl = slice(n * 128, (n + 1) * 128)
                        a2 = n % 2
                        ab = 2 + a2
                        kb = 6 + a2
                        for hh in range(2):
                            S.op('pe', lambda: nc.tensor.matmul(bank(ab)[:, hh * 128:(hh + 1) * 128], lhsT=kdT[:, csl],
                                                                rhs=qgz[:, hh, csl], start=True, stop=True, skip_group_check=True),
                                 r=[("kdT", tg), ("qgz", tg)], w=[bkey(ab)], inc=(hh == 1))
                        S.op('pe', lambda: nc.tensor.matmul(bank(kb)[:, 0:256], lhsT=kg[:, n, :], rhs=gv[:, n, :], start=True, stop=True),
                             r=[("kg", tg), ("gv", n)], w=[bkey(kb)])
                        S.op('dve', lambda: nc.vector.tensor_tensor(
                            out=attm[:, a2, :, :], in0=bank(ab)[:, 0:256].rearrange("p (a b) -> p a b", b=128),
                            in1=caus[:, :].unsqueeze(1).to_broadcast([128, 2, 128]), op=ALU.mult),
                            r=[bkey(ab), "caus"], w=[("attm", a2)])
                        S.op('dve', lambda: nc.vector.scalar_tensor_tensor(
                            out=Sf[:, :], in0=Sf[:, :], scalar=alast[:, n:n + 1], in1=bank(kb)[:, 0:256],
                            op0=ALU.mult, op1=ALU.add), r=["Sf", bkey(kb), ("alast", tg)], w=["Sf"])
                        S.op('act', lambda: nc.scalar.copy(out=Sb[:, (n + 1) % 3, :], in_=Sf[:, :]), r=["Sf"], w=[("Sb", (n + 1) % 3)])

                    def stage_B(n):
                        tg = n // 4
                        csl = slice(n * 128, (n + 1) * 128)
                        a2 = n % 2
                        for hh in range(2):
                            ob = bank(4 + hh)[:, (n % 4) * 128:(n % 4 + 1) * 128]
                            S.op('pe', lambda: nc.tensor.matmul(ob, lhsT=gv[:, n, hh * 128:(hh + 1) * 128], rhs=attm[:, a2, hh, :],
                                                                start=True, stop=False, skip_group_check=True),
                                 r=[("gv", n), ("attm", a2)], w=[bkey(4 + hh)], inc=False)
                            S.op('pe', lambda: nc.tensor.matmul(ob, lhsT=Sb[:, n % 3, hh * 128:(hh + 1) * 128], rhs=qgz[:, hh, csl],
                                                                start=False, stop=True, skip_group_check=True),
                                 r=[("Sb", n % 3), ("qgz", tg)], w=[bkey(4 + hh)])

                    stage_A(0)
                    for n in range(16):
                        if n + 1 < 16:
                            stage_A(n + 1)
                        drip(2)
                        stage_B(n)
                        drip(2)
                        if n % 4 == 3:
                            drip(len(pend))
                            e0, e1 = epi_ops(pr, n // 4, 0), epi_ops(pr, n // 4, 1)
                            e0[0]()
                            e1[0]()
                            for x0, x1 in zip(e0[1:], e1[1:]):
                                pend.append(x0)
                                pend.append(x1)
                    drip(len(pend))
                S.barrier(engines=('act', 'dve', 'sp'))

            with ExitStack() as es3:
                qz = es3.enter_context(nc.sbuf_tensor("qz_%d" % l, [128, 2, SEQ], BF16))
                kT = es3.enter_context(nc.sbuf_tensor("kT_%d" % l, [128, SEQ], BF16))
                Vn = es3.enter_context(nc.sbuf_tensor("Vn_%d" % l, [128, 3, 16, 192], BF16))
                ebf = es3.enter_context(nc.sbuf_tensor("ebf_%d" % l, [128, 3, 512], BF16))
                acc = es3.enter_context(nc.sbuf_tensor("acc_%d" % l, [128, 2, 512], F32))
                rs = es3.enter_context(nc.sbuf_tensor("rs_%d" % l, [128, 512], F32))
                biasM = es3.enter_context(nc.sbuf_tensor("biasM_%d" % l, [128, 5, 2, 128], BF16))
                madd = es3.enter_context(nc.sbuf_tensor("madd_%d" % l, [128, 5, 128], F32))
                et = es3.enter_context(nc.sbuf_tensor("et_%d" % l, [128, 2, 128], F32))
                S.dma('sp', madd[:, :, :], mask_d.rearrange("p (k q) -> p k q", q=128), w=["madd"])
                S.op('dve', lambda: nc.vector.tensor_scalar(out=madd[:, :, :], in0=madd[:, :, :], scalar1=-1.0, scalar2=30000.0,
                                                            op0=ALU.add, op1=ALU.mult), r=["madd"], w=["madd"])
                S.op('dve', lambda: nc.vector.memset(qz[:, :, :], 0.0), w=["qz"])
                S.op('dve', lambda: nc.vector.memset(Vn[:, :, :, 64:128], 1.0), w=["Vn"])
                for pr in range(4):
                    wb = wbA if pr % 2 == 0 else wbB
                    wk = "wbA" if pr % 2 == 0 else "wbB"
                    bv_ = bias_d.rearrange("p (k h q) -> p k h q", k=5, h=8)
                    for kd in range(5):
                        S.dma('sp', et[:, :, :], bv_[:, kd, 2 * pr:2 * pr + 2, :], w=["et"])
                        S.op('dve', lambda: nc.vector.tensor_tensor(
                            out=biasM[:, kd, :, :], in0=et[:, :, :], in1=madd[:, kd, :].unsqueeze(1).to_broadcast([128, 2, 128]),
                            op=ALU.add), r=["et", "madd"], w=["biasM"])
                    for i, c0 in enumerate([1552, 2064, 2576]):
                        S.dma('pool', wb[:, :, i * 128:(i + 1) * 128],
                              w_in_l[:, c0 + pr * 128:c0 + (pr + 1) * 128].rearrange("(kc p) c -> p kc c", p=128), w=WK(wk))
                    for tg in range(4):
                        tsl = slice(tg * 512, (tg + 1) * 512)
                        mm_acc(bank(5), bkey(5), [wb[:, dc, 0:128] for dc in range(8)], [xnT[:, dc, tsl] for dc in range(8)],
                               WK(wk) + xk(tg * 512, 512))
                        for hh in range(2):
                            ps_ = slice(hh * 64, hh * 64 + 64)
                            S.op('act', lambda: nc.scalar.mul(out=qz[ps_, hh, tsl], in_=bank(5)[ps_, :], mul=0.125),
                                 r=[bkey(5)], w=["qz"])
                        mm_acc(bank(6), bkey(6), [wb[:, dc, 128:256] for dc in range(8)], [xnT[:, dc, tsl] for dc in range(8)],
                               WK(wk) + xk(tg * 512, 512))
                        S.op('act', lambda: nc.scalar.copy(out=kT[:, tsl], in_=bank(6)), r=[bkey(6)], w=["kT"])
                    vT = xsb[:, :, :].rearrange("p a b -> p (a b)")
                    for tg in range(4):
                        tsl = slice(tg * 512, (tg + 1) * 512)
                        b = 5 + (tg % 2)
                        mm_acc(bank(b), bkey(b), [wb[:, dc, 256:384] for dc in range(8)], [xnT[:, dc, tsl] for dc in range(8)],
                               WK(wk) + xk(tg * 512, 512))
                        S.op('act', lambda: nc.scalar.copy(out=vT[:, tsl], in_=bank(b)), r=[bkey(b)], w=[("xs", tg // 2)])
                    for lay in range(3):
                        for g8 in range(2):
                            b = 5 + ((lay * 2 + g8) % 2)
                            ptb = bank(b).bitcast(BF16).rearrange("p (a b) -> p a b", b=128)
                            for tt in range(8):
                                ti = g8 * 8 + tt
                                if lay == 0:
                                    tok = slice(ti * 128, (ti + 1) * 128)
                                elif lay == 1:
                                    c_, r_ = ti // 4, ti % 4
                                    tok = slice(512 * c_ + r_, 512 * c_ + 512, 4)
                                else:
                                    tok = slice(ti, SEQ, 16)
                                S.op('pe', lambda: nc.tensor.transpose(out=ptb[:, tt, :], in_=vT[:, tok], identity=ident[:, :]),
                                     r=[("xs", 0), ("xs", 1), "ident"], w=[bkey(b)], inc=(tt == 7))
                            evac_eng = 'act' if (lay * 2 + g8) % 2 == 0 else 'dve'
                            dstv = Vn[:, lay, g8 * 8:(g8 + 1) * 8, :].rearrange("p t (h e) -> p t h e", e=64)[:, :, 0:3:2, :]
                            srcv = ptb.rearrange("p t (h e) -> p t h e", e=64)
                            if evac_eng == 'act':
                                S.op('act', lambda: nc.scalar.copy(out=dstv, in_=srcv), r=[bkey(b)], w=["Vn"])
                            else:
                                S.op('dve', lambda: nc.vector.tensor_copy(out=dstv, in_=srcv), r=[bkey(b)], w=["Vn"])
                    if pr == 3:
                        load_w(wbA, "wbA", w_out_d[l][:, 0:512], 512)
                        load_w(wbB, "wbB", w_out_d[l][:, 512:1024], 512)
                    stages = []
                    for hh in range(2):
                        for c in range(4):
                            g = (hh * 4 + c)
                            u1b, u4b, u16b = (2, 3, 4) if g % 2 == 0 else (5, 6, 4)
                            cur1, prev1, cur4, prev4, d16 = [], [], [], [], []
                            for n2 in range(4):
                                nb = 4 * c + n2
                                qs = slice(nb * 128, (nb + 1) * 128)
                                cur1.append((qs, qs, (0, nb), n2 * 128, 128))
                                if nb > 0:
                                    prev1.append((slice((nb - 1) * 128, nb * 128), qs, (0, nb - 1), n2 * 128, 128))
                            for r_ in range(4):
                                qs = slice(512 * c + r_, 512 * c + 512, 4)
                                cur4.append((qs, qs, (1, 4 * c + r_), r_ * 128, 128))
                                if c > 0:
                                    prev4.append((slice(512 * (c - 1) + r_, 512 * c, 4), qs, (1, 4 * (c - 1) + r_), r_ * 128, 128))
                            for r16 in range(16):
                                d16.append((slice(r16, SEQ, 16), slice(512 * c + r16, 512 * c + 512, 16), (2, r16), r16 * 32, 32))
                            grp = [(1, u1b, cur1), (0, u1b, prev1), (3, u4b, cur4), (2, u4b, prev4), (4, u16b, d16)]
                            grp = [x for x in grp if x[2]]
                            started = set()
                            for gi, (kind, ub, items) in enumerate(grp):
                                first = ub not in started
                                started.add(ub)
                                stages.append(dict(hh=hh, c=c, kind=kind, ub=ub, items=items, first=first,
                                                   last=(gi == len(grp) - 1), ubs=(u1b, u4b, u16b)))

                    def emit_L(i, st):
                        lb = (0, 1, 7)[i % 3]
                        es_ = i % 3
                        hh, c, kind, items = st['hh'], st['c'], st['kind'], st['items']
                        lo = items[0][3]
                        if kind == 4:
                            bv = biasM[:, 4, hh, 32 * c:32 * c + 32].unsqueeze(1).to_broadcast([128, 16, 32])
                        else:
                            bv = biasM[:, kind, hh, :].unsqueeze(1).to_broadcast([128, (512 - lo) // 128, 128])
                        S.op('pe', lambda: nc.tensor.matmul(bank(lb)[:, lo:512], lhsT=ident[:, :], rhs=bv, start=True, stop=False,
                                                            skip_group_check=True), r=["ident", "biasM"], w=[bkey(lb)], inc=False)
                        for (ks, qs, vt, c0, ncol) in items:
                            S.op('pe', lambda: nc.tensor.matmul(bank(lb)[:, c0:c0 + ncol], lhsT=kT[:, ks], rhs=qz[:, hh, qs],
                                                                start=False, stop=(c0 + ncol == 512), skip_group_check=True),
                                 r=["kT", "qz"], w=[bkey(lb)], inc=(c0 + ncol == 512))
                        S.op('act', lambda: nc.scalar.activation(out=ebf[:, es_, lo:512], in_=bank(lb)[:, lo:512], func=AF.Exp),
                             r=[bkey(lb)], w=[("ebf", es_)])

                    def emit_PV(i, st):
                        lb = (0, 1, 7)[i % 3]
                        es_ = i % 3
                        hh, c, ub, items = st['hh'], st['c'], st['ub'], st['items']
                        for ii, (ks, qs, vt, c0, ncol) in enumerate(items):
                            S.op('pe', lambda: nc.tensor.matmul(bank(ub)[:, c0:c0 + ncol], lhsT=Vn[:, vt[0], vt[1], 64 * hh:64 * hh + 128],
                                                                rhs=ebf[:, es_, c0:c0 + ncol], start=(st['first'] and ii == 0), stop=False,
                                                                skip_group_check=True),
                                 r=["Vn", ("ebf", es_)], w=[bkey(ub)], inc=(ii == len(items) - 1))
                        if st['last']:
                            u1_, u4_, u16_ = st['ubs']
                            a_ = acc[:, (hh * 4 + c) % 2, :]
                            ak = ("acc", (hh * 4 + c) % 2)
                            S.op('dve', lambda: nc.vector.tensor_copy(out=a_, in_=bank(u1_)), r=[bkey(u1_)], w=[ak])
                            S.op('dve', lambda: nc.vector.tensor_tensor(
                                out=a_.rearrange("p (i r) -> p i r", r=4), in0=bank(u4_).rearrange("p (r i) -> p i r", r=4),
                                in1=a_.rearrange("p (i r) -> p i r", r=4), op=ALU.add), r=[bkey(u4_), ak], w=[ak])
                            S.op('dve', lambda: nc.vector.tensor_tensor(
                                out=a_.rearrange("p (i r) -> p i r", r=16), in0=bank(u16_).rearrange("p (r i) -> p i r", r=16),
                                in1=a_.rearrange("p (i r) -> p i r", r=16), op=ALU.add), r=[bkey(u16_), ak], w=[ak])
                            up = slice(64 * hh, 64 * hh + 64)
                            sp_ = slice(64 * (1 - hh), 64 * (1 - hh) + 64)

                            def fin():
                                S.op('act', lambda: nc.scalar.activation(out=rs[up, :], in_=a_[sp_, :], func=AF.Ln), r=[ak], w=["rs"])
                                S.op('act', lambda: nc.scalar.activation(out=rs[up, :], in_=rs[up, :], func=AF.Exp, scale=-1.0),
                                     r=["rs"], w=["rs"])
                                S.op('dve', lambda: nc.vector.tensor_tensor(
                                    out=mixT[up, 4 + pr, c * 512:(c + 1) * 512], in0=a_[up, :], in1=rs[up, :],
                                    op=ALU.mult), r=[ak, "rs"], w=[("mixT", 4 + pr, c, hh)])
                            pend_n.append([i + 4, fin])

                    pend_n = []
                    for i, st in enumerate(stages):
                        emit_L(i, st)
                        while pend_n and pend_n[0][0] <= i:
                            pend_n.pop(0)[1]()
                        if i >= 2:
                            emit_PV(i - 2, stages[i - 2])
                    emit_PV(len(stages) - 2, stages[-2])
                    emit_PV(len(stages) - 1, stages[-1])
                    while pend_n:
                        pend_n.pop(0)[1]()

            S.fence([("x2T", t_) for t_ in range(8)] + ["hidT", ("fbuf", 0), ("fbuf", 1)])
            with nc.allow_non_contiguous_dma(reason="gain broadcast"):
                S.dma('sp', gbc[:, :, :], post_mix_d[l:l + 1, :].partition_broadcast(128), w=["gbc"])
            for t in range(16):
                pp = PP[t % 2]
                pks = [bkey(2 * (t % 2)), bkey(2 * (t % 2) + 1)]
                mkeys = [("mixT", hd_, t // 4) for hd_ in range(4)] + [("mixT", 4 + p_, t // 4, h_) for p_ in range(4) for h_ in range(2)]
                for nb_, (wb, wk) in enumerate([(wbA, "wbA"), (wbB, "wbB")]):
                    for kc in range(8):
                        S.op('pe', lambda: nc.tensor.matmul(pp[:, nb_ * 512:(nb_ + 1) * 512], lhsT=mixT[:, kc, t * 128:(t + 1) * 128],
                                                            rhs=wb[:, kc, :], start=(kc == 0), stop=(kc == 7)),
                             r=WK(wk) + mkeys, w=pks, inc=(kc == 7))
                resid_epilogue(pp[:, :], pks, t, "gbc", True, add_eng=('pool' if t % 2 == 0 else 'dve'))
            pre_evs = [load_w(wbA, "wbA", w1_d[l][:, 0:512], 512), load_w(wbB, "wbB", w1_d[l][:, 512:1024], 512)]
            mlp_fence = True

        with ExitStack() as es4:
            hidT = es4.enter_context(nc.sbuf_tensor("hidT_%d" % l, [128, 32, 1024], BF16))
            fbuf = es4.enter_context(nc.sbuf_tensor("fbuf_%d" % l, [128, 8, D], F32))
            x2T = es4.enter_context(nc.sbuf_tensor("x2T_%d" % l, [128, 8, 1024], BF16))
            sqr = tmp4[:, :].rearrange("p (a b) -> p a b", b=512)
            wbs = [(wbA, "wbA"), (wbB, "wbB")]
            with nc.allow_non_contiguous_dma(reason="gain broadcast"):
                S.dma('sp', gbc[:, :, :], post_mlp_d[l:l + 1, :].partition_broadcast(128), w=["gbc"])
            x2keys = [("x2T", t) for t in range(8)]
            wi = [0]

            def mm1(hf, after_fb=None):
                for fb in range(8):
                    wb, wk = wbs[wi[0] % 2]
                    wi[0] += 1
                    if not (hf == 0 and fb < 2):
                        load_w(wb, wk, w1_d[l][:, fb * 512:(fb + 1) * 512], 512)
                    for fc in range(4):
                        for tg in range(2):
                            b = (fc * 2 + tg) % 4
                            mm_acc(bank(b), bkey(b), [wb[:, dc, fc * 128:(fc + 1) * 128] for dc in range(8)],
                                   [x2T[:, dc, tg * 512:(tg + 1) * 512] for dc in range(8)], WK(wk) + x2keys[4 * tg:4 * tg + 4])
                            s2 = b % 2
                            S.op('act', lambda: nc.scalar.activation(out=sqr[:, s2, :], in_=bank(b), func=AF.Square),
                                 r=[bkey(b)], w=[("sqr", s2)])
                            S.op('dve', lambda: nc.vector.scalar_tensor_tensor(
                                out=hidT[:, fb * 4 + fc, tg * 512:(tg + 1) * 512], in0=bank(b), scalar=0.0, in1=sqr[:, s2, :],
                                op0=ALU.is_gt, op1=ALU.mult), r=[bkey(b), ("sqr", s2)], w=["hidT"])
                    if after_fb is not None:
                        after_fb(fb)

            def mm2(hf, after_cb=None):
                for cb in range(8):
                    if after_cb is not None:
                        after_cb(cb)
                    wb, wk = wbs[wi[0] % 2]
                    wi[0] += 1
                    wv = wb[:, :, :].rearrange("p a (b c) -> p (a b) c", c=128)
                    S.dma('pool', wv, w2_d[l][:, cb * 128:(cb + 1) * 128].rearrange("(kc p) c -> p kc c", p=128), w=WK(wk))
                    for g4 in range(2):
                        b = 4 + (cb * 2 + g4) % 2
                        for tt in range(4):
                            t = g4 * 4 + tt
                            mm_acc(bank(b)[:, tt * 128:(tt + 1) * 128], bkey(b),
                                   [hidT[:, fc, t * 128:(t + 1) * 128] for fc in range(32)], [wv[:, fc, :] for fc in range(32)],
                                   WK(wk) + ["hidT"])
                        S.op('act', lambda: nc.scalar.copy(out=fbuf[:, g4 * 4:(g4 + 1) * 4, cb * 128:(cb + 1) * 128],
                                                           in_=bank(b).rearrange("p (t c) -> p t c", c=128)),
                             r=[bkey(b)], w=[("fbuf", g4)])

            def epi(hf, t8):
                resid_epilogue(fbuf[:, t8, :], [("fbuf", t8 // 4)], 8 * hf + t8, "gbc", False, add_eng=('dve' if hf == 0 else 'pool'))

            norm_T(x2T, "x2T", 0, 4, gcols[:, l, 1, :], True, dbase=0)
            norm_T(x2T, "x2T", 4, 4, gcols[:, l, 1, :], True, dbase=0)
            mm1(0)
            norm_T(x2T, "x2T", 8, 4, gcols[:, l, 1, :], True, dbase=8)
            norm_T(x2T, "x2T", 12, 4, gcols[:, l, 1, :], True, dbase=8)
            mm2(0)
            mm1(1, after_fb=lambda fb: epi(0, fb))
            pbf = x2T[:, 0:4, :].rearrange("p a (b c) -> p (a b) c", c=256)
            pT = x2T[:, 4:8, :].rearrange("p (a b) c -> p a (b c)", a=2)
            def p_prep(cb):
                if cb == 4:
                    S.dma('pool', pbf, p_d[l].rearrange("(t p) k -> p t k", p=128), w=["pbf"], war=x2keys)
                if cb != 7:
                    return
                for t in range(16):
                    pt = bank(6 + t % 2).bitcast(BF16).rearrange("p (a b) -> p a b", b=128)
                    for kc in range(2):
                        S.op('pe', lambda: nc.tensor.transpose(out=pt[:, kc, :], in_=pbf[:, t, kc * 128:(kc + 1) * 128],
                                                               identity=ident[:, :]), r=["pbf", "ident"], w=[bkey(6 + t % 2)],
                             inc=(kc == 1))
                    S.op('act', lambda: nc.scalar.copy(out=pT[:, :, t * 128:(t + 1) * 128], in_=pt[:, 0:2, :]),
                         r=[bkey(6 + t % 2)], w=[("pT", t)], war=x2keys)

            mm2(1, after_cb=p_prep)
            S.fence([("xnT", t_) for t_ in range(16)])
            load_w(wbA, "wbA", wg_d[l][:, 0:512], 512)
            load_w(wbB, "wbB", wg_d[l][:, 512:1024], 512)

            hT = hidT[:, 16:24, :]
            wpp = hidT[:, 24:26, :]
            sgm = hidT[:, 26:28, :].rearrange("p a c -> p (a c)").bitcast(F32).rearrange("p (a c) -> p a c", a=2)
            S.dma('pool', wpp, wp_d[l].rearrange("(kc p) c -> p kc c", p=128), w=["wpp"], war=["hidT"])

            def ple_half(hf, between=None):
                norm_T(hT, "hT", 8 * hf, 8, None, False, war=["hidT"])
                for t8 in range(8):
                    t = 8 * hf + t8
                    for nb_, (wb, wk) in enumerate([(wbA, "wbA"), (wbB, "wbB")]):
                        gb = (t % 2) * 2 + nb_
                        pb = 4 + (t % 2)
                        mm_acc(bank(gb), bkey(gb), [hT[:, dc, t8 * 128:(t8 + 1) * 128] for dc in range(8)],
                               [wb[:, dc, :] for dc in range(8)], WK(wk) + [("hT", t8)])
                        s2 = nb_
                        S.op('act', lambda: nc.scalar.activation(out=sgm[:, s2, :], in_=bank(gb), func=AF.Sigmoid),
                             r=[bkey(gb)], w=[("sgm", s2)], war=["hidT"])
                        mm_acc(bank(pb), bkey(pb), [pT[:, kc, t * 128:(t + 1) * 128] for kc in range(2)],
                               [wpp[:, kc, nb_ * 512:(nb_ + 1) * 512] for kc in range(2)], ["wpp", ("pT", t)])
                        S.op('dve', lambda: nc.vector.tensor_tensor(out=sgm[:, s2, :], in0=bank(pb), in1=sgm[:, s2, :], op=ALU.mult),
                             r=[bkey(pb), ("sgm", s2)], w=[("sgm", s2)])
                        ae = 'pool' if nb_ == 0 else 'dve'
                        S.op(ae, lambda: (nc.gpsimd if ae == 'pool' else nc.vector).tensor_tensor(
                            out=h[:, t, nb_ * 512:(nb_ + 1) * 512], in0=h[:, t, nb_ * 512:(nb_ + 1) * 512], in1=sgm[:, s2, :],
                            op=ALU.add), r=[("sgm", s2), ("h", t)], w=[("h", t)])
                    if between is not None:
                        between(t8)

            ple_half(0, between=lambda t8: epi(1, t8))
            ple_half(1)
            pre_evs = gla_prefetch(l + 1) if l + 1 < n_layers else []

    ov = out_d.rearrange("(t p) d -> p t d", p=128)
    evs = []
    for g in range(8):
        evs.append(S.dma('sp', ov[:, 2 * g:2 * g + 2, :], h[:, 2 * g:2 * g + 2, :],
                         r=[("h", t) for t in range(2 * g, 2 * g + 2)]))
    S._wait('sp', evs)
    return nc


_CONST = {}


def _consts():
    if not _CONST:
        kinds = _bias_tables()
        _CONST['idx'] = np.stack([k[0] for k in kinds], 0)
        _CONST['mask'] = np.ascontiguousarray(np.stack([k[1] for k in kinds], 1).reshape(128, 5 * 128)).astype(np.float32)
        _CONST['ident'] = np.eye(128, dtype=np.float32)
        kk = np.arange(128)
        _CONST['caus'] = (kk[:, None] <= kk[None, :]).astype(np.float32)
    return _CONST


def kernel(x, p, w_in, w_gla_a2, b_gla_a, gla_norm_g, w_out, rel_bias, pre_mix_g, post_mix_g, pre_mlp_g, post_mlp_g,
           w_mlp_in, w_mlp_out, w_ple_gate, w_ple_proj, _n_layers=DEPTH, _cores=8):
    f = lambda a: np.ascontiguousarray(np.asarray(a, dtype=np.float32))
    c = _consts()
    rb = f(rel_bias)
    bt = rb[c['idx']]
    bt = np.ascontiguousarray(bt.transpose(1, 0, 3, 2)).reshape(128, 5 * 8 * 128)
    shared = {
        "w_in": f(w_in), "w_gla_a2": f(w_gla_a2), "b_gla_a": f(b_gla_a), "gla_norm_g": f(gla_norm_g), "w_out": f(w_out),
        "pre_mix_g": f(pre_mix_g), "post_mix_g": f(post_mix_g), "pre_mlp_g": f(pre_mlp_g), "post_mlp_g": f(post_mlp_g),
        "w_mlp_in": f(w_mlp_in), "w_mlp_out": f(w_mlp_out), "w_ple_gate": f(w_ple_gate), "w_ple_proj": f(w_ple_proj),
        "bias_tab": bt, "mask_tab": c['mask'], "ident": c['ident'], "caus": c['caus'],
    }
    x = f(x)
    p = f(p)
    nc = build(_n_layers)
    in_maps = []
    for b in range(_cores):
        m = dict(shared)
        m["x"] = x[b]
        m["p"] = np.ascontiguousarray(p[:, b])
        in_maps.append(m)
    res = run_bass_kernel_spmd(nc, in_maps, core_ids=list(range(_cores)))
    return np.stack([np.asarray(r["out"], dtype=np.float32) for r in res.results], 0)
```
